# Optimizing a Trainium2 kernel written in Bass

```python
import math
import jax, jax.numpy as jnp
from jax import lax
import numpy as np

D_MODEL = 1024
BATCH = 8
SEQ = 4096
DEPTH = 2
DEC_BATCH = 8
DEC_SEQ = 8192
PAST_LEN = 128

EPS = 1e-6
CHUNK = 64
CONV_K = 5
D_FF = 4 * D_MODEL
GLA_HEADS = 4
GLA_DK = D_MODEL // 16
GLA_DV = D_MODEL // 8
GLA_RANK = 16
GLA_NORMALIZER = 16.0
GDN_HEADS = 4
GDN_DH = D_MODEL // 8
SSD_DINNER = D_MODEL
SSD_HEADDIM = 64
SSD_HEADS = SSD_DINNER // SSD_HEADDIM
SSD_GROUPS = 2
SSD_HPG = SSD_HEADS // SSD_GROUPS
SSD_DSTATE = 64

GLA_QK = GLA_HEADS * GLA_DK
GLA_V = GLA_HEADS * GLA_DV
GDN_W = GDN_HEADS * GDN_DH
SSD_CONV_CH = SSD_DINNER + 2 * SSD_GROUPS * SSD_DSTATE
D_MIX = GLA_V + GDN_W + SSD_DINNER
IN_SPLITS = (GLA_QK, GLA_QK, GLA_V, GLA_V, 2 * GLA_RANK,
             3 * GDN_W, GDN_W, 2 * GDN_HEADS, 2 * GDN_HEADS,
             SSD_DINNER, SSD_CONV_CH, 2 * SSD_HEADS)
D_IN_PROJ = 2 * GLA_QK + 2 * GLA_V + 2 * GLA_RANK + 4 * GDN_W + 4 * GDN_HEADS + SSD_DINNER + SSD_CONV_CH + 2 * SSD_HEADS

kernel_name = "hymba_style_bidir_gla_gdn_ssd_encoder"


def rmsnorm(x, w):
    xf = x.astype(jnp.float32)
    y = xf * lax.rsqrt(jnp.mean(xf * xf, axis=-1, keepdims=True) + EPS)
    return (y * w.astype(jnp.float32)).astype(x.dtype)


def l2norm(x):
    return x * lax.rsqrt(jnp.sum(x * x, axis=-1, keepdims=True) + EPS)


def flip(t):
    return jnp.flip(t, axis=1)


def dwconv(x, w):
    return lax.conv_general_dilated(
        x, w[:, None, :], window_strides=(1,),
        padding=((CONV_K // 2, CONV_K // 2),),
        dimension_numbers=("NWC", "WIO", "NWC"),
        feature_group_count=x.shape[-1])


def _chunk(t):
    return t.reshape(t.shape[0], t.shape[1] // CHUNK, CHUNK, *t.shape[2:])


def _unchunk(t):
    return t.reshape(t.shape[0], t.shape[1] * t.shape[2], *t.shape[3:])


def _decay_matrix(a):
    tril = jnp.tril(jnp.ones((CHUNK, CHUNK), dtype=bool))
    diff = a[..., :, None] - a[..., None, :]
    return jnp.exp(jnp.where(tril, diff, -jnp.inf))


def scan_states(decay, d_state):
    dec_t = jnp.moveaxis(decay, 1, 0)
    ds_t = jnp.moveaxis(d_state, 1, 0)

    def step(s, inp):
        a, d = inp
        return a * s + d, s

    _, states = lax.scan(step, jnp.zeros_like(ds_t[0]), (dec_t, ds_t))
    return jnp.moveaxis(states, 0, 1)


def gla_dir(q, k, v, g):
    q, k, v, g = (_chunk(t) for t in (q, k, v, g))
    G = jnp.cumsum(g, axis=2)
    g_tot = G[:, :, -1]
    q_in = q * jnp.exp(G)
    k_in = k * jnp.exp(-G)
    tril = jnp.tril(jnp.ones((CHUNK, CHUNK), dtype=bool))
    att = jnp.where(tril, jnp.einsum('bnihd,bnjhd->bnhij', q_in, k_in), 0.0)
    o = jnp.einsum('bnhij,bnjhv->bnihv', att, v)
    d_state = jnp.einsum('bnjhd,bnjhv->bnhdv', k * jnp.exp(g_tot[:, :, None] - G), v)
    states = scan_states(jnp.exp(g_tot)[..., None], d_state)
    o = o + jnp.einsum('bnihd,bnhdv->bnihv', q_in, states)
    return _unchunk(o)


def gla_mixer(q, k, v, gate, lr, up_w, up_b, norm_w):
    b, s = q.shape[:2]
    q = q.reshape(b, s, GLA_HEADS, GLA_DK) * (GLA_DK ** -0.5)
    k = k.reshape(b, s, GLA_HEADS, GLA_DK)
    v = v.reshape(b, s, GLA_HEADS, GLA_DV)
    lr = lr.reshape(b, s, 2, GLA_RANK)
    gk = jax.nn.log_sigmoid(jnp.einsum('bsdr,drk->bsdk', lr, up_w) + up_b) / GLA_NORMALIZER
    gk = gk.reshape(b, s, 2, GLA_HEADS, GLA_DK)
    o = gla_dir(q, k, v, gk[:, :, 0]) + flip(gla_dir(flip(q), flip(k), flip(v), flip(gk[:, :, 1])))
    o = rmsnorm(o, norm_w) * jax.nn.silu(gate.reshape(b, s, GLA_HEADS, GLA_DV))
    return o.reshape(b, s, GLA_V)


def gdn_dir(q, k, v, beta, g):
    dk = q.shape[-1]
    q, k, v, beta, g = (_chunk(t) for t in (q, k, v, beta, g))
    G = jnp.cumsum(g, axis=2)
    Gh = jnp.swapaxes(G, 2, 3)
    L = _decay_matrix(Gh)
    idx = jnp.arange(CHUNK)
    strict = idx[:, None] > idx[None, :]
    kk = jnp.einsum('bnihd,bnjhd->bnhij', k, k)
    bh = jnp.swapaxes(beta, 2, 3)
    tri = jnp.eye(CHUNK, dtype=kk.dtype) + jnp.where(strict, bh[..., :, None] * kk * L, 0.0)
    rhs = jnp.concatenate([k * (beta * jnp.exp(G))[..., None], v * beta[..., None]], axis=-1)
    rhs = jnp.swapaxes(rhs, 2, 3)
    sol = lax.linalg.triangular_solve(tri, rhs, left_side=True, lower=True, unit_diagonal=True)
    w_blk, u_blk = sol[..., :dk], sol[..., dk:]
    a_qk = jnp.einsum('bnihd,bnjhd->bnhij', q, k) * L
    q_dec = jnp.swapaxes(q * jnp.exp(G)[..., None], 2, 3)
    g_tot = Gh[..., -1]
    k_dec = jnp.swapaxes(k * jnp.exp(g_tot[:, :, None, :] - G)[..., None], 2, 3)

    def step(s, inp):
        wc, uc, qc, kc, ac, ec = inp
        u = uc - jnp.einsum('bhcd,bhdv->bhcv', wc, s)
        o = jnp.einsum('bhcd,bhdv->bhcv', qc, s) + jnp.einsum('bhij,bhjv->bhiv', ac, u)
        s = ec[..., None, None] * s + jnp.einsum('bhcd,bhcv->bhdv', kc, u)
        return s, o

    xs = tuple(jnp.moveaxis(t, 1, 0) for t in (w_blk, u_blk, q_dec, k_dec, a_qk, jnp.exp(g_tot)))
    s0 = jnp.zeros((q.shape[0], q.shape[3], dk, v.shape[-1]), dtype=q.dtype)
    _, o = lax.scan(step, s0, xs)
    o = jnp.swapaxes(jnp.moveaxis(o, 0, 1), 2, 3)
    return _unchunk(o)


def gdn_mixer(qkv, gate, beta_raw, a_raw, conv_w, A_log, dt_bias, norm_w):
    b, s = qkv.shape[:2]
    qkv = jax.nn.silu(dwconv(qkv, conv_w))
    q, k, v = (t.reshape(b, s, GDN_HEADS, GDN_DH) for t in jnp.split(qkv, 3, axis=-1))
    q = l2norm(q) * (GDN_DH ** -0.5)
    k = l2norm(k)
    beta = jax.nn.sigmoid(beta_raw.reshape(b, s, 2, GDN_HEADS))
    g = -jnp.exp(A_log) * jax.nn.softplus(a_raw.reshape(b, s, 2, GDN_HEADS) + dt_bias)
    o = gdn_dir(q, k, v, beta[:, :, 0], g[:, :, 0]) + flip(
        gdn_dir(flip(q), flip(k), flip(v), flip(beta[:, :, 1]), flip(g[:, :, 1])))
    o = rmsnorm(o, norm_w) * jax.nn.silu(gate.reshape(b, s, GDN_HEADS, GDN_DH))
    return o.reshape(b, s, GDN_W)


def ssd_dir(x, bm, cm, dt, la):
    b, s = x.shape[:2]
    x = _chunk(x * dt.reshape(b, s, SSD_GROUPS, SSD_HPG)[..., None])
    bm, cm = _chunk(bm), _chunk(cm)
    a_cum = jnp.cumsum(_chunk(la.reshape(b, s, SSD_GROUPS, SSD_HPG)), axis=2)
    ah = jnp.moveaxis(a_cum, 2, -1)
    L = _decay_matrix(ah)
    cb = jnp.einsum('bnigs,bnjgs->bngij', cm, bm)
    y = jnp.einsum('bngij,bnghij,bnjghp->bnighp', cb, L, x)
    a_tot = ah[..., -1]
    d_state = jnp.einsum('bnjgs,bnghj,bnjghp->bnghsp', bm, jnp.exp(a_tot[..., None] - ah), x)
    states = scan_states(jnp.exp(a_tot)[..., None, None], d_state)
    y = y + jnp.einsum('bnigs,bnghi,bnghsp->bnighp', cm, jnp.exp(ah), states)
    return _unchunk(y)


def ssd_mixer(z, xbc, dt_raw, conv_w, conv_b, A_log, dt_bias, D, norm_w):
    b, s = z.shape[:2]
    xbc = jax.nn.silu(dwconv(xbc, conv_w) + conv_b)
    xs, bm, cm = jnp.split(xbc, [SSD_DINNER, SSD_DINNER + SSD_GROUPS * SSD_DSTATE], axis=-1)
    xs = xs.reshape(b, s, SSD_GROUPS, SSD_HPG, SSD_HEADDIM)
    bm = bm.reshape(b, s, SSD_GROUPS, SSD_DSTATE)
    cm = cm.reshape(b, s, SSD_GROUPS, SSD_DSTATE)
    dt = jax.nn.softplus(dt_raw.reshape(b, s, 2, SSD_HEADS) + dt_bias)
    la = dt * (-jnp.exp(A_log))
    y = ssd_dir(xs, bm, cm, dt[:, :, 0], la[:, :, 0]) + flip(
        ssd_dir(flip(xs), flip(bm), flip(cm), flip(dt[:, :, 1]), flip(la[:, :, 1])))
    y = y + D.reshape(SSD_GROUPS, SSD_HPG)[..., None] * xs
    yz = (y.reshape(b, s, SSD_DINNER) * jax.nn.silu(z)).reshape(b, s, SSD_GROUPS, SSD_DINNER // SSD_GROUPS)
    y = rmsnorm(yz, norm_w.reshape(SSD_GROUPS, SSD_DINNER // SSD_GROUPS))
    return y.reshape(b, s, SSD_DINNER)


def encoder_layer(x, norm_mix_w, w_in, gla_gk_up, gla_gk_bias, gla_norm_w,
                  gdn_conv_w, gdn_A_log, gdn_dt_bias, gdn_norm_w,
                  ssd_conv_w, ssd_conv_b, ssd_A_log, ssd_dt_bias, ssd_D, ssd_norm_w,
                  w_out, norm_mlp_w, w_up, w_down):
    f32 = lambda t: t.astype(jnp.float32)
    h = rmsnorm(x, norm_mix_w)
    proj = f32(h @ w_in)
    split_points = [int(i) for i in np.cumsum(IN_SPLITS)[:-1]]
    (gla_q, gla_k, gla_v, gla_g, gla_lr, gdn_qkv, gdn_gate, gdn_beta, gdn_a,
     ssd_z, ssd_xbc, ssd_dt) = jnp.split(proj, split_points, axis=-1)
    o_gla = gla_mixer(gla_q, gla_k, gla_v, gla_g, gla_lr, f32(gla_gk_up), f32(gla_gk_bias), gla_norm_w)
    o_gdn = gdn_mixer(gdn_qkv, gdn_gate, gdn_beta, gdn_a, f32(gdn_conv_w), f32(gdn_A_log),
                      f32(gdn_dt_bias), gdn_norm_w)
    o_ssd = ssd_mixer(ssd_z, ssd_xbc, ssd_dt, f32(ssd_conv_w), f32(ssd_conv_b), f32(ssd_A_log),
                      f32(ssd_dt_bias), f32(ssd_D), ssd_norm_w)
    mix = jnp.concatenate([o_gla, o_gdn, o_ssd], axis=-1).astype(x.dtype)
    x = x + mix @ w_out
    h = rmsnorm(x, norm_mlp_w)
    x = x + jnp.square(jax.nn.relu(h @ w_up)) @ w_down
    return x


def _dt_bias_init(k, shape):
    dt = jnp.exp(jax.random.uniform(k, shape, minval=math.log(1e-3), maxval=math.log(1e-1)))
    return dt + jnp.log(-jnp.expm1(-dt))


def setup_inputs(seed: int = 0) -> dict:
    key = jax.random.key(seed)
    ks = jax.random.split(key, 22)
    nrm = jax.random.normal
    L = DEPTH
    gain = lambda k, shape: 1.0 + 0.02 * nrm(k, shape, jnp.float32)
    return {
        "x_prompt": nrm(ks[0], (BATCH, SEQ, D_MODEL), jnp.float32),
        "x_sample": nrm(ks[1], (DEC_BATCH, DEC_SEQ, D_MODEL), jnp.float32),
        "norm_mix_w": gain(ks[2], (L, D_MODEL)),
        "w_in": nrm(ks[3], (L, D_MODEL, D_IN_PROJ), jnp.float32) * D_MODEL ** -0.5,
        "gla_gk_up": nrm(ks[4], (L, 2, GLA_RANK, GLA_QK), jnp.float32) * GLA_RANK ** -0.5,
        "gla_gk_bias": 0.1 * nrm(ks[5], (L, 2, GLA_QK), jnp.float32),
        "gla_norm_w": gain(ks[6], (L, GLA_DV)),
        "gdn_conv_w": nrm(ks[7], (L, CONV_K, 3 * GDN_W), jnp.float32) * CONV_K ** -0.5,
        "gdn_A_log": jnp.log(jax.random.uniform(ks[8], (L, 2, GDN_HEADS), minval=1.0, maxval=16.0)),
        "gdn_dt_bias": _dt_bias_init(ks[9], (L, 2, GDN_HEADS)),
        "gdn_norm_w": gain(ks[10], (L, GDN_DH)),
        "ssd_conv_w": nrm(ks[11], (L, CONV_K, SSD_CONV_CH), jnp.float32) * CONV_K ** -0.5,
        "ssd_conv_b": 0.02 * nrm(ks[12], (L, SSD_CONV_CH), jnp.float32),
        "ssd_A_log": jnp.log(jax.random.uniform(ks[13], (L, 2, SSD_HEADS), minval=1.0, maxval=16.0)),
        "ssd_dt_bias": _dt_bias_init(ks[14], (L, 2, SSD_HEADS)),
        "ssd_D": gain(ks[15], (L, SSD_HEADS)),
        "ssd_norm_w": gain(ks[16], (L, SSD_DINNER)),
        "w_out": nrm(ks[17], (L, D_MIX, D_MODEL), jnp.float32) * D_MIX ** -0.5,
        "norm_mlp_w": gain(ks[18], (L, D_MODEL)),
        "w_up": nrm(ks[19], (L, D_MODEL, D_FF), jnp.float32) * D_MODEL ** -0.5,
        "w_down": nrm(ks[20], (L, D_FF, D_MODEL), jnp.float32) * D_FF ** -0.5,
        "norm_f_w": gain(ks[21], (D_MODEL,)),
    }


def reference(x_prompt, x_sample, norm_mix_w, w_in, gla_gk_up, gla_gk_bias, gla_norm_w,
              gdn_conv_w, gdn_A_log, gdn_dt_bias, gdn_norm_w,
              ssd_conv_w, ssd_conv_b, ssd_A_log, ssd_dt_bias, ssd_D, ssd_norm_w,
              w_out, norm_mlp_w, w_up, w_down, norm_f_w):
    layer_params = (norm_mix_w, w_in, gla_gk_up, gla_gk_bias, gla_norm_w,
                    gdn_conv_w, gdn_A_log, gdn_dt_bias, gdn_norm_w,
                    ssd_conv_w, ssd_conv_b, ssd_A_log, ssd_dt_bias, ssd_D, ssd_norm_w,
                    w_out, norm_mlp_w, w_up, w_down)

    def trunk(x):
        for l in range(DEPTH):
            x = encoder_layer(x, *[p[l] for p in layer_params])
        return rmsnorm(x, norm_f_w)

    y_prompt = trunk(x_prompt)
    y_sample = trunk(x_sample)
    return (y_prompt, y_sample)
```

```python
import numpy as np
from contextlib import ExitStack
import concourse.bass as bass
import concourse.mybir as mybir
from concourse.bass_utils import run_bass_kernel_spmd

F32 = mybir.dt.float32
BF16 = mybir.dt.bfloat16
AF = mybir.ActivationFunctionType
ALU = mybir.AluOpType
AX = mybir.AxisListType

ENGS = ("pe", "act", "dve", "pool", "sp")
SAME_ENGINE_SYNC = True
EPS = 1e-6
D = 1024
DIN = 5968
DMIX = 2048
DFF = 4096

WSHAPES = {
    "norm_mix_w": [2, 1024], "w_in": [2, 1024, 5968], "gla_gk_up": [2, 2, 16, 256], "gla_gk_bias": [2, 2, 256],
    "gla_norm_w": [2, 128], "gdn_conv_w": [2, 5, 1536], "gdn_A_log": [2, 2, 4], "gdn_dt_bias": [2, 2, 4],
    "gdn_norm_w": [2, 128], "ssd_conv_w": [2, 5, 1280], "ssd_conv_b": [2, 1280], "ssd_A_log": [2, 2, 16],
    "ssd_dt_bias": [2, 2, 16], "ssd_D": [2, 16], "ssd_norm_w": [2, 1024], "w_out": [2, 2048, 1024],
    "norm_mlp_w": [2, 1024], "w_up": [2, 1024, 4096], "w_down": [2, 4096, 1024], "norm_f_w": [1024],
}


class StopBuild(Exception):
    pass


class Buf:
    __slots__ = ("name", "lw", "rd", "excl")

    def __init__(self, name=None, excl=False):
        self.name = name
        self.lw = None
        self.rd = {}
        self.excl = excl


class Sched:
    def __init__(self, nc, n_dma_ch=32):
        self.nc = nc
        self.stream = {e: [] for e in ENGS}
        self.cnt = {e: 0 for e in ENGS}
        self.seen = {e: {} for e in ENGS}
        self.n_dma_ch = n_dma_ch
        self.ch_cnt = [0] * n_dma_ch
        self.ch_next = 0
        self.ch_next_sw = 0

    def _deps(self, eng, reads, writes):
        waits = {}

        def need(tok):
            if tok is None:
                return
            k, v = tok
            if k == eng and not SAME_ENGINE_SYNC:
                return
            if waits.get(k, 0) < v:
                waits[k] = v

        for b in reads:
            need(b.lw)
            if b.excl:
                for k, v in b.rd.items():
                    if k != eng:
                        need((k, v))
        for b in writes:
            need(b.lw)
            for k, v in b.rd.items():
                need((k, v))
        w = []
        seen = self.seen[eng]
        for k, v in waits.items():
            if seen.get(k, 0) < v:
                seen[k] = v
                w.append((k, v))
        return w

    def _tick(self):
        self.nops = getattr(self, "nops", 0) + 1
        lim = getattr(self, "limit", None)
        if lim is not None and self.nops > lim:
            raise StopBuild()

    def op(self, eng, fn, reads=(), writes=()):
        self._tick()
        w = self._deps(eng, reads, writes)
        self.cnt[eng] += 1
        c = self.cnt[eng]
        for b in reads:
            b.rd[eng] = c
        for b in writes:
            b.lw = (eng, c)
            b.rd = {}
        self.stream[eng].append((w, fn, (eng, 1)))

    def dma(self, q, out_ap, in_ap, reads=(), writes=(), slow=False):
        self._tick()
        if q == "sp":
            ch = self.ch_next
            self.ch_next = (self.ch_next + 1) % (self.n_dma_ch - 8)
        else:
            ch = self.n_dma_ch - 8 + self.ch_next_sw
            self.ch_next_sw = (self.ch_next_sw + 1) % 8
        key = "d%d" % ch
        w = self._deps(q, reads, writes)
        prev = 16 * self.ch_cnt[ch]
        if prev and self.seen[q].get(key, 0) < prev:
            self.seen[q][key] = prev
            w.append((key, prev))
        self.ch_cnt[ch] += 1
        v = 16 * self.ch_cnt[ch]
        for b in reads:
            b.rd[key] = v
        for b in writes:
            b.lw = (key, v)
            b.rd = {}

        def fn(e, out_ap=out_ap, in_ap=in_ap, slow=slow):
            if slow:
                return e.dma_start(out=out_ap, in_=in_ap, allow_slow_non_contiguous=True)
            return e.dma_start(out=out_ap, in_=in_ap)

        self.stream[q].append((w, fn, (key, 16)))

    def barrier(self):
        snap = [(e, self.cnt[e]) for e in ENGS if self.cnt[e]]
        snap += [("d%d" % c, 16 * self.ch_cnt[c]) for c in range(self.n_dma_ch) if self.ch_cnt[c]]
        for e in ENGS:
            w = []
            for k, v in snap:
                if v > self.seen[e].get(k, 0):
                    self.seen[e][k] = v
                    w.append((k, v))
            if w:
                self.stream[e].append((w, None, None))

    def emit(self):
        nc = self.nc
        with ExitStack() as st:
            sems = {}
            for e in ENGS:
                sems[e] = st.enter_context(nc.semaphore("s_" + e))
            for c in range(self.n_dma_ch):
                sems["d%d" % c] = st.enter_context(nc.semaphore("s_d%d" % c))
            fin = [("d%d" % c, 16 * self.ch_cnt[c]) for c in range(self.n_dma_ch) if self.ch_cnt[c]]
            block = st.enter_context(nc.Block())

            def run(e, h):
                for w, fn, inc in self.stream[e]:
                    for k, v in w:
                        h.wait_ge(sems[k], v)
                    if fn is not None:
                        fn(h).then_inc(sems[inc[0]], inc[1])
                if e == "sp":
                    for k, v in fin:
                        h.wait_ge(sems[k], v)

            @block.tensor
            def _(h):
                run("pe", h)

            @block.scalar
            def _(h):
                run("act", h)

            @block.vector
            def _(h):
                run("dve", h)

            @block.gpsimd
            def _(h):
                run("pool", h)

            @block.sync
            def _(h):
                run("sp", h)


class Tile:
    __slots__ = ("ap", "buf")

    def __init__(self, ap, name=None):
        self.ap = ap
        self.buf = Buf(name)

    def v(self, pat, **kw):
        return self.ap.rearrange(pat, **kw)


class KB:
    SB_WORDS = 53000

    def __init__(self, seq_lens=(4096, 8192), n_layers=2, dbg=()):
        self.seq_lens = tuple(seq_lens)
        self.NT = sum(seq_lens)
        self.NCH = self.NT // 128
        self.n_layers = n_layers
        self.dbg = set(dbg)
        nc = self.nc = bass.Bass("TRN2", target_bir_lowering=False)
        self.S = Sched(nc)
        for t_ in self.dbg:
            if t_.startswith("lim="):
                self.S.limit = int(t_[4:])
        self.xin = [nc.dram_tensor("x%d" % i, [L, D], F32, kind="ExternalInput").ap() for i, L in enumerate(self.seq_lens)]
        self.W = {k: nc.dram_tensor(k, s, F32, kind="ExternalInput").ap() for k, s in WSHAPES.items()}
        self.yout = [nc.dram_tensor("y%d" % i, [L, D], F32, kind="ExternalOutput").ap() for i, L in enumerate(self.seq_lens)]
        NT = self.NT

        def scr(name, shape, dt):
            kind = "ExternalOutput" if name in self.dbg else "Internal"
            return nc.dram_tensor(name, shape, dt, kind=kind).ap()

        self.Fq = scr("Fq", [256, NT], BF16)
        self.Fk = scr("Fk", [256, NT], BF16)
        self.Fconv = scr("Fconv", [2816, NT], BF16)
        self.Fgate = scr("Fgate", [80, NT], F32)
        self.Tv = scr("Tv", [NT, 512], BF16)
        self.Tg = scr("Tg", [NT, 2048], BF16)
        self.OI = scr("OI", [NT, 2048], F32)
        self.OD = [scr("OF", [NT, 2048], F32), scr("OB", [NT, 2048], F32)]
        self.RB = [scr("RB%d" % d, [self.NCH, 128, self.XB], BF16) for d in range(2)]
        self.RF = [scr("RF%d" % d, [self.NCH, 128, self.XF], F32) for d in range(2)]
        self.RS = scr("RS", [self.NCH, 128, 256], BF16)
        self.X1 = scr("X1", [NT, D], F32)
        self.XR = scr("XR", [NT, D], F32)

    def seq_of(self, t):
        s0 = 0
        for i, L in enumerate(self.seq_lens):
            if t < s0 + L:
                return i, s0, L
            s0 += L
        raise ValueError

    def xrows(self, lst, t0, n):
        i, s0, L = self.seq_of(t0)
        assert t0 + n <= s0 + L
        return lst[i][t0 - s0:t0 - s0 + n, :]

    def reset(self):
        self.sb_off = self.sb_base

    def tile(self, cols, dt=F32, name=None):
        nbytes = cols * (2 if dt == BF16 else 4)
        nbytes = (nbytes + 63) // 64 * 64
        off = self.sb_off
        self.sb_off += nbytes
        assert self.sb_off <= self.SB_WORDS * 4, "SBUF overflow %d" % self.sb_off
        ap = self.big[:, off // 4:(off + nbytes) // 4]
        if dt == BF16:
            ap = ap.bitcast(BF16)
        return Tile(ap[:, 0:cols], name)

    def bank(self, b, dt=F32):
        ap = self.pp[:, b * 512:(b + 1) * 512]
        if dt == BF16:
            ap = ap.bitcast(BF16)
        return ap

    def build(self):
        nc = self.nc
        S = self.S
        with ExitStack() as st:
            self.big = st.enter_context(nc.sbuf_tensor("big", [128, self.SB_WORDS], F32))
            self.pp = st.enter_context(nc.psum_tensor("pp", [128, 4096], F32))
            self.pb = [Buf("bank%d" % b, excl=True) for b in range(8)]
            self.sb_off = 0
            self.consts()
            S.barrier()
            self.sb_base = self.sb_off
            try:
                for l in range(self.n_layers):
                    xsrc = self.xin if l == 0 else [self.XR]
                    self.layer = l
                    self.pass_A(l, xsrc)
                    S.barrier()
                    self.stop("stopA")
                    if "noscan" not in self.dbg:
                        self.pass_P(l)
                        S.barrier()
                        self.stop("stopP")
                        self.pass_B(l)
                        S.barrier()
                        self.stop("stopB")
                    self.pass_C1(l, xsrc)
                    S.barrier()
                    self.pass_C2(l)
                    S.barrier()
            except StopBuild:
                S.barrier()
            S.emit()
        return nc

    def stop(self, tag):
        if tag in self.dbg:
            raise StopBuild()

    def xsrc_rows(self, xsrc, t0, n):
        if len(xsrc) == 1:
            return xsrc[0][t0:t0 + n, :]
        return self.xrows(xsrc, t0, n)

    def consts(self):
        S = self.S
        idf = self.identf = self.tile(128, F32, "identf")
        idb = self.identb = self.tile(128, BF16, "identb")
        S.op("pool", lambda e: e.memset(idf.ap, 1.0), writes=[idf.buf])
        S.op("pool", lambda e: e.affine_select(out=idf.ap, in_=idf.ap, pattern=[[-1, 128]], compare_op=ALU.is_equal,
                                               fill=self.freg(e, 0.0), base=0, channel_multiplier=1), reads=[idf.buf], writes=[idf.buf])
        S.op("dve", lambda e: e.tensor_copy(out=idb.ap, in_=idf.ap), reads=[idf.buf], writes=[idb.buf])
        mL = self.maskL = self.tile(512, F32, "maskL")
        mU = self.maskU = self.tile(512, F32, "maskU")
        for m, pat, cm in ((mL, [[0, 4], [1, 128]], -1), (mU, [[0, 4], [-1, 128]], 1)):
            S.op("pool", lambda e, m=m: e.memset(m.ap, 1.0), writes=[m.buf])
            S.op("pool", lambda e, m=m, pat=pat, cm=cm: e.affine_select(out=m.v("p (h i) -> p h i", h=4), in_=m.v("p (h i) -> p h i", h=4), pattern=pat,
                                                                       compare_op=ALU.is_ge, fill=self.freg(e, 0.0), base=0, channel_multiplier=cm),
                 reads=[m.buf], writes=[m.buf])
        ones = self.onesb = self.tile(128, BF16, "onesb")
        S.op("pool", lambda e: e.memset(ones.ap, 1.0), writes=[ones.buf])
        rm = self.rmask = self.tile(512, F32, "rmask")
        S.op("pool", lambda e: e.memset(rm.ap, 1.0), writes=[rm.buf])
        for c in range(4):
            S.op("pool", lambda e, c=c: e.memset(rm.ap[:, c * 128:c * 128 + 1], 0.0), reads=[rm.buf], writes=[rm.buf])
        self.bmask = [self.tile(512, BF16, "bmask%d" % i) for i in range(4)]
        sel = self.sel = self.tile(20 * 128, F32, "sel")
        keep = self.sb_off
        dts = []
        for bi, bsz in enumerate((16, 32, 64)):
            nbk = 128 // bsz
            et = self.tile(128, F32, "Eb%d" % bi)
            dtile = self.tile(128, F32, "Db%d" % bi)
            S.op("pool", lambda e, et=et, nbk=nbk: e.memset(et.ap[0:nbk, :], 1.0), writes=[et.buf])
            S.op("pool", lambda e, et=et, nbk=nbk, bsz=bsz: e.affine_select(out=et.ap[0:nbk, :], in_=et.ap[0:nbk, :], pattern=[[1, 128]], compare_op=ALU.is_ge,
                                                                        fill=self.freg(e, 0.0), base=0, channel_multiplier=-bsz), reads=[et.buf], writes=[et.buf])
            S.op("pool", lambda e, et=et, nbk=nbk, bsz=bsz: e.affine_select(out=et.ap[0:nbk, :], in_=et.ap[0:nbk, :], pattern=[[-1, 128]], compare_op=ALU.is_ge,
                                                                        fill=self.freg(e, 0.0), base=bsz - 1, channel_multiplier=bsz), reads=[et.buf], writes=[et.buf])
            ps = self.bank(bi)
            S.op("pe", lambda e, et=et, nbk=nbk, ps=ps: e.matmul(ps[:, 0:128], lhsT=et.ap[0:nbk, :], rhs=et.ap[0:nbk, :], start=True, stop=True), reads=[et.buf], writes=[self.pb[bi]])
            S.op("dve", lambda e, dtile=dtile, ps=ps: e.tensor_copy(out=dtile.ap, in_=ps[:, 0:128]), reads=[self.pb[bi]], writes=[dtile.buf])
            dts.append(dtile)
        rep = lambda ap: ap.unsqueeze(1).broadcast_to([128, 4, 128])
        bm = self.bmask
        S.op("dve", lambda e: e.tensor_copy(out=bm[0].v("p (h i) -> p h i", h=4), in_=rep(dts[0].ap)), reads=[dts[0].buf], writes=[bm[0].buf])
        S.op("dve", lambda e: e.tensor_tensor(out=bm[1].v("p (h i) -> p h i", h=4), in0=rep(dts[1].ap), in1=rep(dts[0].ap), op=ALU.subtract), reads=[dts[0].buf, dts[1].buf], writes=[bm[1].buf])
        S.op("dve", lambda e: e.tensor_tensor(out=bm[2].v("p (h i) -> p h i", h=4), in0=rep(dts[2].ap), in1=rep(dts[1].ap), op=ALU.subtract), reads=[dts[1].buf, dts[2].buf], writes=[bm[2].buf])
        S.op("dve", lambda e: e.tensor_scalar(out=bm[3].v("p (h i) -> p h i", h=4), in0=rep(dts[2].ap), scalar1=-1.0, scalar2=1.0, op0=ALU.mult, op1=ALU.add), reads=[dts[2].buf], writes=[bm[3].buf])
        sel2 = self.tile(20 * 128, F32, "sel2")
        for t_, base in ((sel, 0), (sel2, -32)):
            S.op("pool", lambda e, t_=t_: e.memset(t_.ap[0:52, :], 1.0), writes=[t_.buf])
            S.op("pool", lambda e, t_=t_, base=base: e.affine_select(out=t_.ap[0:52, :].rearrange("p (q m) -> p q m", q=20),
                                                                    in_=t_.ap[0:52, :].rearrange("p (q m) -> p q m", q=20), pattern=[[-1, 20], [0, 128]],
                                                                    compare_op=ALU.is_equal, fill=self.freg(e, 0.0), base=base, channel_multiplier=1),
                 reads=[t_.buf], writes=[t_.buf])
        S.op("pool", lambda e: e.tensor_tensor(out=sel.ap[0:52, :], in0=sel.ap[0:52, :], in1=sel2.ap[0:52, :], op=ALU.add), reads=[sel.buf, sel2.buf], writes=[sel.buf])
        self.sb_off = keep

    def rms_T(self, xt, ns, nw, hT, junk, ssq, rstd, xn, banks):
        S = self.S
        T = ns * 128
        xv = xt.v("p (s d) -> p s d", s=ns)
        xnv = xn.v("p (s d) -> p s d", s=ns)
        S.op("pool", lambda e: e.memset(ssq.ap[:, 0:ns], 0.0), writes=[ssq.buf])
        for s in range(ns):
            S.op("act", lambda e, s=s: e.activation(out=junk.ap, in_=xv[:, s, :], func=AF.Square, accum_out=ssq.ap[:, s:s + 1]),
                 reads=[xt.buf, ssq.buf], writes=[junk.buf, ssq.buf])
        S.op("act", lambda e: e.activation(out=rstd.ap[:, 0:ns], in_=ssq.ap[:, 0:ns], func=AF.Ln, bias=EPS, scale=1.0 / D),
             reads=[ssq.buf], writes=[rstd.buf])
        S.op("act", lambda e: e.activation(out=rstd.ap[:, 0:ns], in_=rstd.ap[:, 0:ns], func=AF.Exp, scale=-0.5),
             reads=[rstd.buf], writes=[rstd.buf])
        for s in range(ns):
            eng = "dve" if s % 2 == 0 else "pool"
            S.op(eng, lambda e, s=s: e.tensor_scalar(out=xnv[:, s, :], in0=xv[:, s, :], scalar1=rstd.ap[:, s:s + 1], scalar2=None,
                                                      op0=ALU.mult), reads=[xt.buf, rstd.buf], writes=[xn.buf])
        hv = hT.v("p (k t) -> p k t", k=8)
        for kk in range(4):
            b = banks[kk % len(banks)]
            pbf = self.bank(b, BF16)

            def tr(e, kk=kk, pbf=pbf):
                for j in range(2):
                    k = 2 * kk + j
                    for s in range(ns):
                        ins = e.transpose(pbf[:, j * T + s * 128:j * T + (s + 1) * 128], xnv[:, s, k * 128:(k + 1) * 128], self.identb.ap)
                return ins
            S.op("pe", tr, reads=[xn.buf, self.identb.buf], writes=[self.pb[b]])
            for j in range(2):
                k = 2 * kk + j
                if j == 0:
                    S.op("act", lambda e, k=k, j=j, pbf=pbf: e.activation(out=hv[:, k, :], in_=pbf[:, j * T:(j + 1) * T], func=AF.Copy,
                                                                       scale=nw.ap[:, k:k + 1]),
                         reads=[self.pb[b], nw.buf], writes=[hT.buf])
                else:
                    S.op("dve", lambda e, k=k, j=j, pbf=pbf: e.tensor_scalar(out=hv[:, k, :], in0=pbf[:, j * T:(j + 1) * T],
                                                                          scalar1=nw.ap[:, k:k + 1], scalar2=None, op0=ALU.mult),
                         reads=[self.pb[b], nw.buf], writes=[hT.buf])

    def pass_A(self, l, xsrc):
        S = self.S
        W = self.W
        self.reset()
        T = 512
        ntiles = self.NT // T
        wib = [self.tile(DIN, BF16, "wib%d" % k) for k in range(8)]
        for k in range(8):
            S.dma("pool", wib[k].ap, W["w_in"][l, k * 128:(k + 1) * 128, :], writes=[wib[k].buf])
        nw = self.tile(8, F32, "nwA")
        S.dma("sp", nw.ap, W["norm_mix_w"][l].rearrange("(k p) -> p k", p=128), writes=[nw.buf], slow=True)
        xt = [self.tile(4 * D, F32, "xtA%d" % i) for i in range(2)]
        junk = self.tile(D, BF16, "junkA")
        ssq = self.tile(4, F32, "ssqA")
        rstd = self.tile(4, F32, "rstdA")
        xn = self.tile(4 * D, BF16, "xnA")
        hT = [self.tile(8 * T, BF16, "hTA%d" % i) for i in range(2)]
        stg = [self.tile(512, F32, "stgA%d" % i) for i in range(4)]
        tstg = [self.tile(2560, BF16, "tstgA%d" % i) for i in range(2)]
        groups = []
        for i in range(2):
            groups.append((128 * i, 128, self.Fq, 128 * i, False))
        for i in range(2):
            groups.append((256 + 128 * i, 128, self.Fk, 128 * i, False))
        for i in range(12):
            groups.append((1568 + 128 * i, 128, self.Fconv, 128 * i, False))
        for i in range(10):
            groups.append((4656 + 128 * i, 128, self.Fconv, 1536 + 128 * i, False))
        groups += [(1536, 16, self.Fgate, 0, True), (1552, 16, self.Fgate, 16, True), (5936, 16, self.Fgate, 32, True),
                   (5952, 16, self.Fgate, 48, True), (3624, 8, self.Fgate, 64, True), (3616, 8, self.Fgate, 72, True)]
        tbanks = [(512, False), (1024, True), (3104, True), (3632, True), (4144, True)]
        cnt = {"g": 0, "t": 0}

        def prep(t):
            t0 = t * T
            x = xt[t % 2]
            S.dma("sp", x.v("p (s d) -> p s d", s=4), self.xsrc_rows(xsrc, t0, T).rearrange("(s p) d -> p s d", p=128), writes=[x.buf])
            self.rms_T(x, 4, nw, hT[t % 2], junk, ssq, rstd, xn, banks=[0, 1])

        def main(t):
            t0 = t * T
            h = hT[t % 2]
            hv = h.v("p (k t) -> p k t", k=8)
            for (off, M, dst, row, isf) in groups:
                b = 2 + cnt["g"] % 3
                sg = stg[cnt["g"] % 4]
                cnt["g"] += 1
                ps = self.bank(b)

                def mm(e, off=off, M=M, ps=ps):
                    for k in range(8):
                        ins = e.matmul(ps[0:M, :], lhsT=wib[k].ap[:, off:off + M], rhs=hv[:, k, :], start=(k == 0), stop=(k == 7))
                    return ins
                S.op("pe", mm, reads=[h.buf] + [w.buf for w in wib], writes=[self.pb[b]])
                so = sg.ap if isf else sg.ap.bitcast(BF16)[:, 0:512]
                if cnt["g"] % 2 == 0:
                    S.op("act", lambda e, so=so, ps=ps, M=M: e.activation(out=so[0:M, :], in_=ps[0:M, :], func=AF.Copy),
                         reads=[self.pb[b]], writes=[sg.buf])
                else:
                    S.op("dve", lambda e, so=so, ps=ps, M=M: e.tensor_copy(out=so[0:M, :], in_=ps[0:M, :]),
                         reads=[self.pb[b]], writes=[sg.buf])
                S.dma("sp", dst[row:row + M, t0:t0 + T], so[0:M, :], reads=[sg.buf])
            for s in range(4):
                ts = tstg[cnt["t"] % 2]
                cnt["t"] += 1
                for bi, (off, silu) in enumerate(tbanks):
                    b = 5 + (cnt["g"] % 3)
                    cnt["g"] += 1
                    ps = self.bank(b)

                    def mm(e, off=off, s=s, ps=ps):
                        for k in range(8):
                            ins = e.matmul(ps, lhsT=hv[:, k, s * 128:(s + 1) * 128], rhs=wib[k].ap[:, off:off + 512], start=(k == 0),
                                           stop=(k == 7))
                        return ins
                    S.op("pe", mm, reads=[h.buf] + [w.buf for w in wib], writes=[self.pb[b]])
                    o = ts.ap[:, bi * 512:(bi + 1) * 512]
                    if silu:
                        S.op("act", lambda e, o=o, ps=ps: e.activation(out=o, in_=ps, func=AF.Silu), reads=[self.pb[b]], writes=[ts.buf])
                    else:
                        S.op("dve", lambda e, o=o, ps=ps: e.tensor_copy(out=o, in_=ps), reads=[self.pb[b]], writes=[ts.buf])
                r0 = t0 + s * 128
                S.dma("sp", self.Tv[r0:r0 + 128, :], ts.ap[:, 0:512], reads=[ts.buf])
                S.dma("sp", self.Tg[r0:r0 + 128, :], ts.ap[:, 512:2560], reads=[ts.buf])

        prep(0)
        for t in range(ntiles):
            if t + 1 < ntiles:
                prep(t + 1)
            main(t)

    XB = 3584
    XF = 552

    def arena_mark(self):
        return self.sb_off

    def arena_reset(self, mark):
        self.S.barrier()
        self.sb_off = mark

    def freg(self, e, val):
        c = self.__dict__.setdefault("_fregs", {})
        if val not in c:
            c[val] = e.to_reg(val)
        return c[val]

    def nb(self):
        b = self._bank_rr
        self._bank_rr = (b + 1) % 8
        return b

    def pass_P(self, l):
        S = self.S
        W = self.W
        self.reset()
        self._bank_rr = 0
        T = 512
        ntiles = self.NT // T
        idf = self.identf
        idb = self.identb
        V3 = lambda ap, h: ap.rearrange("p (h i) -> p h i", h=h)

        cwT = self.tile(22 * 5 + 10, F32, "cwT")
        dg = self.tile(110 * 128, BF16, "dg")
        cols = self.tile(8, F32, "gcols")
        upw = self.tile(256, F32, "upw")
        negb = self.tile(4, F32, "negb")
        did = self.tile(16 * 128, F32, "dident")
        QT = self.tile(1024, BF16, "QT")
        KT = self.tile(1024, BF16, "KT")
        RAWC = self.tile(22 * 516, BF16, "RAWC")
        LR = self.tile(512, F32, "LR")
        R1 = self.tile(512, F32, "R1")
        R2 = self.tile(512, F32, "R2")
        VT = self.tile(4 * 512, BF16, "VT")
        S.op("pool", lambda e: e.memset(R2.ap[0:64, :], 0.0), writes=[R2.buf])
        S.op("pool", lambda e: e.memset(R1.ap[0:64, :], 0.0), writes=[R1.buf])
        SA = self.tile(512, F32, "slabA")
        SQ6 = self.tile(512, F32, "slabQ6")
        TM = [self.tile(384, F32, "TM%d" % c) for c in range(4)]
        OG = [self.tile(512, F32, "OG%d" % c) for c in range(2)]
        OS = [self.tile(1024, F32, "OS%d" % c) for c in range(2)]
        ZZ = self.tile(512, F32, "ZZ")
        S.op("pool", lambda e: e.memset(ZZ.ap, 0.0), writes=[ZZ.buf])
        mark = self.arena_mark()
        cwr = self.tile(2816, F32, "cwr")
        S.dma("sp", cwr.ap[0:5, 0:1536], W["gdn_conv_w"][l], writes=[cwr.buf])
        S.dma("sp", cwr.ap[0:5, 1536:2816], W["ssd_conv_w"][l], writes=[cwr.buf])
        cbr = self.tile(1280, F32, "cbr")
        S.dma("sp", cbr.ap[0:1, :], W["ssd_conv_b"][l].rearrange("(o c) -> o c", o=1), writes=[cbr.buf])
        b = self.nb()
        ps = self.bank(b)

        def mmcw(e):
            for cc in range(22):
                ins = e.matmul(ps[:, cc * 5:cc * 5 + 5], lhsT=cwr.ap[0:5, cc * 128:(cc + 1) * 128], rhs=idf.ap[0:5, 0:5], start=True, stop=True)
            for cc in range(10):
                ins = e.matmul(ps[:, 110 + cc:111 + cc], lhsT=cbr.ap[0:1, cc * 128:(cc + 1) * 128], rhs=idf.ap[0:1, 0:1], start=True, stop=True)
            return ins
        S.op("pe", mmcw, reads=[cwr.buf, cbr.buf, idf.buf], writes=[self.pb[b]])
        S.op("dve", lambda e: e.tensor_copy(out=cwT.ap, in_=ps[:, 0:120]), reads=[self.pb[b]], writes=[cwT.buf])
        for i in range(110):
            eng = "dve" if i % 2 == 0 else "pool"
            S.op(eng, lambda e, i=i: e.tensor_scalar(out=dg.ap[:, i * 128:(i + 1) * 128], in0=idf.ap, scalar1=cwT.ap[:, i:i + 1], scalar2=None, op0=ALU.mult),
                 reads=[idf.buf, cwT.buf], writes=[dg.buf])
        S.op("pool", lambda e: e.memset(cols.ap[0:64, :], 0.0), writes=[cols.buf])
        for d in range(2):
            rb = 32 * d
            S.dma("sp", cols.ap[rb:rb + 16, 0:1], W["ssd_dt_bias"][l, d].rearrange("(h o) -> h o", o=1), reads=[cols.buf], writes=[cols.buf], slow=True)
            S.dma("sp", cols.ap[rb + 16:rb + 20, 0:1], W["gdn_dt_bias"][l, d].rearrange("(h o) -> h o", o=1), reads=[cols.buf], writes=[cols.buf], slow=True)
            S.dma("sp", cols.ap[rb:rb + 16, 1:2], W["ssd_A_log"][l, d].rearrange("(h o) -> h o", o=1), reads=[cols.buf], writes=[cols.buf], slow=True)
            S.dma("sp", cols.ap[rb + 16:rb + 20, 1:2], W["gdn_A_log"][l, d].rearrange("(h o) -> h o", o=1), reads=[cols.buf], writes=[cols.buf], slow=True)
        S.op("act", lambda e: e.activation(out=cols.ap[0:52, 1:2], in_=cols.ap[0:52, 1:2], func=AF.Exp), reads=[cols.buf], writes=[cols.buf])
        S.op("dve", lambda e: e.tensor_scalar(out=cols.ap[0:52, 1:2], in0=cols.ap[0:52, 1:2], scalar1=-1.0, scalar2=None, op0=ALU.mult), reads=[cols.buf], writes=[cols.buf])
        for d in range(2):
            S.op("pool", lambda e, d=d: e.memset(cols.ap[32 * d:32 * d + 16, 2:3], 1.0), reads=[cols.buf], writes=[cols.buf])
        S.op("dve", lambda e: e.tensor_scalar(out=cols.ap[0:52, 3:4], in0=cols.ap[0:52, 2:3], scalar1=-1.0, scalar2=None, op0=ALU.add), reads=[cols.buf], writes=[cols.buf])
        for d in range(2):
            S.dma("sp", upw.ap[32 * d:32 * d + 16, :], W["gla_gk_up"][l, d], writes=[upw.buf])
            S.dma("sp", negb.ap[:, 2 * d:2 * d + 2], W["gla_gk_bias"][l, d].rearrange("(c p) -> p c", p=128), writes=[negb.buf], slow=True)
        S.op("dve", lambda e: e.tensor_scalar(out=negb.ap, in0=negb.ap, scalar1=-1.0, scalar2=None, op0=ALU.mult), reads=[negb.buf], writes=[negb.buf])
        drep = self.tile(16, F32, "drep")
        S.dma("sp", drep.ap, W["ssd_D"][l].partition_broadcast(128), writes=[drep.buf])
        S.op("dve", lambda e: e.tensor_tensor(out=V3(did.ap, 16), in0=idf.ap.unsqueeze(1).broadcast_to([128, 16, 128]),
                                              in1=drep.ap.unsqueeze(2).broadcast_to([128, 16, 128]), op=ALU.mult), reads=[idf.buf, drep.buf], writes=[did.buf])
        self.arena_reset(mark)
        self.stop("stopP0")
        rawv = RAWC.v("p (c t) -> p c t", c=22)

        for t in range(ntiles):
            t0 = t * T
            si, s0, L = self.seq_of(t0)
            S.dma("sp", QT.v("p (c t) -> p c t", c=2), self.Fq[:, t0:t0 + T].rearrange("(c p) t -> p c t", p=128), writes=[QT.buf])
            S.dma("sp", KT.v("p (c t) -> p c t", c=2), self.Fk[:, t0:t0 + T].rearrange("(c p) t -> p c t", p=128), writes=[KT.buf])
            lo = max(t0 - 2, s0)
            hi = min(t0 + T + 2, s0 + L)
            if lo > t0 - 2:
                S.op("pool", lambda e: e.memset(rawv[:, :, 0:2], 0.0), writes=[RAWC.buf])
            if hi < t0 + T + 2:
                S.op("pool", lambda e: e.memset(rawv[:, :, 514:516], 0.0), writes=[RAWC.buf])
            S.dma("sp", rawv[:, :, lo - (t0 - 2):hi - (t0 - 2)], self.Fconv[:, lo:hi].rearrange("(c p) t -> p c t", p=128), reads=[RAWC.buf], writes=[RAWC.buf])
            for d in range(2):
                rb = 32 * d
                S.dma("sp", LR.ap[rb:rb + 16, :], self.Fgate[16 * d:16 * d + 16, t0:t0 + T], writes=[LR.buf])
                S.dma("sp", R1.ap[rb:rb + 16, :], self.Fgate[32 + 16 * d:48 + 16 * d, t0:t0 + T], reads=[R1.buf], writes=[R1.buf])
                S.dma("sp", R1.ap[rb + 16:rb + 20, :], self.Fgate[64 + 4 * d:68 + 4 * d, t0:t0 + T], reads=[R1.buf], writes=[R1.buf])
                S.dma("sp", R2.ap[rb + 16:rb + 20, :], self.Fgate[72 + 4 * d:76 + 4 * d, t0:t0 + T], reads=[R2.buf], writes=[R2.buf])
            S.dma("sp", VT.v("p (s c) -> p s c", s=4), self.Tv[t0:t0 + T, :].rearrange("(s p) c -> p s c", p=128), writes=[VT.buf])

            P52 = slice(0, 52)
            sl = [self.tile(512, F32, "rs%d" % i) for i in range(9)]
            E1s, SP1, LA, SP2, LNDT, Q1, Q2, Q3, Q4 = sl
            Q5 = self.tile(512, F32, "rsQ5")
            TOT = self.tile(4, F32, "rsTOT")
            tmpb = self.tile(512, F32, "rstmp")
            S.op("act", lambda e: e.activation(out=E1s.ap[P52], in_=R1.ap[P52], func=AF.Exp, bias=cols.ap[P52, 0:1]), reads=[R1.buf, cols.buf], writes=[E1s.buf])
            S.op("act", lambda e: e.activation(out=SP1.ap[P52], in_=E1s.ap[P52], func=AF.Ln, bias=1.0), reads=[E1s.buf], writes=[SP1.buf])
            S.op("dve", lambda e: e.tensor_scalar(out=LA.ap[P52], in0=SP1.ap[P52], scalar1=cols.ap[P52, 1:2], scalar2=None, op0=ALU.mult), reads=[SP1.buf, cols.buf], writes=[LA.buf])
            S.op("act", lambda e: e.activation(out=E1s.ap[P52], in_=R2.ap[P52], func=AF.Exp, scale=-1.0), reads=[R2.buf, SP1.buf], writes=[E1s.buf])
            S.op("act", lambda e: e.activation(out=SP2.ap[P52], in_=E1s.ap[P52], func=AF.Ln, bias=1.0), reads=[E1s.buf], writes=[SP2.buf])
            S.op("dve", lambda e: e.tensor_tensor_scan(out=SA.ap[P52], data0=self.rmask.ap[P52], data1=LA.ap[P52], initial=0.0, op0=ALU.mult, op1=ALU.add),
                 reads=[LA.buf, self.rmask.buf], writes=[SA.buf])
            S.op("dve", lambda e: e.tensor_copy(out=TOT.ap[P52], in_=V3(SA.ap[P52], 4)[:, :, 127]), reads=[SA.buf], writes=[TOT.buf])
            PB = slice(32, 52)
            S.op("dve", lambda e: e.tensor_tensor(out=tmpb.ap[PB], in0=LA.ap[PB], in1=SA.ap[PB], op=ALU.subtract), reads=[LA.buf, SA.buf], writes=[tmpb.buf])
            S.op("dve", lambda e: e.tensor_tensor(out=V3(SA.ap[PB], 4), in0=V3(tmpb.ap[PB], 4), in1=TOT.ap[PB].unsqueeze(2).broadcast_to([20, 4, 128]), op=ALU.add),
                 reads=[tmpb.buf, TOT.buf], writes=[SA.buf])
            S.op("act", lambda e: e.activation(out=LNDT.ap[P52], in_=SP1.ap[P52], func=AF.Ln), reads=[SP1.buf], writes=[LNDT.buf])
            S.op("dve", lambda e: e.scalar_tensor_tensor(out=Q1.ap[P52], in0=LNDT.ap[P52], scalar=cols.ap[P52, 2:3], in1=SA.ap[P52], op0=ALU.mult, op1=ALU.subtract),
                 reads=[LNDT.buf, SA.buf, cols.buf], writes=[Q1.buf])
            S.op("dve", lambda e: e.tensor_tensor(out=V3(tmpb.ap[P52], 4), in0=V3(Q1.ap[P52], 4), in1=TOT.ap[P52].unsqueeze(2).broadcast_to([52, 4, 128]), op=ALU.add),
                 reads=[Q1.buf, TOT.buf], writes=[tmpb.buf])
            S.op("act", lambda e: e.activation(out=Q2.ap[P52], in_=tmpb.ap[P52], func=AF.Exp), reads=[tmpb.buf], writes=[Q2.buf])
            S.op("dve", lambda e: e.scalar_tensor_tensor(out=SQ6.ap[P52], in0=SP2.ap[P52], scalar=cols.ap[P52, 3:4], in1=SA.ap[P52], op0=ALU.mult, op1=ALU.add),
                 reads=[SP2.buf, SA.buf, cols.buf], writes=[SQ6.buf])
            S.op("act", lambda e: e.activation(out=Q3.ap[P52], in_=SQ6.ap[P52], func=AF.Exp), reads=[SQ6.buf], writes=[Q3.buf])
            S.op("act", lambda e: e.activation(out=Q4.ap[P52], in_=SP2.ap[P52], func=AF.Exp, scale=-1.0), reads=[SP2.buf], writes=[Q4.buf])
            S.op("act", lambda e: e.activation(out=V3(Q5.ap[P52], 4), in_=TOT.ap[P52].unsqueeze(2).broadcast_to([52, 4, 128]), func=AF.Exp), reads=[TOT.buf], writes=[Q5.buf])
            slabs = [Q1, Q2, Q3, Q4, Q5, SQ6]
            for c in range(4):
                b = self.nb()
                ps = self.bank(b)

                def mmt(e, c=c, ps=ps):
                    for q, sb_ in enumerate(slabs):
                        ins = e.matmul(ps[:, q * 64:q * 64 + 52], lhsT=sb_.ap[P52, c * 128:(c + 1) * 128], rhs=idf.ap[0:52, 0:52], start=True, stop=True)
                    return ins
                S.op("pe", mmt, reads=[x.buf for x in slabs] + [idf.buf], writes=[self.pb[b]])
                S.op("dve", lambda e, c=c, ps=ps: e.tensor_copy(out=V3(TM[c].ap, 6)[:, :, 0:52], in_=V3(ps[:, 0:384], 6)[:, :, 0:52]), reads=[self.pb[b]], writes=[TM[c].buf])
            self.arena_reset(mark)
            tmv = [V3(TM[c].ap, 6) for c in range(4)]
            self.stop("stopP1")

            QIN = [self.tile(1024, BF16, "QIN%d" % d) for d in range(2)]
            KIN = [self.tile(1024, BF16, "KIN%d" % d) for d in range(2)]
            KST = [self.tile(1024, BF16, "KST%d" % d) for d in range(2)]
            DEC = self.tile(16, F32, "DECg")
            qtv = QT.v("p (c t) -> p c t", c=2)
            ktv = KT.v("p (c t) -> p c t", c=2)
            gm = self.arena_mark()
            for d in range(2):
                rb = 32 * d
                for cc in range(2):
                    if d or cc:
                        self.arena_reset(gm)
                    SPg = self.tile(512, F32, "SPg")
                    Gp = self.tile(512, F32, "Gp")
                    TOTg = self.tile(4, F32, "TOTg")
                    tg = self.tile(512, F32, "tg")
                    ex = [self.tile(512, F32, "ex%d" % i) for i in range(3)]
                    b = self.nb()
                    ps = self.bank(b)
                    S.op("pe", lambda e, ps=ps, rb=rb, cc=cc: e.matmul(ps, lhsT=upw.ap[rb:rb + 16, cc * 128:(cc + 1) * 128], rhs=LR.ap[rb:rb + 16, :], start=True, stop=True),
                         reads=[upw.buf, LR.buf], writes=[self.pb[b]])
                    S.op("act", lambda e, ps=ps, SPg=SPg, d=d, cc=cc: e.activation(out=SPg.ap, in_=ps, func=AF.Exp, scale=-1.0, bias=negb.ap[:, 2 * d + cc:2 * d + cc + 1]),
                         reads=[self.pb[b], negb.buf], writes=[SPg.buf])
                    S.op("act", lambda e, SPg=SPg: e.activation(out=SPg.ap, in_=SPg.ap, func=AF.Ln, bias=1.0), reads=[SPg.buf], writes=[SPg.buf])
                    S.op("dve", lambda e, SPg=SPg, Gp=Gp: e.tensor_tensor_scan(out=Gp.ap, data0=self.rmask.ap, data1=SPg.ap, initial=0.0, op0=ALU.mult, op1=ALU.add),
                         reads=[SPg.buf, self.rmask.buf], writes=[Gp.buf])
                    S.op("dve", lambda e, Gp=Gp, TOTg=TOTg: e.tensor_copy(out=TOTg.ap, in_=V3(Gp.ap, 4)[:, :, 127]), reads=[Gp.buf], writes=[TOTg.buf])
                    if d == 1:
                        S.op("dve", lambda e, SPg=SPg, Gp=Gp, tg=tg: e.tensor_tensor(out=tg.ap, in0=SPg.ap, in1=Gp.ap, op=ALU.subtract), reads=[SPg.buf, Gp.buf], writes=[tg.buf])
                        S.op("dve", lambda e, Gp=Gp, tg=tg, TOTg=TOTg: e.tensor_tensor(out=V3(Gp.ap, 4), in0=V3(tg.ap, 4), in1=TOTg.ap.unsqueeze(2).broadcast_to([128, 4, 128]), op=ALU.add),
                             reads=[tg.buf, TOTg.buf], writes=[Gp.buf])
                    S.op("act", lambda e, Gp=Gp, ex=ex: e.activation(out=ex[0].ap, in_=Gp.ap, func=AF.Exp, scale=-1.0 / 16, bias=float(np.log(0.125))), reads=[Gp.buf], writes=[ex[0].buf])
                    S.op("dve", lambda e, ex=ex, d=d, cc=cc: e.tensor_tensor(out=QIN[d].ap[:, cc * 512:(cc + 1) * 512], in0=qtv[:, cc, :], in1=ex[0].ap, op=ALU.mult),
                         reads=[QT.buf, ex[0].buf], writes=[QIN[d].buf])
                    S.op("act", lambda e, Gp=Gp, ex=ex: e.activation(out=ex[1].ap, in_=Gp.ap, func=AF.Exp, scale=1.0 / 16), reads=[Gp.buf], writes=[ex[1].buf])
                    S.op("pool", lambda e, ex=ex, d=d, cc=cc: e.tensor_tensor(out=KIN[d].ap[:, cc * 512:(cc + 1) * 512], in0=ktv[:, cc, :], in1=ex[1].ap, op=ALU.mult),
                         reads=[KT.buf, ex[1].buf], writes=[KIN[d].buf])
                    S.op("dve", lambda e, Gp=Gp, tg=tg, TOTg=TOTg: e.tensor_tensor(out=V3(tg.ap, 4), in0=V3(Gp.ap, 4), in1=TOTg.ap.unsqueeze(2).broadcast_to([128, 4, 128]), op=ALU.subtract),
                         reads=[Gp.buf, TOTg.buf], writes=[tg.buf])
                    S.op("act", lambda e, tg=tg, ex=ex: e.activation(out=ex[2].ap, in_=tg.ap, func=AF.Exp, scale=1.0 / 16), reads=[tg.buf], writes=[ex[2].buf])
                    S.op("pool", lambda e, ex=ex, d=d, cc=cc: e.tensor_tensor(out=KST[d].ap[:, cc * 512:(cc + 1) * 512], in0=ktv[:, cc, :], in1=ex[2].ap, op=ALU.mult),
                         reads=[KT.buf, ex[2].buf], writes=[KST[d].buf])
                    o4 = (d * 2 + cc) * 4
                    S.op("act", lambda e, TOTg=TOTg, o4=o4: e.activation(out=DEC.ap[:, o4:o4 + 4], in_=TOTg.ap, func=AF.Exp, scale=-1.0 / 16), reads=[TOTg.buf], writes=[DEC.buf])
            self.stop("stopG1")
            qinv = [V3(QIN[d].ap, 2) for d in range(2)]
            kinv = [V3(KIN[d].ap, 2) for d in range(2)]
            kstv = [V3(KST[d].ap, 2) for d in range(2)]
            vtv = VT.v("p (s c) -> p s c", s=4)
            for c in range(4):
                self.arena_reset(gm)
                ch = t * 4 + c
                csl = slice(c * 128, (c + 1) * 128)
                a1 = self.tile(512, F32, "a1")
                a2 = self.tile(512, F32, "a2")
                attT = self.tile(512, BF16, "attT")
                kstt = self.tile(512, BF16, "kstt")
                for d in range(2):
                    bA, bB = self.nb(), self.nb()
                    psA, psB = self.bank(bA), self.bank(bB)

                    def mma(e, d=d, psA=psA, psB=psB, csl=csl):
                        for h in range(4):
                            cc, ee = h // 2, h % 2
                            pso_ = psA if ee == 0 else psB
                            ins = e.matmul(pso_[:, cc * 128:(cc + 1) * 128], lhsT=kinv[d][64 * ee:64 * ee + 64, cc, csl], rhs=qinv[d][64 * ee:64 * ee + 64, cc, csl], start=True, stop=True)
                        return ins
                    S.op("pe", mma, reads=[KIN[d].buf, QIN[d].buf], writes=[self.pb[bA], self.pb[bB]])
                    at_, mk_ = (a1, self.maskL) if d == 0 else (a2, self.maskU)
                    for ee, (bb_, pp_) in enumerate(((bA, psA), (bB, psB))):
                        S.op("dve", lambda e, at_=at_, mk_=mk_, pp_=pp_, ee=ee: e.tensor_tensor(out=V3(at_.ap, 4)[:, ee:4:2, :], in0=V3(pp_[:, 0:256], 2), in1=V3(mk_.ap[:, 0:256], 2), op=ALU.mult),
                             reads=[self.pb[bb_], mk_.buf], writes=[at_.buf])
                S.op("pool", lambda e, a1=a1, a2=a2, attT=attT: e.tensor_tensor(out=attT.ap, in0=a1.ap, in1=a2.ap, op=ALU.add), reads=[a1.buf, a2.buf], writes=[attT.buf])
                self.stop("stopG2a")
                b = self.nb()
                ps = self.bank(b)

                def mmo(e, ps=ps, attT=attT, c=c):
                    for h in range(4):
                        ins = e.matmul(ps[:, h * 128:(h + 1) * 128], lhsT=attT.ap[:, h * 128:(h + 1) * 128], rhs=vtv[:, c, h * 128:(h + 1) * 128], start=True, stop=True)
                    return ins
                S.op("pe", mmo, reads=[attT.buf, VT.buf], writes=[self.pb[b]])
                S.op("act", lambda e, ps=ps, c=c: e.activation(out=OG[c % 2].ap, in_=ps, func=AF.Copy), reads=[self.pb[b]], writes=[OG[c % 2].buf])
                S.dma("sp", self.OI[ch * 128:(ch + 1) * 128, 0:512], OG[c % 2].ap, reads=[OG[c % 2].buf])
                S.dma("sp", self.OI[ch * 128:(ch + 1) * 128, 512:1024], ZZ.ap, reads=[ZZ.buf])
                self.stop("stopG2b")
                b = self.nb()
                pbf = self.bank(b, BF16)

                def trk(e, pbf=pbf, csl=csl):
                    for d in range(2):
                        for cc in range(2):
                            ins = e.transpose(pbf[:, (d * 2 + cc) * 128:(d * 2 + cc + 1) * 128], kstv[d][:, cc, csl], idb.ap)
                    return ins
                S.op("pe", trk, reads=[KST[0].buf, KST[1].buf, idb.buf], writes=[self.pb[b]])
                S.op("dve", lambda e, pbf=pbf, kstt=kstt: e.tensor_copy(out=kstt.ap, in_=pbf[:, 0:512]), reads=[self.pb[b]], writes=[kstt.buf])
                self.stop("stopG2")
                for d in range(2):
                    S.dma("sp", V3(self.RB[d][ch][:, 0:256], 2), qinv[d][:, :, csl], reads=[QIN[d].buf])
                    S.dma("sp", self.RB[d][ch][:, 256:512], kstt.ap[:, d * 256:(d + 1) * 256], reads=[kstt.buf])
                    S.dma("sp", self.RF[d][ch][:, 512:514], V3(DEC.ap, 4)[:, 2 * d:2 * d + 2, c], reads=[DEC.buf], slow=True)
            self.arena_reset(mark)

            self.stop("stopP2")
            CVS = self.tile(10 * 512, BF16, "CVS")
            cvs = CVS.v("p (c t) -> p c t", c=10)
            for cc in range(10):
                b = self.nb()
                ps = self.bank(b)

                def mmc(e, cc=cc, ps=ps):
                    for k in range(5):
                        i = (12 + cc) * 5 + k
                        ins = e.matmul(ps, lhsT=dg.ap[:, i * 128:(i + 1) * 128], rhs=rawv[:, 12 + cc, k:k + 512], start=(k == 0), stop=(k == 4))
                    return ins
                S.op("pe", mmc, reads=[dg.buf, RAWC.buf], writes=[self.pb[b]])
                S.op("act", lambda e, cc=cc, ps=ps: e.activation(out=cvs[:, cc, :], in_=ps, func=AF.Silu, bias=cwT.ap[:, 110 + cc:111 + cc]), reads=[self.pb[b], cwT.buf], writes=[CVS.buf])
            sm = self.arena_mark()
            for c in range(4):
                if c:
                    self.arena_reset(sm)
                ch = t * 4 + c
                csl = slice(c * 128, (c + 1) * 128)
                XTOK = self.tile(1024, BF16, "XTOK")
                BTOK = self.tile(128, BF16, "BTOK")
                CBT = self.tile(256, F32, "CBT")
                b = self.nb()
                pbf = self.bank(b, BF16)

                def trx(e, pbf=pbf, csl=csl):
                    for cc in range(8):
                        ins = e.transpose(pbf[:, cc * 128:(cc + 1) * 128], cvs[:, cc, csl], idb.ap)
                    return ins
                S.op("pe", trx, reads=[CVS.buf, idb.buf], writes=[self.pb[b]])
                S.op("act", lambda e, pbf=pbf, XTOK=XTOK: e.activation(out=XTOK.ap, in_=pbf, func=AF.Copy), reads=[self.pb[b]], writes=[XTOK.buf])
                b = self.nb()
                pbf = self.bank(b, BF16)
                S.op("pe", lambda e, pbf=pbf, csl=csl: e.transpose(pbf[:, 0:128], cvs[:, 8, csl], idb.ap), reads=[CVS.buf, idb.buf], writes=[self.pb[b]])
                S.op("dve", lambda e, pbf=pbf, BTOK=BTOK: e.tensor_copy(out=BTOK.ap, in_=pbf[:, 0:128]), reads=[self.pb[b]], writes=[BTOK.buf])
                S.dma("sp", self.RS[ch][:, 0:128], cvs[:, 9, csl], reads=[CVS.buf])
                S.dma("sp", self.RS[ch][:, 128:256], BTOK.ap, reads=[BTOK.buf])
                bA, bB = self.nb(), self.nb()
                psA, psB = self.bank(bA), self.bank(bB)

                def mmcb(e, psA=psA, psB=psB, csl=csl):
                    e.matmul(psA[:, 0:128], lhsT=cvs[0:64, 8, csl], rhs=cvs[0:64, 9, csl], start=True, stop=True)
                    return e.matmul(psB[:, 0:128], lhsT=cvs[64:128, 8, csl], rhs=cvs[64:128, 9, csl], start=True, stop=True)
                S.op("pe", mmcb, reads=[CVS.buf], writes=[self.pb[bA], self.pb[bB]])
                S.op("act", lambda e, psA=psA, CBT=CBT: e.activation(out=CBT.ap[:, 0:128], in_=psA[:, 0:128], func=AF.Copy), reads=[self.pb[bA]], writes=[CBT.buf])
                S.op("act", lambda e, psB=psB, CBT=CBT: e.activation(out=CBT.ap[:, 128:256], in_=psB[:, 0:128], func=AF.Copy), reads=[self.pb[bB]], writes=[CBT.buf])
                MT = [self.tile(512, BF16, "MT%d" % g4) for g4 in range(4)]
                for g4 in range(4):
                    LP = []
                    for d in range(2):
                        rb = 32 * d
                        LT = self.tile(512, F32, "LT%d" % d)
                        b = self.nb()
                        ps = self.bank(b)

                        def mms(e, ps=ps, rb=rb, g4=g4, csl=csl):
                            for hh in range(4):
                                h = 4 * g4 + hh
                                ins = e.matmul(ps[:, hh * 128:(hh + 1) * 128], lhsT=self.sel.ap[rb:rb + 20, h * 128:(h + 1) * 128], rhs=SA.ap[rb:rb + 20, csl], start=True, stop=True)
                            return ins
                        S.op("pe", mms, reads=[self.sel.buf, SA.buf], writes=[self.pb[b]])
                        S.op("dve", lambda e, ps=ps, LT=LT, rb=rb, g4=g4, c=c: e.tensor_tensor(out=V3(LT.ap, 4), in0=V3(ps, 4),
                                                                                             in1=tmv[c][:, 0, rb + 4 * g4:rb + 4 * g4 + 4].unsqueeze(2).broadcast_to([128, 4, 128]), op=ALU.add),
                             reads=[self.pb[b], TM[c].buf], writes=[LT.buf])
                        pat, cm = (([[0, 4], [1, 128]], -1) if d == 0 else ([[0, 4], [-1, 128]], 1))
                        S.op("pool", lambda e, LT=LT, pat=pat, cm=cm: e.affine_select(out=V3(LT.ap, 4), in_=V3(LT.ap, 4), pattern=pat, compare_op=ALU.is_ge, fill=self.freg(e, -30000.0),
                                                                                    base=0, channel_multiplier=cm), reads=[LT.buf], writes=[LT.buf])
                        S.op("act", lambda e, LT=LT: e.activation(out=LT.ap, in_=LT.ap, func=AF.Exp), reads=[LT.buf], writes=[LT.buf])
                        LP.append(LT)
                    g = g4 // 2
                    S.op("pool", lambda e, LP=LP: e.tensor_tensor(out=LP[0].ap, in0=LP[0].ap, in1=LP[1].ap, op=ALU.add), reads=[LP[0].buf, LP[1].buf], writes=[LP[0].buf])
                    S.op("dve", lambda e, LP=LP, g=g, CBT=CBT: e.tensor_tensor(out=V3(LP[0].ap, 4), in0=V3(LP[0].ap, 4), in1=CBT.ap[:, g * 128:(g + 1) * 128].unsqueeze(1).broadcast_to([128, 4, 128]),
                                                                            op=ALU.mult), reads=[LP[0].buf, CBT.buf], writes=[LP[0].buf])
                    S.op("pool", lambda e, LP=LP, g4=g4: e.tensor_tensor(out=MT[g4].ap, in0=LP[0].ap, in1=did.ap[:, g4 * 512:(g4 + 1) * 512], op=ALU.add),
                         reads=[LP[0].buf, did.buf], writes=[MT[g4].buf])
                for half in range(2):
                    b = self.nb()
                    ps = self.bank(b)

                    def mmy(e, ps=ps, half=half, XTOK=XTOK):
                        for hh in range(8):
                            h = half * 8 + hh
                            ins = e.matmul(ps[:, hh * 64:(hh + 1) * 64], lhsT=MT[h // 4].ap[:, (h % 4) * 128:(h % 4 + 1) * 128], rhs=XTOK.ap[:, h * 64:(h + 1) * 64], start=True, stop=True)
                        return ins
                    S.op("pe", mmy, reads=[m_.buf for m_ in MT] + [XTOK.buf], writes=[self.pb[b]])
                    S.op("act", lambda e, ps=ps, half=half, c=c: e.activation(out=OS[c % 2].ap[:, half * 512:(half + 1) * 512], in_=ps, func=AF.Copy),
                         reads=[self.pb[b]], writes=[OS[c % 2].buf])
                for d in range(2):
                    rb = 32 * d
                    XS = self.tile(1024, BF16, "XS%d" % d)
                    eng = "dve" if d == 0 else "pool"
                    S.op(eng, lambda e, XS=XS, XTOK=XTOK, rb=rb, c=c: e.tensor_tensor(out=V3(XS.ap, 16), in0=V3(XTOK.ap, 16),
                                                                                   in1=tmv[c][:, 1, rb:rb + 16].unsqueeze(2).broadcast_to([128, 16, 64]), op=ALU.mult),
                         reads=[XTOK.buf, TM[c].buf], writes=[XS.buf])
                    S.dma("sp", self.RB[d][ch][:, 2560:3584], XS.ap, reads=[XS.buf])
                    S.dma("sp", self.RF[d][ch][:, 518:534], tmv[c][:, 2, rb:rb + 16], reads=[TM[c].buf], slow=True)
                    S.dma("sp", self.RF[d][ch][:, 534:550], tmv[c][:, 4, rb:rb + 16], reads=[TM[c].buf], slow=True)
                S.dma("sp", self.OI[ch * 128:(ch + 1) * 128, 1024:2048], OS[c % 2].ap, reads=[OS[c % 2].buf])
            self.arena_reset(mark)

            self.stop("stopP3")
            CVG = self.tile(12 * 512, BF16, "CVG")
            cvg = CVG.v("p (c t) -> p c t", c=12)
            for cc in range(12):
                b = self.nb()
                ps = self.bank(b)

                def mmc(e, cc=cc, ps=ps):
                    for k in range(5):
                        i = cc * 5 + k
                        ins = e.matmul(ps, lhsT=dg.ap[:, i * 128:(i + 1) * 128], rhs=rawv[:, cc, k:k + 512], start=(k == 0), stop=(k == 4))
                    return ins
                S.op("pe", mmc, reads=[dg.buf, RAWC.buf], writes=[self.pb[b]])
                S.op("act", lambda e, cc=cc, ps=ps: e.activation(out=cvg[:, cc, :], in_=ps, func=AF.Silu), reads=[self.pb[b]], writes=[CVG.buf])
            QKN = self.tile(8 * 512, BF16, "QKN")
            qkn = QKN.v("p (c t) -> p c t", c=8)
            gmark = self.arena_mark()
            SQs = [self.tile(512, BF16, "SQ%d" % i) for i in range(2)]
            RSTs = [self.tile(512, F32, "RST%d" % i) for i in range(2)]
            for cc in range(8):
                SQ = SQs[cc % 2]
                RST = RSTs[cc % 2]
                S.op("pool", lambda e, cc=cc, SQ=SQ: e.tensor_tensor(out=SQ.ap, in0=cvg[:, cc, :], in1=cvg[:, cc, :], op=ALU.mult), reads=[CVG.buf], writes=[SQ.buf])
                b = self.nb()
                ps = self.bank(b)
                S.op("pe", lambda e, ps=ps, SQ=SQ: e.matmul(ps, lhsT=self.onesb.ap, rhs=SQ.ap, start=True, stop=True), reads=[self.onesb.buf, SQ.buf], writes=[self.pb[b]])
                S.op("act", lambda e, ps=ps, RST=RST: e.activation(out=RST.ap, in_=ps, func=AF.Ln, bias=EPS), reads=[self.pb[b]], writes=[RST.buf])
                bias = float(np.log(128.0 ** -0.5)) if cc < 4 else 0.0
                S.op("act", lambda e, RST=RST, bias=bias: e.activation(out=RST.ap, in_=RST.ap, func=AF.Exp, scale=-0.5, bias=bias), reads=[RST.buf], writes=[RST.buf])
                S.op("dve", lambda e, cc=cc, RST=RST: e.tensor_tensor(out=qkn[:, cc, :], in0=cvg[:, cc, :], in1=RST.ap, op=ALU.mult), reads=[CVG.buf, RST.buf], writes=[QKN.buf])
            self.stop("stopD2")
            for c in range(4):
                self.arena_reset(gmark)
                ch = t * 4 + c
                csl = slice(c * 128, (c + 1) * 128)
                KVT = self.tile(1024, BF16, "KVT")
                NKK = self.tile(512, F32, "NKK")
                QKT = self.tile(512, F32, "QKT")
                b = self.nb()
                pbf = self.bank(b, BF16)

                def trkv(e, pbf=pbf, csl=csl):
                    for h in range(4):
                        ins = e.transpose(pbf[:, h * 128:(h + 1) * 128], qkn[:, 4 + h, csl], idb.ap)
                    for h in range(4):
                        ins = e.transpose(pbf[:, 512 + h * 128:512 + (h + 1) * 128], cvg[:, 8 + h, csl], idb.ap)
                    return ins
                S.op("pe", trkv, reads=[QKN.buf, CVG.buf, idb.buf], writes=[self.pb[b]])
                S.op("act", lambda e, pbf=pbf, KVT=KVT: e.activation(out=KVT.ap, in_=pbf, func=AF.Copy), reads=[self.pb[b]], writes=[KVT.buf])
                b = self.nb()
                ps = self.bank(b)

                def mmkk(e, ps=ps, csl=csl):
                    for h in range(4):
                        ins = e.matmul(ps[:, h * 128:(h + 1) * 128], lhsT=qkn[:, 4 + h, csl], rhs=qkn[:, 4 + h, csl], start=True, stop=True)
                    return ins
                S.op("pe", mmkk, reads=[QKN.buf], writes=[self.pb[b]])
                S.op("act", lambda e, ps=ps, NKK=NKK: e.activation(out=NKK.ap, in_=ps, func=AF.Copy, scale=-1.0), reads=[self.pb[b]], writes=[NKK.buf])
                b = self.nb()
                ps = self.bank(b)

                def mmqk(e, ps=ps, csl=csl):
                    for h in range(4):
                        ins = e.matmul(ps[:, h * 128:(h + 1) * 128], lhsT=qkn[:, 4 + h, csl], rhs=qkn[:, h, csl], start=True, stop=True)
                    return ins
                S.op("pe", mmqk, reads=[QKN.buf], writes=[self.pb[b]])
                S.op("dve", lambda e, ps=ps, QKT=QKT: e.tensor_copy(out=QKT.ap, in_=ps), reads=[self.pb[b]], writes=[QKT.buf])
                self.stop("stopD3")
                U0 = [self.tile(512, BF16, "U0_%d" % d) for d in range(2)]
                N0 = [self.tile(512, BF16, "N0_%d" % d) for d in range(2)]
                emark = self.arena_mark()
                for d in range(2):
                    rb = 32 * d
                    bg = self.nb()
                    psg = self.bank(bg)
                    bgb = self.nb()
                    psgb = self.bank(bgb)

                    def mmbg(e, psg=psg, rb=rb, csl=csl, src=SA):
                        for h in range(4):
                            ins = e.matmul(psg[:, h * 128:(h + 1) * 128], lhsT=self.sel.ap[rb:rb + 20, (16 + h) * 128:(17 + h) * 128], rhs=src.ap[rb:rb + 20, csl], start=True, stop=True)
                        return ins
                    S.op("pe", mmbg, reads=[self.sel.buf, SA.buf], writes=[self.pb[bg]])
                    S.op("pe", lambda e, psgb=psgb, rb=rb, csl=csl: mmbg(e, psgb, rb, csl, SQ6), reads=[self.sel.buf, SQ6.buf], writes=[self.pb[bgb]])
                    nG = tmv[c][:, 0, rb + 16:rb + 20].unsqueeze(2).broadcast_to([128, 4, 128])
                    GB = tmv[c][:, 5, rb + 16:rb + 20].unsqueeze(2).broadcast_to([128, 4, 128])
                    E3 = self.tile(512, F32, "E3_%d" % d)
                    E1 = self.tile(512, F32, "E1_%d" % d)
                    E2 = self.tile(512, F32, "E2_%d" % d)
                    EG = self.tile(512, F32, "EG_%d" % d)
                    AQK = self.tile(512, BF16, "AQK%d" % d)
                    QDT = self.tile(512, BF16, "QDT%d" % d)
                    if d == 0:
                        m3 = ([[0, 4], [1, 128]], -1, 0)
                        m1 = ([[0, 4], [1, 128]], -1, -1)
                        m2 = ([[0, 4], [-1, 128]], 1, -1)
                    else:
                        m3 = ([[0, 4], [-1, 128]], 1, 0)
                        m1 = ([[0, 4], [-1, 128]], 1, -1)
                        m2 = ([[0, 4], [1, 128]], -1, -1)
                    S.op("dve", lambda e, E3=E3, psg=psg, nG=nG: e.tensor_tensor(out=V3(E3.ap, 4), in0=V3(psg, 4), in1=nG, op=ALU.add), reads=[self.pb[bg], TM[c].buf], writes=[E3.buf])
                    S.op("dve", lambda e, E1=E1, psgb=psgb, nG=nG: e.tensor_tensor(out=V3(E1.ap, 4), in0=V3(psgb, 4), in1=nG, op=ALU.add), reads=[self.pb[bgb], TM[c].buf], writes=[E1.buf])
                    S.op("dve", lambda e, E2=E2, psg=psg, GB=GB: e.scalar_tensor_tensor(out=V3(E2.ap, 4), in0=V3(psg, 4), scalar=-1.0, in1=GB, op0=ALU.mult, op1=ALU.add),
                         reads=[self.pb[bg], TM[c].buf], writes=[E2.buf])
                    S.op("act", lambda e, EG=EG, psg=psg: e.activation(out=EG.ap, in_=psg, func=AF.Exp), reads=[self.pb[bg]], writes=[EG.buf])
                    for Et, (pat, cm, base_) in ((E3, m3), (E1, m1), (E2, m2)):
                        S.op("pool", lambda e, Et=Et, pat=pat, cm=cm, base_=base_: e.affine_select(out=V3(Et.ap, 4), in_=V3(Et.ap, 4), pattern=pat, compare_op=ALU.is_ge, fill=self.freg(e, -30000.0),
                                                                                                 base=base_, channel_multiplier=cm), reads=[Et.buf], writes=[Et.buf])
                        S.op("act", lambda e, Et=Et: e.activation(out=Et.ap, in_=Et.ap, func=AF.Exp), reads=[Et.buf], writes=[Et.buf])
                    S.op("dve", lambda e, AQK=AQK, E3=E3, QKT=QKT: e.tensor_tensor(out=AQK.ap, in0=QKT.ap, in1=E3.ap, op=ALU.mult), reads=[QKT.buf, E3.buf], writes=[AQK.buf])
                    S.op("pool", lambda e, d=d, E1=E1, NKK=NKK: e.tensor_tensor(out=U0[d].ap, in0=NKK.ap, in1=E1.ap, op=ALU.mult), reads=[NKK.buf, E1.buf], writes=[U0[d].buf])
                    S.op("pool", lambda e, d=d, E2=E2, NKK=NKK: e.tensor_tensor(out=N0[d].ap, in0=NKK.ap, in1=E2.ap, op=ALU.mult), reads=[NKK.buf, E2.buf], writes=[N0[d].buf])
                    S.op("dve", lambda e, QDT=QDT, EG=EG, csl=csl: e.tensor_tensor(out=V3(QDT.ap, 4), in0=qkn[:, 0:4, csl], in1=V3(EG.ap, 4), op=ALU.mult), reads=[QKN.buf, EG.buf], writes=[QDT.buf])
                    S.dma("sp", self.RB[d][ch][:, 1024:1536], AQK.ap, reads=[AQK.buf])
                    S.dma("sp", self.RB[d][ch][:, 1536:2048], QDT.ap, reads=[QDT.buf])
                    S.dma("sp", self.RF[d][ch][:, 514:518], tmv[c][:, 4, rb + 16:rb + 20], reads=[TM[c].buf], slow=True)
                self.arena_reset(emark)
                idrep = idf.ap.unsqueeze(1).broadcast_to([128, 4, 128])

                def mm4(e, ps, lt, rt):
                    for h in range(4):
                        ins = e.matmul(ps[:, h * 128:(h + 1) * 128], lhsT=lt.ap[:, h * 128:(h + 1) * 128], rhs=rt.ap[:, h * 128:(h + 1) * 128], start=True, stop=True)
                    return ins

                def mmop(lt, rt):
                    b = self.nb()
                    ps = self.bank(b)
                    S.op("pe", lambda e, ps=ps, lt=lt, rt=rt: mm4(e, ps, lt, rt), reads=[lt.buf, rt.buf], writes=[self.pb[b]])
                    return b, ps

                def evac(eng, b, ps, o):
                    if eng == "act":
                        S.op("act", lambda e, ps=ps, o=o: e.activation(out=o.ap, in_=ps, func=AF.Copy), reads=[self.pb[b]], writes=[o.buf])
                    else:
                        S.op("dve", lambda e, ps=ps, o=o: e.tensor_copy(out=o.ap, in_=ps), reads=[self.pb[b]], writes=[o.buf])

                def evacadd(b, ps, o, xi):
                    S.op("dve", lambda e, ps=ps, o=o, xi=xi: e.tensor_tensor(out=o.ap, in0=ps, in1=xi.ap, op=ALU.add), reads=[self.pb[b], xi.buf], writes=[o.buf])

                XT = [None, None]
                for d in range(2):
                    Nn = [self.tile(512, BF16, "Nn%d%d" % (d, i)) for i in range(2)]
                    Un = [self.tile(512, BF16, "Un%d%d" % (d, i)) for i in range(2)]
                    Yy = [self.tile(512, BF16, "Yy%d%d" % (d, i)) for i in range(2)]
                    Yt = [self.tile(512, BF16, "Yt%d%d" % (d, i)) for i in range(2)]
                    No = self.tile(512, BF16, "No%d" % d)
                    Uo = self.tile(512, BF16, "Uo%d" % d)
                    Zz = self.tile(512, BF16, "Zz%d" % d)
                    Zp = self.tile(512, BF16, "Zp%d" % d)
                    S.op("pool", lambda e, d=d, o=Nn[0]: e.tensor_tensor(out=o.ap, in0=N0[d].ap, in1=self.bmask[0].ap, op=ALU.mult), reads=[N0[d].buf, self.bmask[0].buf], writes=[Nn[0].buf])
                    S.op("dve", lambda e, d=d, o=Un[0]: e.tensor_tensor(out=o.ap, in0=U0[d].ap, in1=self.bmask[0].ap, op=ALU.mult), reads=[U0[d].buf, self.bmask[0].buf], writes=[Un[0].buf])
                    S.op("pool", lambda e, o=Yy[0], i_=Nn[0]: e.tensor_tensor(out=V3(o.ap, 4), in0=V3(i_.ap, 4), in1=idrep, op=ALU.add), reads=[Nn[0].buf, idf.buf], writes=[Yy[0].buf])
                    S.op("pool", lambda e, o=Yt[0], i_=Un[0]: e.tensor_tensor(out=V3(o.ap, 4), in0=V3(i_.ap, 4), in1=idrep, op=ALU.add), reads=[Un[0].buf, idf.buf], writes=[Yt[0].buf])
                    cur = 0
                    for m in range(3):
                        nxt = 1 - cur
                        b, ps = mmop(Nn[cur], Un[cur])
                        evac("act", b, ps, Un[nxt])
                        b, ps = mmop(Un[cur], Nn[cur])
                        evac("dve", b, ps, Nn[nxt])
                        b, ps = mmop(Nn[nxt], Yt[cur])
                        evacadd(b, ps, Yt[nxt], Yt[cur])
                        b, ps = mmop(Un[nxt], Yy[cur])
                        evacadd(b, ps, Yy[nxt], Yy[cur])
                        cur = nxt
                    for lvl in range(3):
                        nxt = 1 - cur
                        mk = self.bmask[1 + lvl]
                        S.op("pool", lambda e, d=d, mk=mk, No=No: e.tensor_tensor(out=No.ap, in0=N0[d].ap, in1=mk.ap, op=ALU.mult), reads=[N0[d].buf, mk.buf], writes=[No.buf])
                        b, ps = mmop(No, Yt[cur])
                        evac("act", b, ps, Zz)
                        if lvl < 2:
                            S.op("pool", lambda e, d=d, mk=mk, Uo=Uo: e.tensor_tensor(out=Uo.ap, in0=U0[d].ap, in1=mk.ap, op=ALU.mult), reads=[U0[d].buf, mk.buf], writes=[Uo.buf])
                            b, ps = mmop(Uo, Yy[cur])
                            evac("dve", b, ps, Zp)
                        b, ps = mmop(Yy[cur], Zz)
                        evacadd(b, ps, Yt[nxt], Yt[cur])
                        if lvl < 2:
                            b, ps = mmop(Yt[cur], Zp)
                            evacadd(b, ps, Yy[nxt], Yy[cur])
                        cur = nxt
                    XT[d] = Yt[cur]
                for d in range(2):
                    rb = 32 * d
                    Xt = XT[d]
                    RK = self.tile(512, BF16, "RK%d" % d)
                    RV = self.tile(512, BF16, "RV%d" % d)
                    KD = self.tile(512, BF16, "KD%d" % d)
                    WT = self.tile(512, BF16, "WT%d" % d)
                    UU = self.tile(512, F32, "UU%d" % d)
                    bc = lambda q: tmv[c][:, q, rb + 16:rb + 20].unsqueeze(2).broadcast_to([128, 4, 128])
                    bc1, bc2, bc3 = bc(1), bc(2), bc(3)
                    S.op("dve", lambda e, RK=RK, KVT=KVT, bc2=bc2: e.tensor_tensor(out=V3(RK.ap, 4), in0=V3(KVT.ap[:, 0:512], 4), in1=bc2, op=ALU.mult), reads=[KVT.buf, TM[c].buf], writes=[RK.buf])
                    S.op("pool", lambda e, RV=RV, KVT=KVT, bc3=bc3: e.tensor_tensor(out=V3(RV.ap, 4), in0=V3(KVT.ap[:, 512:1024], 4), in1=bc3, op=ALU.mult), reads=[KVT.buf, TM[c].buf], writes=[RV.buf])
                    S.op("pool", lambda e, KD=KD, KVT=KVT, bc1=bc1: e.tensor_tensor(out=V3(KD.ap, 4), in0=V3(KVT.ap[:, 0:512], 4), in1=bc1, op=ALU.mult), reads=[KVT.buf, TM[c].buf], writes=[KD.buf])
                    S.dma("sp", self.RB[d][ch][:, 2048:2560], KD.ap, reads=[KD.buf])
                    b = self.nb()
                    ps = self.bank(b)

                    def mmw(e, ps=ps, RK=RK, Xt=Xt):
                        for h in range(4):
                            ins = e.matmul(ps[:, h * 128:(h + 1) * 128], lhsT=RK.ap[:, h * 128:(h + 1) * 128], rhs=Xt.ap[:, h * 128:(h + 1) * 128], start=True, stop=True)
                        return ins
                    S.op("pe", mmw, reads=[RK.buf, Xt.buf], writes=[self.pb[b]])
                    S.op("act", lambda e, ps=ps, WT=WT: e.activation(out=WT.ap, in_=ps, func=AF.Copy), reads=[self.pb[b]], writes=[WT.buf])
                    S.dma("sp", self.RB[d][ch][:, 512:1024], WT.ap, reads=[WT.buf])
                    b = self.nb()
                    ps = self.bank(b)

                    def mmu(e, ps=ps, RV=RV, Xt=Xt):
                        for h in range(4):
                            ins = e.matmul(ps[:, h * 128:(h + 1) * 128], lhsT=Xt.ap[:, h * 128:(h + 1) * 128], rhs=RV.ap[:, h * 128:(h + 1) * 128], start=True, stop=True)
                        return ins
                    S.op("pe", mmu, reads=[RV.buf, Xt.buf], writes=[self.pb[b]])
                    S.op("dve", lambda e, ps=ps, UU=UU: e.tensor_copy(out=UU.ap, in_=ps), reads=[self.pb[b]], writes=[UU.buf])
                    S.dma("sp", self.RF[d][ch][:, 0:512], UU.ap, reads=[UU.buf])
            self.arena_reset(mark)

    def pass_B(self, l):
        S = self.S
        self.reset()
        self._bank_rr = 0
        V3 = lambda ap, h: ap.rearrange("p (h i) -> p h i", h=h)
        XB, XF = self.XB, self.XF
        NBUF = 2
        RBt = [[self.tile(XB, BF16, "RBt%d%d" % (d, i)) for i in range(NBUF)] for d in range(2)]
        RFt = [[self.tile(XF, F32, "RFt%d%d" % (d, i)) for i in range(NBUF)] for d in range(2)]
        RSt = [[self.tile(256, BF16, "RSt%d%d" % (d, i)) for i in range(NBUF)] for d in range(2)]
        Vt = [[self.tile(512, BF16, "Vt%d%d" % (d, i)) for i in range(NBUF)] for d in range(2)]
        Ot = [[self.tile(2048, F32, "Ot%d%d" % (d, i)) for i in range(NBUF)] for d in range(2)]
        Sg = [[self.tile(128, F32, "Sg%d%d" % (d, hp)) for hp in range(2)] for d in range(2)]
        Sgb = [[self.tile(128, BF16, "Sgb%d%d" % (d, hp)) for hp in range(2)] for d in range(2)]
        Sd = [self.tile(512, F32, "Sd%d" % d) for d in range(2)]
        Sdb = [self.tile(512, BF16, "Sdb%d" % d) for d in range(2)]
        Ss = [self.tile(512, F32, "Ss%d" % d) for d in range(2)]
        Ssb = [self.tile(512, BF16, "Ssb%d" % d) for d in range(2)]
        UP = [self.tile(512, BF16, "UP%d" % d) for d in range(2)]

        def load(c, d, i):
            S.dma("sp", RBt[d][i].ap, self.RB[d][c], writes=[RBt[d][i].buf])
            S.dma("sp", RFt[d][i].ap[:, 0:550], self.RF[d][c][:, 0:550], writes=[RFt[d][i].buf])
            S.dma("sp", RSt[d][i].ap, self.RS[c], writes=[RSt[d][i].buf])
            S.dma("sp", Vt[d][i].ap, self.Tv[c * 128:(c + 1) * 128, :], writes=[Vt[d][i].buf])

        def step(c, d, i):
            rb_, rf_, rs_, vt_, ot_ = RBt[d][i], RFt[d][i], RSt[d][i], Vt[d][i], Ot[d][i]
            rb = rb_.ap
            rf = rf_.ap
            b1 = self.nb()
            ps1 = self.bank(b1)

            def mm1(e):
                for h in range(4):
                    ins = e.matmul(ps1[:, h * 128:(h + 1) * 128], lhsT=rb[:, 512 + h * 128:512 + (h + 1) * 128], rhs=Sdb[d].ap[:, h * 128:(h + 1) * 128], start=True, stop=True)
                return ins
            S.op("pe", mm1, reads=[rb_.buf, Sdb[d].buf], writes=[self.pb[b1]])
            S.op("dve", lambda e: e.tensor_tensor(out=UP[d].ap, in0=rf[:, 0:512], in1=ps1, op=ALU.subtract), reads=[rf_.buf, self.pb[b1]], writes=[UP[d].buf])
            b2 = self.nb()
            ps2 = self.bank(b2)

            def mm2(e):
                for h in range(4):
                    hs = slice(h * 128, (h + 1) * 128)
                    e.matmul(ps2[:, hs], lhsT=rb[:, 1536 + h * 128:1536 + (h + 1) * 128], rhs=Sdb[d].ap[:, hs], start=True, stop=False)
                    ins = e.matmul(ps2[:, hs], lhsT=rb[:, 1024 + h * 128:1024 + (h + 1) * 128], rhs=UP[d].ap[:, hs], start=False, stop=True)
                return ins
            S.op("pe", mm2, reads=[rb_.buf, Sdb[d].buf, UP[d].buf], writes=[self.pb[b2]])
            S.op("act", lambda e: e.activation(out=ot_.ap[:, 512:1024], in_=ps2, func=AF.Copy), reads=[self.pb[b2]], writes=[ot_.buf])
            b3 = self.nb()
            ps3 = self.bank(b3)

            def mm3(e):
                for h in range(4):
                    hs = slice(h * 128, (h + 1) * 128)
                    ins = e.matmul(ps3[:, hs], lhsT=rb[:, 2048 + h * 128:2048 + (h + 1) * 128], rhs=UP[d].ap[:, hs], start=True, stop=True)
                return ins
            S.op("pe", mm3, reads=[rb_.buf, UP[d].buf], writes=[self.pb[b3]])
            S.op("dve", lambda e: e.tensor_tensor(out=V3(Sd[d].ap, 4), in0=V3(Sd[d].ap, 4), in1=rf[:, 514:518].unsqueeze(2).broadcast_to([128, 4, 128]), op=ALU.mult),
                 reads=[Sd[d].buf, rf_.buf], writes=[Sd[d].buf])
            S.op("dve", lambda e: e.tensor_tensor(out=Sd[d].ap, in0=Sd[d].ap, in1=ps3, op=ALU.add), reads=[Sd[d].buf, self.pb[b3]], writes=[Sd[d].buf])
            S.op("act", lambda e: e.activation(out=Sdb[d].ap, in_=Sd[d].ap, func=AF.Copy), reads=[Sd[d].buf], writes=[Sdb[d].buf])
            boA, boB = self.nb(), self.nb()
            psoA, psoB = self.bank(boA), self.bank(boB)

            def mmgo(e):
                for h in range(4):
                    hp, ee = h // 2, h % 2
                    pso_ = psoA if ee == 0 else psoB
                    ins = e.matmul(pso_[:, hp * 128:(hp + 1) * 128], lhsT=rb[64 * ee:64 * ee + 64, hp * 128:(hp + 1) * 128], rhs=Sgb[d][hp].ap[64 * ee:64 * ee + 64, :], start=True, stop=True)
                return ins
            S.op("pe", mmgo, reads=[rb_.buf, Sgb[d][0].buf, Sgb[d][1].buf], writes=[self.pb[boA], self.pb[boB]])
            S.op("act", lambda e: e.activation(out=V3(ot_.ap[:, 0:512], 4)[:, 0:4:2, :], in_=V3(psoA[:, 0:256], 2), func=AF.Copy), reads=[self.pb[boA]], writes=[ot_.buf])
            S.op("act", lambda e: e.activation(out=V3(ot_.ap[:, 0:512], 4)[:, 1:4:2, :], in_=V3(psoB[:, 0:256], 2), func=AF.Copy), reads=[self.pb[boB]], writes=[ot_.buf])
            bs = self.nb()
            pss = self.bank(bs)

            def mmgs(e):
                for hp in range(2):
                    ins = e.matmul(pss[:, hp * 256:(hp + 1) * 256], lhsT=rb[:, 256 + hp * 128:256 + (hp + 1) * 128], rhs=vt_.ap[:, hp * 256:(hp + 1) * 256], start=True, stop=True)
                return ins
            S.op("pe", mmgs, reads=[rb_.buf, vt_.buf], writes=[self.pb[bs]])
            for hp in range(2):
                for ee in range(2):
                    psl = slice(64 * ee, 64 * ee + 64)
                    S.op("dve", lambda e, hp=hp, ee=ee, psl=psl: e.scalar_tensor_tensor(out=Sg[d][hp].ap[psl, :], in0=Sg[d][hp].ap[psl, :], scalar=rf[psl, 512 + hp:513 + hp],
                                                                                     in1=pss[psl, hp * 256 + ee * 128:hp * 256 + (ee + 1) * 128], op0=ALU.mult, op1=ALU.add),
                         reads=[Sg[d][hp].buf, rf_.buf, self.pb[bs]], writes=[Sg[d][hp].buf])
                S.op("act", lambda e, hp=hp: e.activation(out=Sgb[d][hp].ap, in_=Sg[d][hp].ap, func=AF.Copy), reads=[Sg[d][hp].buf], writes=[Sgb[d][hp].buf])
            for g in range(2):
                gsl = slice(64 * g, 64 * g + 64)
                by = self.nb()
                psy = self.bank(by)
                S.op("pe", lambda e, psy=psy, gsl=gsl: e.matmul(psy, lhsT=rs_.ap[gsl, 0:128], rhs=Ssb[d].ap[gsl, :], start=True, stop=True), reads=[rs_.buf, Ssb[d].buf], writes=[self.pb[by]])
                S.op("dve", lambda e, psy=psy, g=g: e.tensor_tensor(out=V3(ot_.ap[:, 1024 + g * 512:1536 + g * 512], 8), in0=V3(psy, 8),
                                                                  in1=rf[:, 518 + 8 * g:526 + 8 * g].unsqueeze(2).broadcast_to([128, 8, 64]), op=ALU.mult),
                     reads=[self.pb[by], rf_.buf], writes=[ot_.buf])
            for g in range(2):
                gsl = slice(64 * g, 64 * g + 64)
                bd = self.nb()
                psd = self.bank(bd)
                S.op("pe", lambda e, psd=psd, g=g: e.matmul(psd, lhsT=rs_.ap[:, 128:256], rhs=rb[:, 2560 + g * 512:3072 + g * 512], start=True, stop=True), reads=[rs_.buf, rb_.buf], writes=[self.pb[bd]])
                S.op("dve", lambda e, gsl=gsl, g=g: e.tensor_tensor(out=V3(Ss[d].ap[gsl, :], 8), in0=V3(Ss[d].ap[gsl, :], 8),
                                                                  in1=rf[gsl, 534 + 8 * g:542 + 8 * g].unsqueeze(2).broadcast_to([64, 8, 64]), op=ALU.mult),
                     reads=[Ss[d].buf, rf_.buf], writes=[Ss[d].buf])
                S.op("dve", lambda e, gsl=gsl, psd=psd: e.tensor_tensor(out=Ss[d].ap[gsl, :], in0=Ss[d].ap[gsl, :], in1=psd[gsl, :], op=ALU.add),
                     reads=[Ss[d].buf, self.pb[bd]], writes=[Ss[d].buf])
            S.op("act", lambda e: e.activation(out=Ssb[d].ap, in_=Ss[d].ap, func=AF.Copy), reads=[Ss[d].buf], writes=[Ssb[d].buf])
            S.dma("sp", self.OD[d][c * 128:(c + 1) * 128, :], ot_.ap, reads=[ot_.buf])

        c0 = 0
        for L in self.seq_lens:
            N = L // 128
            for d in range(2):
                for hp in range(2):
                    S.op("pool", lambda e, d=d, hp=hp: e.memset(Sg[d][hp].ap, 0.0), writes=[Sg[d][hp].buf])
                    S.op("pool", lambda e, d=d, hp=hp: e.memset(Sgb[d][hp].ap, 0.0), writes=[Sgb[d][hp].buf])
                for t_ in (Sd[d], Sdb[d], Ss[d], Ssb[d]):
                    S.op("pool", lambda e, t_=t_: e.memset(t_.ap, 0.0), writes=[t_.buf])
            order = []
            for n in range(N):
                order.append((c0 + n, 0))
                order.append((c0 + N - 1 - n, 1))
            cnts = [0, 0]
            slots = []
            for (c, d) in order:
                slots.append(cnts[d] % NBUF)
                cnts[d] += 1
            load(order[0][0], order[0][1], slots[0])
            if len(order) > 1:
                load(order[1][0], order[1][1], slots[1])
            for j, (c, d) in enumerate(order):
                if j + 2 < len(order):
                    load(order[j + 2][0], order[j + 2][1], slots[j + 2])
                step(c, d, slots[j])
            c0 += N

    def pass_C1(self, l, xsrc):
        S = self.S
        W = self.W
        self.reset()
        noscan = "noscan" in self.dbg
        wob = [self.tile(D, BF16, "wob%d" % c) for c in range(16)]
        for c in range(16):
            S.dma("pool", wob[c].ap, W["w_out"][l, c * 128:(c + 1) * 128, :], writes=[wob[c].buf])
        nwr = self.tile(DMIX, F32, "nwrep")
        for h in range(4):
            S.dma("sp", nwr.ap[:, h * 128:(h + 1) * 128], W["gla_norm_w"][l].partition_broadcast(128), writes=[nwr.buf])
            S.dma("sp", nwr.ap[:, 512 + h * 128:512 + (h + 1) * 128], W["gdn_norm_w"][l].partition_broadcast(128), writes=[nwr.buf])
        S.dma("sp", nwr.ap[:, 1024:2048], W["ssd_norm_w"][l].partition_broadcast(128), writes=[nwr.buf])
        NB = 2
        oi = [self.tile(DMIX, F32, "oi%d" % i) for i in range(NB)]
        of = [self.tile(DMIX, F32, "of%d" % i) for i in range(NB)]
        ob = [self.tile(DMIX, F32, "ob%d" % i) for i in range(NB)]
        gt = [self.tile(DMIX, BF16, "gt%d" % i) for i in range(NB)]
        xs = [self.tile(D, F32, "xs%d" % i) for i in range(NB)]
        sq = self.tile(DMIX, F32, "sqC")
        ss = self.tile(16, F32, "ssC")
        rs = self.tile(16, F32, "rsC")
        mix = self.tile(DMIX, BF16, "mix")
        mixT = [self.tile(DMIX, BF16, "mixT%d" % i) for i in range(2)]
        x1 = [self.tile(D, F32, "x1_%d" % i) for i in range(2)]
        n = self.NCH

        def load(c):
            r0 = c * 128
            i = c % NB
            if not noscan:
                S.dma("sp", oi[i].ap, self.OI[r0:r0 + 128, :], writes=[oi[i].buf])
                S.dma("sp", of[i].ap, self.OD[0][r0:r0 + 128, :], writes=[of[i].buf])
                S.dma("sp", ob[i].ap, self.OD[1][r0:r0 + 128, :], writes=[ob[i].buf])
            else:
                S.dma("sp", oi[i].ap[:, 0:1024], self.xsrc_rows(xsrc, r0, 128), writes=[oi[i].buf])
                S.dma("sp", oi[i].ap[:, 1024:2048], self.xsrc_rows(xsrc, r0, 128), writes=[oi[i].buf])
            S.dma("sp", gt[i].ap, self.Tg[r0:r0 + 128, :], writes=[gt[i].buf])
            S.dma("sp", xs[i].ap, self.xsrc_rows(xsrc, r0, 128), writes=[xs[i].buf])

        def comp(c):
            r0 = c * 128
            i = c % NB
            o = oi[i]
            if not noscan:
                S.op("dve", lambda e: e.tensor_tensor(out=o.ap, in0=o.ap, in1=of[i].ap, op=ALU.add), reads=[o.buf, of[i].buf], writes=[o.buf])
                S.op("pool", lambda e: e.tensor_tensor(out=o.ap, in0=o.ap, in1=ob[i].ap, op=ALU.add), reads=[o.buf, ob[i].buf], writes=[o.buf])
            S.op("dve", lambda e: e.tensor_tensor(out=o.ap[:, 1024:2048], in0=o.ap[:, 1024:2048], in1=gt[i].ap[:, 1024:2048], op=ALU.mult),
                 reads=[o.buf, gt[i].buf], writes=[o.buf])
            S.op("pool", lambda e: e.tensor_tensor(out=sq.ap, in0=o.ap, in1=o.ap, op=ALU.mult), reads=[o.buf], writes=[sq.buf])
            S.op("dve", lambda e: e.tensor_reduce(out=ss.ap[:, 0:8], in_=sq.ap[:, 0:1024].rearrange("p (h d) -> p h d", h=8), axis=AX.X, op=ALU.add),
                 reads=[sq.buf], writes=[ss.buf])
            S.op("dve", lambda e: e.tensor_reduce(out=ss.ap[:, 8:10], in_=sq.ap[:, 1024:2048].rearrange("p (h d) -> p h d", h=2), axis=AX.X, op=ALU.add),
                 reads=[sq.buf], writes=[ss.buf])
            S.op("act", lambda e: e.activation(out=rs.ap[:, 0:8], in_=ss.ap[:, 0:8], func=AF.Ln, bias=EPS, scale=1.0 / 128), reads=[ss.buf], writes=[rs.buf])
            S.op("act", lambda e: e.activation(out=rs.ap[:, 8:10], in_=ss.ap[:, 8:10], func=AF.Ln, bias=EPS, scale=1.0 / 512), reads=[ss.buf, rs.buf], writes=[rs.buf])
            S.op("act", lambda e: e.activation(out=rs.ap[:, 0:10], in_=rs.ap[:, 0:10], func=AF.Exp, scale=-0.5), reads=[rs.buf], writes=[rs.buf])
            S.op("dve", lambda e: e.tensor_tensor(out=o.ap[:, 0:1024].rearrange("p (h d) -> p h d", h=8), in0=o.ap[:, 0:1024].rearrange("p (h d) -> p h d", h=8),
                                                  in1=rs.ap[:, 0:8].unsqueeze(2).broadcast_to([128, 8, 128]), op=ALU.mult), reads=[o.buf, rs.buf], writes=[o.buf])
            S.op("dve", lambda e: e.tensor_tensor(out=o.ap[:, 1024:2048].rearrange("p (h d) -> p h d", h=2), in0=o.ap[:, 1024:2048].rearrange("p (h d) -> p h d", h=2),
                                                  in1=rs.ap[:, 8:10].unsqueeze(2).broadcast_to([128, 2, 512]), op=ALU.mult), reads=[o.buf, rs.buf], writes=[o.buf])
            S.op("pool", lambda e: e.tensor_tensor(out=o.ap, in0=o.ap, in1=nwr.ap, op=ALU.mult), reads=[o.buf, nwr.buf], writes=[o.buf])
            S.op("dve", lambda e: e.tensor_tensor(out=mix.ap[:, 0:1024], in0=o.ap[:, 0:1024], in1=gt[i].ap[:, 0:1024], op=ALU.mult),
                 reads=[o.buf, gt[i].buf], writes=[mix.buf])
            S.op("act", lambda e: e.activation(out=mix.ap[:, 1024:2048], in_=o.ap[:, 1024:2048], func=AF.Copy), reads=[o.buf], writes=[mix.buf])
            mT = mixT[c % 2]
            for half in range(2):
                b = half
                pbf = self.bank(b, BF16)

                def tr(e, half=half, pbf=pbf):
                    for j in range(8):
                        cc = half * 8 + j
                        ins = e.transpose(pbf[:, j * 128:(j + 1) * 128], mix.ap[:, cc * 128:(cc + 1) * 128], self.identb.ap)
                    return ins
                S.op("pe", tr, reads=[mix.buf, self.identb.buf], writes=[self.pb[b]])
                if half == 0:
                    S.op("act", lambda e, pbf=pbf: e.activation(out=mT.ap[:, 0:1024], in_=pbf, func=AF.Copy), reads=[self.pb[b]], writes=[mT.buf])
                else:
                    S.op("dve", lambda e, pbf=pbf: e.tensor_copy(out=mT.ap[:, 1024:2048], in_=pbf), reads=[self.pb[b]], writes=[mT.buf])
            xo = x1[c % 2]
            for half in range(2):
                b = 2 + (2 * c + half) % 4
                ps = self.bank(b)

                def mm(e, half=half, ps=ps):
                    for cc in range(16):
                        ins = e.matmul(ps, lhsT=mT.ap[:, cc * 128:(cc + 1) * 128], rhs=wob[cc].ap[:, half * 512:(half + 1) * 512], start=(cc == 0), stop=(cc == 15))
                    return ins
                S.op("pe", mm, reads=[mT.buf] + [w.buf for w in wob], writes=[self.pb[b]])
                S.op("dve", lambda e, half=half, ps=ps: e.tensor_tensor(out=xo.ap[:, half * 512:(half + 1) * 512], in0=ps, in1=xs[i].ap[:, half * 512:(half + 1) * 512], op=ALU.add),
                     reads=[self.pb[b], xs[i].buf], writes=[xo.buf])
            S.dma("sp", self.X1[r0:r0 + 128, :], xo.ap, reads=[xo.buf])

        load(0)
        for c in range(n):
            if c + 1 < n:
                load(c + 1)
            comp(c)

    def pass_C2(self, l):
        S = self.S
        W = self.W
        self.reset()
        T = 256
        ns = 2
        ntiles = self.NT // T
        last = (l == self.n_layers - 1)
        wub = [self.tile(DFF, BF16, "wub%d" % k) for k in range(8)]
        for k in range(8):
            S.dma("pool", wub[k].ap, W["w_up"][l, k * 128:(k + 1) * 128, :], writes=[wub[k].buf])
        wdb = [self.tile(D, BF16, "wdb%d" % f) for f in range(32)]
        for f in range(32):
            S.dma("pool", wdb[f].ap, W["w_down"][l, f * 128:(f + 1) * 128, :], writes=[wdb[f].buf])
        nw = self.tile(8, F32, "nwC")
        S.dma("sp", nw.ap, W["norm_mlp_w"][l].rearrange("(k p) -> p k", p=128), writes=[nw.buf], slow=True)
        if last:
            nfr = self.tile(D, F32, "nfr")
            S.dma("sp", nfr.ap, W["norm_f_w"].partition_broadcast(128), writes=[nfr.buf])
        xt = [self.tile(ns * D, F32, "xtC%d" % i) for i in range(2)]
        junk = self.tile(D, BF16, "junkC")
        ssq = self.tile(4, F32, "ssqC")
        rstd = self.tile(4, F32, "rstdC")
        xn = self.tile(ns * D, BF16, "xnC")
        hT = [self.tile(8 * T, BF16, "hTC%d" % i) for i in range(2)]
        aT = self.tile(32 * T, BF16, "aT")
        rl = [self.tile(T, F32, "rl%d" % i) for i in range(3)]
        xo = [self.tile(D, F32, "xoC0")] * 2
        ss2 = self.tile(4, F32, "ss2")
        rs2 = self.tile(4, F32, "rs2")
        cnt = {"g": 0, "o": 0}

        def prep(t):
            t0 = t * T
            x = xt[t % 2]
            S.dma("sp", x.v("p (s d) -> p s d", s=ns), self.X1[t0:t0 + T, :].rearrange("(s p) d -> p s d", p=128), writes=[x.buf])
            self.rms_T(x, ns, nw, hT[t % 2], junk, ssq, rstd, xn, banks=[0, 1])

        def main(t):
            t0 = t * T
            x = xt[t % 2]
            h = hT[t % 2]
            hv = h.v("p (k t) -> p k t", k=8)
            av = aT.v("p (f t) -> p f t", f=32)
            for f in range(32):
                b = 2 + cnt["g"] % 3
                r = rl[cnt["g"] % 3]
                cnt["g"] += 1
                ps = self.bank(b)

                def mm(e, f=f, ps=ps):
                    for k in range(8):
                        ins = e.matmul(ps[:, 0:T], lhsT=wub[k].ap[:, f * 128:(f + 1) * 128], rhs=hv[:, k, :], start=(k == 0), stop=(k == 7))
                    return ins
                S.op("pe", mm, reads=[h.buf] + [w.buf for w in wub], writes=[self.pb[b]])
                S.op("act", lambda e, ps=ps, r=r: e.activation(out=r.ap, in_=ps[:, 0:T], func=AF.Relu), reads=[self.pb[b]], writes=[r.buf])
                eng = "pool" if f % 2 == 0 else "dve"
                S.op(eng, lambda e, f=f, r=r: e.tensor_tensor(out=av[:, f, :], in0=r.ap, in1=r.ap, op=ALU.mult), reads=[r.buf], writes=[aT.buf])
            xv = x.v("p (s d) -> p s d", s=ns)
            for s in range(ns):
                o = xo[cnt["o"] % 2]
                cnt["o"] += 1
                for half in range(2):
                    b = 5 + cnt["g"] % 3
                    cnt["g"] += 1
                    ps = self.bank(b)

                    def mm(e, s=s, half=half, ps=ps):
                        for f in range(32):
                            ins = e.matmul(ps, lhsT=av[:, f, s * 128:(s + 1) * 128], rhs=wdb[f].ap[:, half * 512:(half + 1) * 512], start=(f == 0), stop=(f == 31))
                        return ins
                    S.op("pe", mm, reads=[aT.buf] + [w.buf for w in wdb], writes=[self.pb[b]])
                    S.op("dve", lambda e, s=s, half=half, ps=ps, o=o: e.tensor_tensor(out=o.ap[:, half * 512:(half + 1) * 512], in0=ps,
                                                                                     in1=xv[:, s, half * 512:(half + 1) * 512], op=ALU.add),
                         reads=[self.pb[b], x.buf], writes=[o.buf])
                r0 = t0 + s * 128
                if not last:
                    S.dma("sp", self.XR[r0:r0 + 128, :], o.ap, reads=[o.buf])
                else:
                    S.op("pool", lambda e: e.memset(ss2.ap[:, 0:1], 0.0), writes=[ss2.buf])
                    S.op("act", lambda e, o=o: e.activation(out=junk.ap, in_=o.ap, func=AF.Square, accum_out=ss2.ap[:, 0:1]), reads=[o.buf, ss2.buf], writes=[junk.buf, ss2.buf])
                    S.op("act", lambda e: e.activation(out=rs2.ap[:, 0:1], in_=ss2.ap[:, 0:1], func=AF.Ln, bias=EPS, scale=1.0 / D), reads=[ss2.buf], writes=[rs2.buf])
                    S.op("act", lambda e: e.activation(out=rs2.ap[:, 0:1], in_=rs2.ap[:, 0:1], func=AF.Exp, scale=-0.5), reads=[rs2.buf], writes=[rs2.buf])
                    S.op("dve", lambda e, o=o: e.scalar_tensor_tensor(out=o.ap, in0=o.ap, scalar=rs2.ap[:, 0:1], in1=nfr.ap, op0=ALU.mult, op1=ALU.mult),
                         reads=[o.buf, rs2.buf, nfr.buf], writes=[o.buf])
                    S.dma("sp", self.xrows(self.yout, r0, 128), o.ap, reads=[o.buf])

        prep(0)
        for t in range(ntiles):
            if t + 1 < ntiles:
                prep(t + 1)
            main(t)


_CACHE = {}


def kernel(**inputs):
    n = 8
    key = "full"
    if key not in _CACHE:
        _CACHE[key] = KB().build()
    nc = _CACHE[key]
    in_maps = []
    for i in range(n):
        m = {"x0": np.ascontiguousarray(inputs["x_prompt"][i], dtype=np.float32),
             "x1": np.ascontiguousarray(inputs["x_sample"][i], dtype=np.float32)}
        for k in WSHAPES:
            m[k] = np.ascontiguousarray(inputs[k], dtype=np.float32)
        in_maps.append(m)
    res = run_bass_kernel_spmd(nc, in_maps, core_ids=list(range(n)))
    yp = np.stack([res.results[i]["y0"] for i in range(n)], axis=0)
    ys = np.stack([res.results[i]["y1"] for i in range(n)], axis=0)
    return (yp, ys)
```

```python
import numpy as np
from contextlib import ExitStack
import concourse.bass as bass
import concourse.mybir as mybir
from concourse.bass_utils import run_bass_kernel_spmd

F32 = mybir.dt.float32
BF16 = mybir.dt.bfloat16
AF = mybir.ActivationFunctionType
ALU = mybir.AluOpType
AX = mybir.AxisListType

ENGS = ("pe", "act", "dve", "pool", "sp")
SAME_ENGINE_SYNC = True
EPS = 1e-6
D = 1024
DIN = 5968
DMIX = 2048
DFF = 4096

WSHAPES = {
    "norm_mix_w": [2, 1024], "w_in": [2, 1024, 5968], "gla_gk_up": [2, 2, 16, 256], "gla_gk_bias": [2, 2, 256],
    "gla_norm_w": [2, 128], "gdn_conv_w": [2, 5, 1536], "gdn_A_log": [2, 2, 4], "gdn_dt_bias": [2, 2, 4],
    "gdn_norm_w": [2, 128], "ssd_conv_w": [2, 5, 1280], "ssd_conv_b": [2, 1280], "ssd_A_log": [2, 2, 16],
    "ssd_dt_bias": [2, 2, 16], "ssd_D": [2, 16], "ssd_norm_w": [2, 1024], "w_out": [2, 2048, 1024],
    "norm_mlp_w": [2, 1024], "w_up": [2, 1024, 4096], "w_down": [2, 4096, 1024], "norm_f_w": [1024],
}


class StopBuild(Exception):
    pass


class Buf:
    __slots__ = ("name", "lw", "rd", "excl")

    def __init__(self, name=None, excl=False):
        self.name = name
        self.lw = None
        self.rd = {}
        self.excl = excl


class Sched:
    def __init__(self, nc, n_dma_ch=32):
        self.nc = nc
        self.stream = {e: [] for e in ENGS}
        self.cnt = {e: 0 for e in ENGS}
        self.seen = {e: {} for e in ENGS}
        self.n_dma_ch = n_dma_ch
        self.ch_cnt = [0] * n_dma_ch
        self.ch_next = 0
        self.ch_next_sw = 0

    def _deps(self, eng, reads, writes):
        waits = {}

        def need(tok):
            if tok is None:
                return
            k, v = tok
            if k == eng and not SAME_ENGINE_SYNC:
                return
            if waits.get(k, 0) < v:
                waits[k] = v

        for b in reads:
            need(b.lw)
            if b.excl:
                for k, v in b.rd.items():
                    if k != eng:
                        need((k, v))
        for b in writes:
            need(b.lw)
            for k, v in b.rd.items():
                need((k, v))
        w = []
        seen = self.seen[eng]
        for k, v in waits.items():
            if seen.get(k, 0) < v:
                seen[k] = v
                w.append((k, v))
        return w

    def _tick(self):
        self.nops = getattr(self, "nops", 0) + 1
        lim = getattr(self, "limit", None)
        if lim is not None and self.nops > lim:
            raise StopBuild()

    def op(self, eng, fn, reads=(), writes=()):
        self._tick()
        w = self._deps(eng, reads, writes)
        self.cnt[eng] += 1
        c = self.cnt[eng]
        for b in reads:
            b.rd[eng] = c
        for b in writes:
            b.lw = (eng, c)
            b.rd = {}
        self.stream[eng].append((w, fn, (eng, 1)))

    def dma(self, q, out_ap, in_ap, reads=(), writes=(), slow=False):
        self._tick()
        if q == "sp":
            ch = self.ch_next
            self.ch_next = (self.ch_next + 1) % (self.n_dma_ch - 8)
        else:
            ch = self.n_dma_ch - 8 + self.ch_next_sw
            self.ch_next_sw = (self.ch_next_sw + 1) % 8
        key = "d%d" % ch
        w = self._deps(q, reads, writes)
        prev = 16 * self.ch_cnt[ch]
        if prev and self.seen[q].get(key, 0) < prev:
            self.seen[q][key] = prev
            w.append((key, prev))
        self.ch_cnt[ch] += 1
        v = 16 * self.ch_cnt[ch]
        for b in reads:
            b.rd[key] = v
        for b in writes:
            b.lw = (key, v)
            b.rd = {}

        def fn(e, out_ap=out_ap, in_ap=in_ap, slow=slow):
            if slow:
                return e.dma_start(out=out_ap, in_=in_ap, allow_slow_non_contiguous=True)
            return e.dma_start(out=out_ap, in_=in_ap)

        self.stream[q].append((w, fn, (key, 16)))

    def barrier(self):
        snap = [(e, self.cnt[e]) for e in ENGS if self.cnt[e]]
        snap += [("d%d" % c, 16 * self.ch_cnt[c]) for c in range(self.n_dma_ch) if self.ch_cnt[c]]
        for e in ENGS:
            w = []
            for k, v in snap:
                if v > self.seen[e].get(k, 0):
                    self.seen[e][k] = v
                    w.append((k, v))
            if w:
                self.stream[e].append((w, None, None))

    def emit(self):
        nc = self.nc
        with ExitStack() as st:
            sems = {}
            for e in ENGS:
                sems[e] = st.enter_context(nc.semaphore("s_" + e))
            for c in range(self.n_dma_ch):
                sems["d%d" % c] = st.enter_context(nc.semaphore("s_d%d" % c))
            fin = [("d%d" % c, 16 * self.ch_cnt[c]) for c in range(self.n_dma_ch) if self.ch_cnt[c]]
            block = st.enter_context(nc.Block())

            def run(e, h):
                for w, fn, inc in self.stream[e]:
                    for k, v in w:
                        h.wait_ge(sems[k], v)
                    if fn is not None:
                        fn(h).then_inc(sems[inc[0]], inc[1])
                if e == "sp":
                    for k, v in fin:
                        h.wait_ge(sems[k], v)

            @block.tensor
            def _(h):
                run("pe", h)

            @block.scalar
            def _(h):
                run("act", h)

            @block.vector
            def _(h):
                run("dve", h)

            @block.gpsimd
            def _(h):
                run("pool", h)

            @block.sync
            def _(h):
                run("sp", h)


class Tile:
    __slots__ = ("ap", "buf")

    def __init__(self, ap, name=None):
        self.ap = ap
        self.buf = Buf(name)

    def v(self, pat, **kw):
        return self.ap.rearrange(pat, **kw)


class KB:
    SB_WORDS = 53000

    def __init__(self, seq_lens=(4096, 8192), n_layers=2, dbg=()):
        self.seq_lens = tuple(seq_lens)
        self.NT = sum(seq_lens)
        self.NCH = self.NT // 128
        self.n_layers = n_layers
        self.dbg = set(dbg)
        nc = self.nc = bass.Bass("TRN2", target_bir_lowering=False)
        self.S = Sched(nc)
        for t_ in self.dbg:
            if t_.startswith("lim="):
                self.S.limit = int(t_[4:])
        self.xin = [nc.dram_tensor("x%d" % i, [L, D], F32, kind="ExternalInput").ap() for i, L in enumerate(self.seq_lens)]
        self.W = {k: nc.dram_tensor(k, s, F32, kind="ExternalInput").ap() for k, s in WSHAPES.items()}
        self.yout = [nc.dram_tensor("y%d" % i, [L, D], F32, kind="ExternalOutput").ap() for i, L in enumerate(self.seq_lens)]
        NT = self.NT

        def scr(name, shape, dt):
            kind = "ExternalOutput" if name in self.dbg else "Internal"
            return nc.dram_tensor(name, shape, dt, kind=kind).ap()

        self.Fq = scr("Fq", [256, NT], BF16)
        self.Fk = scr("Fk", [256, NT], BF16)
        self.Fconv = scr("Fconv", [2816, NT], BF16)
        self.Fgate = scr("Fgate", [80, NT], F32)
        self.Tv = scr("Tv", [NT, 512], BF16)
        self.Tg = scr("Tg", [NT, 2048], BF16)
        self.OI = scr("OI", [NT, 2048], F32)
        self.OD = [scr("OF", [NT, 2048], F32), scr("OB", [NT, 2048], F32)]
        self.RB = [scr("RB%d" % d, [self.NCH, 128, self.XB], BF16) for d in range(2)]
        self.RF = [scr("RF%d" % d, [self.NCH, 128, self.XF], F32) for d in range(2)]
        self.RS = scr("RS", [self.NCH, 128, 256], BF16)
        self.X1 = scr("X1", [NT, D], F32)
        self.XR = scr("XR", [NT, D], F32)

    def seq_of(self, t):
        s0 = 0
        for i, L in enumerate(self.seq_lens):
            if t < s0 + L:
                return i, s0, L
            s0 += L
        raise ValueError

    def xrows(self, lst, t0, n):
        i, s0, L = self.seq_of(t0)
        assert t0 + n <= s0 + L
        return lst[i][t0 - s0:t0 - s0 + n, :]

    def reset(self):
        self.sb_off = self.sb_base

    def tile(self, cols, dt=F32, name=None):
        nbytes = cols * (2 if dt == BF16 else 4)
        nbytes = (nbytes + 63) // 64 * 64
        off = self.sb_off
        self.sb_off += nbytes
        assert self.sb_off <= self.SB_WORDS * 4, "SBUF overflow %d" % self.sb_off
        ap = self.big[:, off // 4:(off + nbytes) // 4]
        if dt == BF16:
            ap = ap.bitcast(BF16)
        return Tile(ap[:, 0:cols], name)

    def bank(self, b, dt=F32):
        ap = self.pp[:, b * 512:(b + 1) * 512]
        if dt == BF16:
            ap = ap.bitcast(BF16)
        return ap

    def build(self):
        nc = self.nc
        S = self.S
        with ExitStack() as st:
            self.big = st.enter_context(nc.sbuf_tensor("big", [128, self.SB_WORDS], F32))
            self.pp = st.enter_context(nc.psum_tensor("pp", [128, 4096], F32))
            self.pb = [Buf("bank%d" % b, excl=True) for b in range(8)]
            self.sb_off = 0
            self.consts()
            S.barrier()
            self.sb_base = self.sb_off
            try:
                for l in range(self.n_layers):
                    xsrc = self.xin if l == 0 else [self.XR]
                    self.layer = l
                    self.pass_A(l, xsrc)
                    S.barrier()
                    self.stop("stopA")
                    if "noscan" not in self.dbg:
                        self.pass_P(l)
                        S.barrier()
                        self.stop("stopP")
                        self.pass_B(l)
                        S.barrier()
                        self.stop("stopB")
                    self.pass_C1(l, xsrc)
                    S.barrier()
                    self.pass_C2(l)
                    S.barrier()
            except StopBuild:
                S.barrier()
            S.emit()
        return nc

    def stop(self, tag):
        if tag in self.dbg:
            raise StopBuild()

    def xsrc_rows(self, xsrc, t0, n):
        if len(xsrc) == 1:
            return xsrc[0][t0:t0 + n, :]
        return self.xrows(xsrc, t0, n)

    def consts(self):
        S = self.S
        idf = self.identf = self.tile(128, F32, "identf")
        idb = self.identb = self.tile(128, BF16, "identb")
        S.op("pool", lambda e: e.memset(idf.ap, 1.0), writes=[idf.buf])
        S.op("pool", lambda e: e.affine_select(out=idf.ap, in_=idf.ap, pattern=[[-1, 128]], compare_op=ALU.is_equal,
                                               fill=self.freg(e, 0.0), base=0, channel_multiplier=1), reads=[idf.buf], writes=[idf.buf])
        S.op("dve", lambda e: e.tensor_copy(out=idb.ap, in_=idf.ap), reads=[idf.buf], writes=[idb.buf])
        mL = self.maskL = self.tile(512, F32, "maskL")
        mU = self.maskU = self.tile(512, F32, "maskU")
        for m, pat, cm in ((mL, [[0, 4], [1, 128]], -1), (mU, [[0, 4], [-1, 128]], 1)):
            S.op("pool", lambda e, m=m: e.memset(m.ap, 1.0), writes=[m.buf])
            S.op("pool", lambda e, m=m, pat=pat, cm=cm: e.affine_select(out=m.v("p (h i) -> p h i", h=4), in_=m.v("p (h i) -> p h i", h=4), pattern=pat,
                                                                       compare_op=ALU.is_ge, fill=self.freg(e, 0.0), base=0, channel_multiplier=cm),
                 reads=[m.buf], writes=[m.buf])
        ones = self.onesb = self.tile(128, BF16, "onesb")
        S.op("pool", lambda e: e.memset(ones.ap, 1.0), writes=[ones.buf])
        rm = self.rmask = self.tile(512, F32, "rmask")
        S.op("pool", lambda e: e.memset(rm.ap, 1.0), writes=[rm.buf])
        for c in range(4):
            S.op("pool", lambda e, c=c: e.memset(rm.ap[:, c * 128:c * 128 + 1], 0.0), reads=[rm.buf], writes=[rm.buf])
        self.bmask = [self.tile(512, BF16, "bmask%d" % i) for i in range(4)]
        sel = self.sel = self.tile(20 * 128, F32, "sel")
        keep = self.sb_off
        dts = []
        for bi, bsz in enumerate((16, 32, 64)):
            nbk = 128 // bsz
            et = self.tile(128, F32, "Eb%d" % bi)
            dtile = self.tile(128, F32, "Db%d" % bi)
            S.op("pool", lambda e, et=et, nbk=nbk: e.memset(et.ap[0:nbk, :], 1.0), writes=[et.buf])
            S.op("pool", lambda e, et=et, nbk=nbk, bsz=bsz: e.affine_select(out=et.ap[0:nbk, :], in_=et.ap[0:nbk, :], pattern=[[1, 128]], compare_op=ALU.is_ge,
                                                                        fill=self.freg(e, 0.0), base=0, channel_multiplier=-bsz), reads=[et.buf], writes=[et.buf])
            S.op("pool", lambda e, et=et, nbk=nbk, bsz=bsz: e.affine_select(out=et.ap[0:nbk, :], in_=et.ap[0:nbk, :], pattern=[[-1, 128]], compare_op=ALU.is_ge,
                                                                        fill=self.freg(e, 0.0), base=bsz - 1, channel_multiplier=bsz), reads=[et.buf], writes=[et.buf])
            ps = self.bank(bi)
            S.op("pe", lambda e, et=et, nbk=nbk, ps=ps: e.matmul(ps[:, 0:128], lhsT=et.ap[0:nbk, :], rhs=et.ap[0:nbk, :], start=True, stop=True), reads=[et.buf], writes=[self.pb[bi]])
            S.op("dve", lambda e, dtile=dtile, ps=ps: e.tensor_copy(out=dtile.ap, in_=ps[:, 0:128]), reads=[self.pb[bi]], writes=[dtile.buf])
            dts.append(dtile)
        rep = lambda ap: ap.unsqueeze(1).broadcast_to([128, 4, 128])
        bm = self.bmask
        S.op("dve", lambda e: e.tensor_copy(out=bm[0].v("p (h i) -> p h i", h=4), in_=rep(dts[0].ap)), reads=[dts[0].buf], writes=[bm[0].buf])
        S.op("dve", lambda e: e.tensor_tensor(out=bm[1].v("p (h i) -> p h i", h=4), in0=rep(dts[1].ap), in1=rep(dts[0].ap), op=ALU.subtract), reads=[dts[0].buf, dts[1].buf], writes=[bm[1].buf])
        S.op("dve", lambda e: e.tensor_tensor(out=bm[2].v("p (h i) -> p h i", h=4), in0=rep(dts[2].ap), in1=rep(dts[1].ap), op=ALU.subtract), reads=[dts[1].buf, dts[2].buf], writes=[bm[2].buf])
        S.op("dve", lambda e: e.tensor_scalar(out=bm[3].v("p (h i) -> p h i", h=4), in0=rep(dts[2].ap), scalar1=-1.0, scalar2=1.0, op0=ALU.mult, op1=ALU.add), reads=[dts[2].buf], writes=[bm[3].buf])
        sel2 = self.tile(20 * 128, F32, "sel2")
        for t_, base in ((sel, 0), (sel2, -32)):
            S.op("pool", lambda e, t_=t_: e.memset(t_.ap[0:52, :], 1.0), writes=[t_.buf])
            S.op("pool", lambda e, t_=t_, base=base: e.affine_select(out=t_.ap[0:52, :].rearrange("p (q m) -> p q m", q=20),
                                                                    in_=t_.ap[0:52, :].rearrange("p (q m) -> p q m", q=20), pattern=[[-1, 20], [0, 128]],
                                                                    compare_op=ALU.is_equal, fill=self.freg(e, 0.0), base=base, channel_multiplier=1),
                 reads=[t_.buf], writes=[t_.buf])
        S.op("pool", lambda e: e.tensor_tensor(out=sel.ap[0:52, :], in0=sel.ap[0:52, :], in1=sel2.ap[0:52, :], op=ALU.add), reads=[sel.buf, sel2.buf], writes=[sel.buf])
        self.sb_off = keep

    def rms_T(self, xt, ns, nw, hT, junk, ssq, rstd, xn, banks):
        S = self.S
        T = ns * 128
        xv = xt.v("p (s d) -> p s d", s=ns)
        xnv = xn.v("p (s d) -> p s d", s=ns)
        S.op("pool", lambda e: e.memset(ssq.ap[:, 0:ns], 0.0), writes=[ssq.buf])
        for s in range(ns):
            S.op("act", lambda e, s=s: e.activation(out=junk.ap, in_=xv[:, s, :], func=AF.Square, accum_out=ssq.ap[:, s:s + 1]),
                 reads=[xt.buf, ssq.buf], writes=[junk.buf, ssq.buf])
        S.op("act", lambda e: e.activation(out=rstd.ap[:, 0:ns], in_=ssq.ap[:, 0:ns], func=AF.Ln, bias=EPS, scale=1.0 / D),
             reads=[ssq.buf], writes=[rstd.buf])
        S.op("act", lambda e: e.activation(out=rstd.ap[:, 0:ns], in_=rstd.ap[:, 0:ns], func=AF.Exp, scale=-0.5),
             reads=[rstd.buf], writes=[rstd.buf])
        for s in range(ns):
            eng = "dve" if s % 2 == 0 else "pool"
            S.op(eng, lambda e, s=s: e.tensor_scalar(out=xnv[:, s, :], in0=xv[:, s, :], scalar1=rstd.ap[:, s:s + 1], scalar2=None,
                                                      op0=ALU.mult), reads=[xt.buf, rstd.buf], writes=[xn.buf])
        hv = hT.v("p (k t) -> p k t", k=8)
        for kk in range(4):
            b = banks[kk % len(banks)]
            pbf = self.bank(b, BF16)

            def tr(e, kk=kk, pbf=pbf):
                for j in range(2):
                    k = 2 * kk + j
                    for s in range(ns):
                        ins = e.transpose(pbf[:, j * T + s * 128:j * T + (s + 1) * 128], xnv[:, s, k * 128:(k + 1) * 128], self.identb.ap)
                return ins
            S.op("pe", tr, reads=[xn.buf, self.identb.buf], writes=[self.pb[b]])
            for j in range(2):
                k = 2 * kk + j
                if j == 0:
                    S.op("act", lambda e, k=k, j=j, pbf=pbf: e.activation(out=hv[:, k, :], in_=pbf[:, j * T:(j + 1) * T], func=AF.Copy,
                                                                       scale=nw.ap[:, k:k + 1]),
                         reads=[self.pb[b], nw.buf], writes=[hT.buf])
                else:
                    S.op("dve", lambda e, k=k, j=j, pbf=pbf: e.tensor_scalar(out=hv[:, k, :], in0=pbf[:, j * T:(j + 1) * T],
                                                                          scalar1=nw.ap[:, k:k + 1], scalar2=None, op0=ALU.mult),
                         reads=[self.pb[b], nw.buf], writes=[hT.buf])

    def pass_A(self, l, xsrc):
        S = self.S
        W = self.W
        self.reset()
        T = 512
        ntiles = self.NT // T
        wib = [self.tile(DIN, BF16, "wib%d" % k) for k in range(8)]
        for k in range(8):
            S.dma("pool", wib[k].ap, W["w_in"][l, k * 128:(k + 1) * 128, :], writes=[wib[k].buf])
        nw = self.tile(8, F32, "nwA")
        S.dma("sp", nw.ap, W["norm_mix_w"][l].rearrange("(k p) -> p k", p=128), writes=[nw.buf], slow=True)
        xt = [self.tile(4 * D, F32, "xtA%d" % i) for i in range(2)]
        junk = self.tile(D, BF16, "junkA")
        ssq = self.tile(4, F32, "ssqA")
        rstd = self.tile(4, F32, "rstdA")
        xn = self.tile(4 * D, BF16, "xnA")
        hT = [self.tile(8 * T, BF16, "hTA%d" % i) for i in range(2)]
        stg = [self.tile(512, F32, "stgA%d" % i) for i in range(4)]
        tstg = [self.tile(2560, BF16, "tstgA%d" % i) for i in range(2)]
        groups = []
        for i in range(2):
            groups.append((128 * i, 128, self.Fq, 128 * i, False))
        for i in range(2):
            groups.append((256 + 128 * i, 128, self.Fk, 128 * i, False))
        for i in range(12):
            groups.append((1568 + 128 * i, 128, self.Fconv, 128 * i, False))
        for i in range(10):
            groups.append((4656 + 128 * i, 128, self.Fconv, 1536 + 128 * i, False))
        groups += [(1536, 16, self.Fgate, 0, True), (1552, 16, self.Fgate, 16, True), (5936, 16, self.Fgate, 32, True),
                   (5952, 16, self.Fgate, 48, True), (3624, 8, self.Fgate, 64, True), (3616, 8, self.Fgate, 72, True)]
        tbanks = [(512, False), (1024, True), (3104, True), (3632, True), (4144, True)]
        cnt = {"g": 0, "t": 0}

        def prep(t):
            t0 = t * T
            x = xt[t % 2]
            S.dma("sp", x.v("p (s d) -> p s d", s=4), self.xsrc_rows(xsrc, t0, T).rearrange("(s p) d -> p s d", p=128), writes=[x.buf])
            self.rms_T(x, 4, nw, hT[t % 2], junk, ssq, rstd, xn, banks=[0, 1])

        def main(t):
            t0 = t * T
            h = hT[t % 2]
            hv = h.v("p (k t) -> p k t", k=8)
            for (off, M, dst, row, isf) in groups:
                b = 2 + cnt["g"] % 3
                sg = stg[cnt["g"] % 4]
                cnt["g"] += 1
                ps = self.bank(b)

                def mm(e, off=off, M=M, ps=ps):
                    for k in range(8):
                        ins = e.matmul(ps[0:M, :], lhsT=wib[k].ap[:, off:off + M], rhs=hv[:, k, :], start=(k == 0), stop=(k == 7))
                    return ins
                S.op("pe", mm, reads=[h.buf] + [w.buf for w in wib], writes=[self.pb[b]])
                so = sg.ap if isf else sg.ap.bitcast(BF16)[:, 0:512]
                if cnt["g"] % 2 == 0:
                    S.op("act", lambda e, so=so, ps=ps, M=M: e.activation(out=so[0:M, :], in_=ps[0:M, :], func=AF.Copy),
                         reads=[self.pb[b]], writes=[sg.buf])
                else:
                    S.op("dve", lambda e, so=so, ps=ps, M=M: e.tensor_copy(out=so[0:M, :], in_=ps[0:M, :]),
                         reads=[self.pb[b]], writes=[sg.buf])
                S.dma("sp", dst[row:row + M, t0:t0 + T], so[0:M, :], reads=[sg.buf])
            for s in range(4):
                ts = tstg[cnt["t"] % 2]
                cnt["t"] += 1
                for bi, (off, silu) in enumerate(tbanks):
                    b = 5 + (cnt["g"] % 3)
                    cnt["g"] += 1
                    ps = self.bank(b)

                    def mm(e, off=off, s=s, ps=ps):
                        for k in range(8):
                            ins = e.matmul(ps, lhsT=hv[:, k, s * 128:(s + 1) * 128], rhs=wib[k].ap[:, off:off + 512], start=(k == 0),
                                           stop=(k == 7))
                        return ins
                    S.op("pe", mm, reads=[h.buf] + [w.buf for w in wib], writes=[self.pb[b]])
                    o = ts.ap[:, bi * 512:(bi + 1) * 512]
                    if silu:
                        S.op("act", lambda e, o=o, ps=ps: e.activation(out=o, in_=ps, func=AF.Silu), reads=[self.pb[b]], writes=[ts.buf])
                    else:
                        S.op("dve", lambda e, o=o, ps=ps: e.tensor_copy(out=o, in_=ps), reads=[self.pb[b]], writes=[ts.buf])
                r0 = t0 + s * 128
                S.dma("sp", self.Tv[r0:r0 + 128, :], ts.ap[:, 0:512], reads=[ts.buf])
                S.dma("sp", self.Tg[r0:r0 + 128, :], ts.ap[:, 512:2560], reads=[ts.buf])

        prep(0)
        for t in range(ntiles):
            if t + 1 < ntiles:
                prep(t + 1)
            main(t)

    XB = 3584
    XF = 552

    def arena_mark(self):
        return self.sb_off

    def arena_reset(self, mark):
        self.S.barrier()
        self.sb_off = mark

    def freg(self, e, val):
        c = self.__dict__.setdefault("_fregs", {})
        if val not in c:
            c[val] = e.to_reg(val)
        return c[val]

    def nb(self):
        b = self._bank_rr
        self._bank_rr = (b + 1) % 8
        return b

    def pass_P(self, l):
        S = self.S
        W = self.W
        self.reset()
        self._bank_rr = 0
        T = 512
        ntiles = self.NT // T
        idf = self.identf
        idb = self.identb
        V3 = lambda ap, h: ap.rearrange("p (h i) -> p h i", h=h)

        cwT = self.tile(22 * 5 + 10, F32, "cwT")
        dg = self.tile(110 * 128, BF16, "dg")
        cols = self.tile(8, F32, "gcols")
        upw = self.tile(256, F32, "upw")
        negb = self.tile(4, F32, "negb")
        did = self.tile(16 * 128, F32, "dident")
        QT = self.tile(1024, BF16, "QT")
        KT = self.tile(1024, BF16, "KT")
        RAWC = self.tile(22 * 516, BF16, "RAWC")
        LR = self.tile(512, F32, "LR")
        R1 = self.tile(512, F32, "R1")
        R2 = self.tile(512, F32, "R2")
        VT = self.tile(4 * 512, BF16, "VT")
        S.op("pool", lambda e: e.memset(R2.ap[0:64, :], 0.0), writes=[R2.buf])
        S.op("pool", lambda e: e.memset(R1.ap[0:64, :], 0.0), writes=[R1.buf])
        SA = self.tile(512, F32, "slabA")
        SQ6 = self.tile(512, F32, "slabQ6")
        TM = [self.tile(384, F32, "TM%d" % c) for c in range(4)]
        OG = [self.tile(512, F32, "OG%d" % c) for c in range(2)]
        OS = [self.tile(1024, F32, "OS%d" % c) for c in range(2)]
        ZZ = self.tile(512, F32, "ZZ")
        S.op("pool", lambda e: e.memset(ZZ.ap, 0.0), writes=[ZZ.buf])
        mark = self.arena_mark()
        cwr = self.tile(2816, F32, "cwr")
        S.dma("sp", cwr.ap[0:5, 0:1536], W["gdn_conv_w"][l], writes=[cwr.buf])
        S.dma("sp", cwr.ap[0:5, 1536:2816], W["ssd_conv_w"][l], writes=[cwr.buf])
        cbr = self.tile(1280, F32, "cbr")
        S.dma("sp", cbr.ap[0:1, :], W["ssd_conv_b"][l].rearrange("(o c) -> o c", o=1), writes=[cbr.buf])
        b = self.nb()
        ps = self.bank(b)

        def mmcw(e):
            for cc in range(22):
                ins = e.matmul(ps[:, cc * 5:cc * 5 + 5], lhsT=cwr.ap[0:5, cc * 128:(cc + 1) * 128], rhs=idf.ap[0:5, 0:5], start=True, stop=True)
            for cc in range(10):
                ins = e.matmul(ps[:, 110 + cc:111 + cc], lhsT=cbr.ap[0:1, cc * 128:(cc + 1) * 128], rhs=idf.ap[0:1, 0:1], start=True, stop=True)
            return ins
        S.op("pe", mmcw, reads=[cwr.buf, cbr.buf, idf.buf], writes=[self.pb[b]])
        S.op("dve", lambda e: e.tensor_copy(out=cwT.ap, in_=ps[:, 0:120]), reads=[self.pb[b]], writes=[cwT.buf])
        for i in range(110):
            eng = "dve" if i % 2 == 0 else "pool"
            S.op(eng, lambda e, i=i: e.tensor_scalar(out=dg.ap[:, i * 128:(i + 1) * 128], in0=idf.ap, scalar1=cwT.ap[:, i:i + 1], scalar2=None, op0=ALU.mult),
                 reads=[idf.buf, cwT.buf], writes=[dg.buf])
        S.op("pool", lambda e: e.memset(cols.ap[0:64, :], 0.0), writes=[cols.buf])
        for d in range(2):
            rb = 32 * d
            S.dma("sp", cols.ap[rb:rb + 16, 0:1], W["ssd_dt_bias"][l, d].rearrange("(h o) -> h o", o=1), reads=[cols.buf], writes=[cols.buf], slow=True)
            S.dma("sp", cols.ap[rb + 16:rb + 20, 0:1], W["gdn_dt_bias"][l, d].rearrange("(h o) -> h o", o=1), reads=[cols.buf], writes=[cols.buf], slow=True)
            S.dma("sp", cols.ap[rb:rb + 16, 1:2], W["ssd_A_log"][l, d].rearrange("(h o) -> h o", o=1), reads=[cols.buf], writes=[cols.buf], slow=True)
            S.dma("sp", cols.ap[rb + 16:rb + 20, 1:2], W["gdn_A_log"][l, d].rearrange("(h o) -> h o", o=1), reads=[cols.buf], writes=[cols.buf], slow=True)
        S.op("act", lambda e: e.activation(out=cols.ap[0:52, 1:2], in_=cols.ap[0:52, 1:2], func=AF.Exp), reads=[cols.buf], writes=[cols.buf])
        S.op("dve", lambda e: e.tensor_scalar(out=cols.ap[0:52, 1:2], in0=cols.ap[0:52, 1:2], scalar1=-1.0, scalar2=None, op0=ALU.mult), reads=[cols.buf], writes=[cols.buf])
        for d in range(2):
            S.op("pool", lambda e, d=d: e.memset(cols.ap[32 * d:32 * d + 16, 2:3], 1.0), reads=[cols.buf], writes=[cols.buf])
        S.op("dve", lambda e: e.tensor_scalar(out=cols.ap[0:52, 3:4], in0=cols.ap[0:52, 2:3], scalar1=-1.0, scalar2=None, op0=ALU.add), reads=[cols.buf], writes=[cols.buf])
        for d in range(2):
            S.dma("sp", upw.ap[32 * d:32 * d + 16, :], W["gla_gk_up"][l, d], writes=[upw.buf])
            S.dma("sp", negb.ap[:, 2 * d:2 * d + 2], W["gla_gk_bias"][l, d].rearrange("(c p) -> p c", p=128), writes=[negb.buf], slow=True)
        S.op("dve", lambda e: e.tensor_scalar(out=negb.ap, in0=negb.ap, scalar1=-1.0, scalar2=None, op0=ALU.mult), reads=[negb.buf], writes=[negb.buf])
        drep = self.tile(16, F32, "drep")
        S.dma("sp", drep.ap, W["ssd_D"][l].partition_broadcast(128), writes=[drep.buf])
        S.op("dve", lambda e: e.tensor_tensor(out=V3(did.ap, 16), in0=idf.ap.unsqueeze(1).broadcast_to([128, 16, 128]),
                                              in1=drep.ap.unsqueeze(2).broadcast_to([128, 16, 128]), op=ALU.mult), reads=[idf.buf, drep.buf], writes=[did.buf])
        self.arena_reset(mark)
        self.stop("stopP0")
        rawv = RAWC.v("p (c t) -> p c t", c=22)

        for t in range(ntiles):
            t0 = t * T
            si, s0, L = self.seq_of(t0)
            S.dma("sp", QT.v("p (c t) -> p c t", c=2), self.Fq[:, t0:t0 + T].rearrange("(c p) t -> p c t", p=128), writes=[QT.buf])
            S.dma("sp", KT.v("p (c t) -> p c t", c=2), self.Fk[:, t0:t0 + T].rearrange("(c p) t -> p c t", p=128), writes=[KT.buf])
            lo = max(t0 - 2, s0)
            hi = min(t0 + T + 2, s0 + L)
            if lo > t0 - 2:
                S.op("pool", lambda e: e.memset(rawv[:, :, 0:2], 0.0), writes=[RAWC.buf])
            if hi < t0 + T + 2:
                S.op("pool", lambda e: e.memset(rawv[:, :, 514:516], 0.0), writes=[RAWC.buf])
            S.dma("sp", rawv[:, :, lo - (t0 - 2):hi - (t0 - 2)], self.Fconv[:, lo:hi].rearrange("(c p) t -> p c t", p=128), reads=[RAWC.buf], writes=[RAWC.buf])
            for d in range(2):
                rb = 32 * d
                S.dma("sp", LR.ap[rb:rb + 16, :], self.Fgate[16 * d:16 * d + 16, t0:t0 + T], writes=[LR.buf])
                S.dma("sp", R1.ap[rb:rb + 16, :], self.Fgate[32 + 16 * d:48 + 16 * d, t0:t0 + T], reads=[R1.buf], writes=[R1.buf])
                S.dma("sp", R1.ap[rb + 16:rb + 20, :], self.Fgate[64 + 4 * d:68 + 4 * d, t0:t0 + T], reads=[R1.buf], writes=[R1.buf])
                S.dma("sp", R2.ap[rb + 16:rb + 20, :], self.Fgate[72 + 4 * d:76 + 4 * d, t0:t0 + T], reads=[R2.buf], writes=[R2.buf])
            S.dma("sp", VT.v("p (s c) -> p s c", s=4), self.Tv[t0:t0 + T, :].rearrange("(s p) c -> p s c", p=128), writes=[VT.buf])

            P52 = slice(0, 52)
            sl = [self.tile(512, F32, "rs%d" % i) for i in range(9)]
            E1s, SP1, LA, SP2, LNDT, Q1, Q2, Q3, Q4 = sl
            Q5 = self.tile(512, F32, "rsQ5")
            TOT = self.tile(4, F32, "rsTOT")
            tmpb = self.tile(512, F32, "rstmp")
            S.op("act", lambda e: e.activation(out=E1s.ap[P52], in_=R1.ap[P52], func=AF.Exp, bias=cols.ap[P52, 0:1]), reads=[R1.buf, cols.buf], writes=[E1s.buf])
            S.op("act", lambda e: e.activation(out=SP1.ap[P52], in_=E1s.ap[P52], func=AF.Ln, bias=1.0), reads=[E1s.buf], writes=[SP1.buf])
            S.op("dve", lambda e: e.tensor_scalar(out=LA.ap[P52], in0=SP1.ap[P52], scalar1=cols.ap[P52, 1:2], scalar2=None, op0=ALU.mult), reads=[SP1.buf, cols.buf], writes=[LA.buf])
            S.op("act", lambda e: e.activation(out=E1s.ap[P52], in_=R2.ap[P52], func=AF.Exp, scale=-1.0), reads=[R2.buf, SP1.buf], writes=[E1s.buf])
            S.op("act", lambda e: e.activation(out=SP2.ap[P52], in_=E1s.ap[P52], func=AF.Ln, bias=1.0), reads=[E1s.buf], writes=[SP2.buf])
            S.op("dve", lambda e: e.tensor_tensor_scan(out=SA.ap[P52], data0=self.rmask.ap[P52], data1=LA.ap[P52], initial=0.0, op0=ALU.mult, op1=ALU.add),
                 reads=[LA.buf, self.rmask.buf], writes=[SA.buf])
            S.op("dve", lambda e: e.tensor_copy(out=TOT.ap[P52], in_=V3(SA.ap[P52], 4)[:, :, 127]), reads=[SA.buf], writes=[TOT.buf])
            PB = slice(32, 52)
            S.op("dve", lambda e: e.tensor_tensor(out=tmpb.ap[PB], in0=LA.ap[PB], in1=SA.ap[PB], op=ALU.subtract), reads=[LA.buf, SA.buf], writes=[tmpb.buf])
            S.op("dve", lambda e: e.tensor_tensor(out=V3(SA.ap[PB], 4), in0=V3(tmpb.ap[PB], 4), in1=TOT.ap[PB].unsqueeze(2).broadcast_to([20, 4, 128]), op=ALU.add),
                 reads=[tmpb.buf, TOT.buf], writes=[SA.buf])
            S.op("act", lambda e: e.activation(out=LNDT.ap[P52], in_=SP1.ap[P52], func=AF.Ln), reads=[SP1.buf], writes=[LNDT.buf])
            S.op("dve", lambda e: e.scalar_tensor_tensor(out=Q1.ap[P52], in0=LNDT.ap[P52], scalar=cols.ap[P52, 2:3], in1=SA.ap[P52], op0=ALU.mult, op1=ALU.subtract),
                 reads=[LNDT.buf, SA.buf, cols.buf], writes=[Q1.buf])
            S.op("dve", lambda e: e.tensor_tensor(out=V3(tmpb.ap[P52], 4), in0=V3(Q1.ap[P52], 4), in1=TOT.ap[P52].unsqueeze(2).broadcast_to([52, 4, 128]), op=ALU.add),
                 reads=[Q1.buf, TOT.buf], writes=[tmpb.buf])
            S.op("act", lambda e: e.activation(out=Q2.ap[P52], in_=tmpb.ap[P52], func=AF.Exp), reads=[tmpb.buf], writes=[Q2.buf])
            S.op("dve", lambda e: e.scalar_tensor_tensor(out=SQ6.ap[P52], in0=SP2.ap[P52], scalar=cols.ap[P52, 3:4], in1=SA.ap[P52], op0=ALU.mult, op1=ALU.add),
                 reads=[SP2.buf, SA.buf, cols.buf], writes=[SQ6.buf])
            S.op("act", lambda e: e.activation(out=Q3.ap[P52], in_=SQ6.ap[P52], func=AF.Exp), reads=[SQ6.buf], writes=[Q3.buf])
            S.op("act", lambda e: e.activation(out=Q4.ap[P52], in_=SP2.ap[P52], func=AF.Exp, scale=-1.0), reads=[SP2.buf], writes=[Q4.buf])
            S.op("act", lambda e: e.activation(out=V3(Q5.ap[P52], 4), in_=TOT.ap[P52].unsqueeze(2).broadcast_to([52, 4, 128]), func=AF.Exp), reads=[TOT.buf], writes=[Q5.buf])
            slabs = [Q1, Q2, Q3, Q4, Q5, SQ6]
            for c in range(4):
                b = self.nb()
                ps = self.bank(b)

                def mmt(e, c=c, ps=ps):
                    for q, sb_ in enumerate(slabs):
                        ins = e.matmul(ps[:, q * 64:q * 64 + 52], lhsT=sb_.ap[P52, c * 128:(c + 1) * 128], rhs=idf.ap[0:52, 0:52], start=True, stop=True)
                    return ins
                S.op("pe", mmt, reads=[x.buf for x in slabs] + [idf.buf], writes=[self.pb[b]])
                S.op("dve", lambda e, c=c, ps=ps: e.tensor_copy(out=V3(TM[c].ap, 6)[:, :, 0:52], in_=V3(ps[:, 0:384], 6)[:, :, 0:52]), reads=[self.pb[b]], writes=[TM[c].buf])
            self.arena_reset(mark)
            tmv = [V3(TM[c].ap, 6) for c in range(4)]
            self.stop("stopP1")

            QIN = [self.tile(1024, BF16, "QIN%d" % d) for d in range(2)]
            KIN = [self.tile(1024, BF16, "KIN%d" % d) for d in range(2)]
            KST = [self.tile(1024, BF16, "KST%d" % d) for d in range(2)]
            DEC = self.tile(16, F32, "DECg")
            qtv = QT.v("p (c t) -> p c t", c=2)
            ktv = KT.v("p (c t) -> p c t", c=2)
            gm = self.arena_mark()
            combos = [(d, cc) for d in range(2) for cc in range(2)]
            gt_ = {}
            for (d, cc) in combos:
                gt_[(d, cc)] = dict(SPg=self.tile(512, F32, "SPg"), Gp=self.tile(512, F32, "Gp"), TOTg=self.tile(4, F32, "TOTg"), tg=self.tile(512, F32, "tg"),
                                    ex=[self.tile(512, F32, "ex%d" % i) for i in range(3)])
            for (d, cc) in combos:
                rb = 32 * d
                SPg = gt_[(d, cc)]["SPg"]
                b = self.nb()
                ps = self.bank(b)
                S.op("pe", lambda e, ps=ps, rb=rb, cc=cc: e.matmul(ps, lhsT=upw.ap[rb:rb + 16, cc * 128:(cc + 1) * 128], rhs=LR.ap[rb:rb + 16, :], start=True, stop=True),
                     reads=[upw.buf, LR.buf], writes=[self.pb[b]])
                S.op("act", lambda e, ps=ps, SPg=SPg, d=d, cc=cc: e.activation(out=SPg.ap, in_=ps, func=AF.Exp, scale=-1.0, bias=negb.ap[:, 2 * d + cc:2 * d + cc + 1]),
                     reads=[self.pb[b], negb.buf], writes=[SPg.buf])
                S.op("act", lambda e, SPg=SPg: e.activation(out=SPg.ap, in_=SPg.ap, func=AF.Ln, bias=1.0), reads=[SPg.buf], writes=[SPg.buf])
            for (d, cc) in combos:
                g_ = gt_[(d, cc)]
                SPg, Gp, TOTg, tg = g_["SPg"], g_["Gp"], g_["TOTg"], g_["tg"]
                S.op("dve", lambda e, SPg=SPg, Gp=Gp: e.tensor_tensor_scan(out=Gp.ap, data0=self.rmask.ap, data1=SPg.ap, initial=0.0, op0=ALU.mult, op1=ALU.add),
                     reads=[SPg.buf, self.rmask.buf], writes=[Gp.buf])
                S.op("dve", lambda e, Gp=Gp, TOTg=TOTg: e.tensor_copy(out=TOTg.ap, in_=V3(Gp.ap, 4)[:, :, 127]), reads=[Gp.buf], writes=[TOTg.buf])
                if d == 1:
                    S.op("dve", lambda e, SPg=SPg, Gp=Gp, tg=tg: e.tensor_tensor(out=tg.ap, in0=SPg.ap, in1=Gp.ap, op=ALU.subtract), reads=[SPg.buf, Gp.buf], writes=[tg.buf])
                    S.op("dve", lambda e, Gp=Gp, tg=tg, TOTg=TOTg: e.tensor_tensor(out=V3(Gp.ap, 4), in0=V3(tg.ap, 4), in1=TOTg.ap.unsqueeze(2).broadcast_to([128, 4, 128]), op=ALU.add),
                         reads=[tg.buf, TOTg.buf], writes=[Gp.buf])
            for (d, cc) in combos:
                g_ = gt_[(d, cc)]
                Gp, TOTg, ex = g_["Gp"], g_["TOTg"], g_["ex"]
                tg2 = g_["SPg"]
                S.op("act", lambda e, Gp=Gp, ex=ex: e.activation(out=ex[0].ap, in_=Gp.ap, func=AF.Exp, scale=-1.0 / 16, bias=float(np.log(0.125))), reads=[Gp.buf], writes=[ex[0].buf])
                S.op("act", lambda e, Gp=Gp, ex=ex: e.activation(out=ex[1].ap, in_=Gp.ap, func=AF.Exp, scale=1.0 / 16), reads=[Gp.buf], writes=[ex[1].buf])
                S.op("dve", lambda e, Gp=Gp, tg2=tg2, TOTg=TOTg: e.tensor_tensor(out=V3(tg2.ap, 4), in0=V3(Gp.ap, 4), in1=TOTg.ap.unsqueeze(2).broadcast_to([128, 4, 128]), op=ALU.subtract),
                     reads=[Gp.buf, TOTg.buf], writes=[tg2.buf])
                S.op("act", lambda e, tg2=tg2, ex=ex: e.activation(out=ex[2].ap, in_=tg2.ap, func=AF.Exp, scale=1.0 / 16), reads=[tg2.buf], writes=[ex[2].buf])
                o4 = (d * 2 + cc) * 4
                S.op("act", lambda e, TOTg=TOTg, o4=o4: e.activation(out=DEC.ap[:, o4:o4 + 4], in_=TOTg.ap, func=AF.Exp, scale=-1.0 / 16), reads=[TOTg.buf], writes=[DEC.buf])
            for (d, cc) in combos:
                ex = gt_[(d, cc)]["ex"]
                S.op("dve", lambda e, ex=ex, d=d, cc=cc: e.tensor_tensor(out=QIN[d].ap[:, cc * 512:(cc + 1) * 512], in0=qtv[:, cc, :], in1=ex[0].ap, op=ALU.mult),
                     reads=[QT.buf, ex[0].buf], writes=[QIN[d].buf])
                S.op("pool", lambda e, ex=ex, d=d, cc=cc: e.tensor_tensor(out=KIN[d].ap[:, cc * 512:(cc + 1) * 512], in0=ktv[:, cc, :], in1=ex[1].ap, op=ALU.mult),
                     reads=[KT.buf, ex[1].buf], writes=[KIN[d].buf])
                S.op("pool", lambda e, ex=ex, d=d, cc=cc: e.tensor_tensor(out=KST[d].ap[:, cc * 512:(cc + 1) * 512], in0=ktv[:, cc, :], in1=ex[2].ap, op=ALU.mult),
                     reads=[KT.buf, ex[2].buf], writes=[KST[d].buf])
            self.arena_reset(gm)
            self.stop("stopG1")
            qinv = [V3(QIN[d].ap, 2) for d in range(2)]
            kinv = [V3(KIN[d].ap, 2) for d in range(2)]
            kstv = [V3(KST[d].ap, 2) for d in range(2)]
            vtv = VT.v("p (s c) -> p s c", s=4)
            for c in range(4):
                ch = t * 4 + c
                csl = slice(c * 128, (c + 1) * 128)
                a1 = self.tile(512, F32, "a1")
                a2 = self.tile(512, F32, "a2")
                attT = self.tile(512, BF16, "attT")
                kstt = self.tile(512, BF16, "kstt")
                for d in range(2):
                    bA, bB = self.nb(), self.nb()
                    psA, psB = self.bank(bA), self.bank(bB)

                    def mma(e, d=d, psA=psA, psB=psB, csl=csl):
                        for h in range(4):
                            cc, ee = h // 2, h % 2
                            pso_ = psA if ee == 0 else psB
                            ins = e.matmul(pso_[:, cc * 128:(cc + 1) * 128], lhsT=kinv[d][64 * ee:64 * ee + 64, cc, csl], rhs=qinv[d][64 * ee:64 * ee + 64, cc, csl], start=True, stop=True)
                        return ins
                    S.op("pe", mma, reads=[KIN[d].buf, QIN[d].buf], writes=[self.pb[bA], self.pb[bB]])
                    at_, mk_ = (a1, self.maskL) if d == 0 else (a2, self.maskU)
                    for ee, (bb_, pp_) in enumerate(((bA, psA), (bB, psB))):
                        S.op("dve", lambda e, at_=at_, mk_=mk_, pp_=pp_, ee=ee: e.tensor_tensor(out=V3(at_.ap, 4)[:, ee:4:2, :], in0=V3(pp_[:, 0:256], 2), in1=V3(mk_.ap[:, 0:256], 2), op=ALU.mult),
                             reads=[self.pb[bb_], mk_.buf], writes=[at_.buf])
                S.op("pool", lambda e, a1=a1, a2=a2, attT=attT: e.tensor_tensor(out=attT.ap, in0=a1.ap, in1=a2.ap, op=ALU.add), reads=[a1.buf, a2.buf], writes=[attT.buf])
                self.stop("stopG2a")
                b = self.nb()
                ps = self.bank(b)

                def mmo(e, ps=ps, attT=attT, c=c):
                    for h in range(4):
                        ins = e.matmul(ps[:, h * 128:(h + 1) * 128], lhsT=attT.ap[:, h * 128:(h + 1) * 128], rhs=vtv[:, c, h * 128:(h + 1) * 128], start=True, stop=True)
                    return ins
                S.op("pe", mmo, reads=[attT.buf, VT.buf], writes=[self.pb[b]])
                S.op("act", lambda e, ps=ps, c=c: e.activation(out=OG[c % 2].ap, in_=ps, func=AF.Copy), reads=[self.pb[b]], writes=[OG[c % 2].buf])
                S.dma("sp", self.OI[ch * 128:(ch + 1) * 128, 0:512], OG[c % 2].ap, reads=[OG[c % 2].buf])
                S.dma("sp", self.OI[ch * 128:(ch + 1) * 128, 512:1024], ZZ.ap, reads=[ZZ.buf])
                self.stop("stopG2b")
                b = self.nb()
                pbf = self.bank(b, BF16)

                def trk(e, pbf=pbf, csl=csl):
                    for d in range(2):
                        for cc in range(2):
                            ins = e.transpose(pbf[:, (d * 2 + cc) * 128:(d * 2 + cc + 1) * 128], kstv[d][:, cc, csl], idb.ap)
                    return ins
                S.op("pe", trk, reads=[KST[0].buf, KST[1].buf, idb.buf], writes=[self.pb[b]])
                S.op("dve", lambda e, pbf=pbf, kstt=kstt: e.tensor_copy(out=kstt.ap, in_=pbf[:, 0:512]), reads=[self.pb[b]], writes=[kstt.buf])
                self.stop("stopG2")
                for d in range(2):
                    S.dma("sp", V3(self.RB[d][ch][:, 0:256], 2), qinv[d][:, :, csl], reads=[QIN[d].buf])
                    S.dma("sp", self.RB[d][ch][:, 256:512], kstt.ap[:, d * 256:(d + 1) * 256], reads=[kstt.buf])
                    S.dma("sp", self.RF[d][ch][:, 512:514], V3(DEC.ap, 4)[:, 2 * d:2 * d + 2, c], reads=[DEC.buf], slow=True)
            self.arena_reset(mark)

            self.stop("stopP2")
            CVS = self.tile(10 * 512, BF16, "CVS")
            cvs = CVS.v("p (c t) -> p c t", c=10)
            for cc in range(10):
                b = self.nb()
                ps = self.bank(b)

                def mmc(e, cc=cc, ps=ps):
                    for k in range(5):
                        i = (12 + cc) * 5 + k
                        ins = e.matmul(ps, lhsT=dg.ap[:, i * 128:(i + 1) * 128], rhs=rawv[:, 12 + cc, k:k + 512], start=(k == 0), stop=(k == 4))
                    return ins
                S.op("pe", mmc, reads=[dg.buf, RAWC.buf], writes=[self.pb[b]])
                S.op("act", lambda e, cc=cc, ps=ps: e.activation(out=cvs[:, cc, :], in_=ps, func=AF.Silu, bias=cwT.ap[:, 110 + cc:111 + cc]), reads=[self.pb[b], cwT.buf], writes=[CVS.buf])
            sm = self.arena_mark()
            for c in range(4):
                if c == 2:
                    self.arena_reset(sm)
                ch = t * 4 + c
                csl = slice(c * 128, (c + 1) * 128)
                XTOK = self.tile(1024, BF16, "XTOK")
                BTOK = self.tile(128, BF16, "BTOK")
                CBT = self.tile(256, F32, "CBT")
                b = self.nb()
                pbf = self.bank(b, BF16)

                def trx(e, pbf=pbf, csl=csl):
                    for cc in range(8):
                        ins = e.transpose(pbf[:, cc * 128:(cc + 1) * 128], cvs[:, cc, csl], idb.ap)
                    return ins
                S.op("pe", trx, reads=[CVS.buf, idb.buf], writes=[self.pb[b]])
                S.op("act", lambda e, pbf=pbf, XTOK=XTOK: e.activation(out=XTOK.ap, in_=pbf, func=AF.Copy), reads=[self.pb[b]], writes=[XTOK.buf])
                b = self.nb()
                pbf = self.bank(b, BF16)
                S.op("pe", lambda e, pbf=pbf, csl=csl: e.transpose(pbf[:, 0:128], cvs[:, 8, csl], idb.ap), reads=[CVS.buf, idb.buf], writes=[self.pb[b]])
                S.op("dve", lambda e, pbf=pbf, BTOK=BTOK: e.tensor_copy(out=BTOK.ap, in_=pbf[:, 0:128]), reads=[self.pb[b]], writes=[BTOK.buf])
                S.dma("sp", self.RS[ch][:, 0:128], cvs[:, 9, csl], reads=[CVS.buf])
                S.dma("sp", self.RS[ch][:, 128:256], BTOK.ap, reads=[BTOK.buf])
                bA, bB = self.nb(), self.nb()
                psA, psB = self.bank(bA), self.bank(bB)

                def mmcb(e, psA=psA, psB=psB, csl=csl):
                    e.matmul(psA[:, 0:128], lhsT=cvs[0:64, 8, csl], rhs=cvs[0:64, 9, csl], start=True, stop=True)
                    return e.matmul(psB[:, 0:128], lhsT=cvs[64:128, 8, csl], rhs=cvs[64:128, 9, csl], start=True, stop=True)
                S.op("pe", mmcb, reads=[CVS.buf], writes=[self.pb[bA], self.pb[bB]])
                S.op("act", lambda e, psA=psA, CBT=CBT: e.activation(out=CBT.ap[:, 0:128], in_=psA[:, 0:128], func=AF.Copy), reads=[self.pb[bA]], writes=[CBT.buf])
                S.op("act", lambda e, psB=psB, CBT=CBT: e.activation(out=CBT.ap[:, 128:256], in_=psB[:, 0:128], func=AF.Copy), reads=[self.pb[bB]], writes=[CBT.buf])
                MT = [self.tile(512, BF16, "MT%d" % g4) for g4 in range(4)]
                LPs = {}
                for g4 in range(4):
                    for d in range(2):
                        rb = 32 * d
                        LT = self.tile(512, F32, "LT%d%d" % (g4, d))
                        b = self.nb()
                        ps = self.bank(b)

                        def mms(e, ps=ps, rb=rb, g4=g4, csl=csl):
                            for hh in range(4):
                                h = 4 * g4 + hh
                                ins = e.matmul(ps[:, hh * 128:(hh + 1) * 128], lhsT=self.sel.ap[rb:rb + 20, h * 128:(h + 1) * 128], rhs=SA.ap[rb:rb + 20, csl], start=True, stop=True)
                            return ins
                        S.op("pe", mms, reads=[self.sel.buf, SA.buf], writes=[self.pb[b]])
                        S.op("dve", lambda e, ps=ps, LT=LT, rb=rb, g4=g4, c=c: e.tensor_tensor(out=V3(LT.ap, 4), in0=V3(ps, 4),
                                                                                             in1=tmv[c][:, 0, rb + 4 * g4:rb + 4 * g4 + 4].unsqueeze(2).broadcast_to([128, 4, 128]), op=ALU.add),
                             reads=[self.pb[b], TM[c].buf], writes=[LT.buf])
                        pat, cm = (([[0, 4], [1, 128]], -1) if d == 0 else ([[0, 4], [-1, 128]], 1))
                        S.op("pool", lambda e, LT=LT, pat=pat, cm=cm: e.affine_select(out=V3(LT.ap, 4), in_=V3(LT.ap, 4), pattern=pat, compare_op=ALU.is_ge, fill=self.freg(e, -30000.0),
                                                                                    base=0, channel_multiplier=cm), reads=[LT.buf], writes=[LT.buf])
                        S.op("act", lambda e, LT=LT: e.activation(out=LT.ap, in_=LT.ap, func=AF.Exp), reads=[LT.buf], writes=[LT.buf])
                        LPs[(g4, d)] = LT
                for g4 in range(4):
                    LP = [LPs[(g4, 0)], LPs[(g4, 1)]]
                    g = g4 // 2
                    S.op("pool", lambda e, LP=LP: e.tensor_tensor(out=LP[0].ap, in0=LP[0].ap, in1=LP[1].ap, op=ALU.add), reads=[LP[0].buf, LP[1].buf], writes=[LP[0].buf])
                    S.op("dve", lambda e, LP=LP, g=g, CBT=CBT: e.tensor_tensor(out=V3(LP[0].ap, 4), in0=V3(LP[0].ap, 4), in1=CBT.ap[:, g * 128:(g + 1) * 128].unsqueeze(1).broadcast_to([128, 4, 128]),
                                                                            op=ALU.mult), reads=[LP[0].buf, CBT.buf], writes=[LP[0].buf])
                    S.op("pool", lambda e, LP=LP, g4=g4: e.tensor_tensor(out=MT[g4].ap, in0=LP[0].ap, in1=did.ap[:, g4 * 512:(g4 + 1) * 512], op=ALU.add),
                         reads=[LP[0].buf, did.buf], writes=[MT[g4].buf])
                for half in range(2):
                    b = self.nb()
                    ps = self.bank(b)

                    def mmy(e, ps=ps, half=half, XTOK=XTOK):
                        for hh in range(8):
                            h = half * 8 + hh
                            ins = e.matmul(ps[:, hh * 64:(hh + 1) * 64], lhsT=MT[h // 4].ap[:, (h % 4) * 128:(h % 4 + 1) * 128], rhs=XTOK.ap[:, h * 64:(h + 1) * 64], start=True, stop=True)
                        return ins
                    S.op("pe", mmy, reads=[m_.buf for m_ in MT] + [XTOK.buf], writes=[self.pb[b]])
                    S.op("act", lambda e, ps=ps, half=half, c=c: e.activation(out=OS[c % 2].ap[:, half * 512:(half + 1) * 512], in_=ps, func=AF.Copy),
                         reads=[self.pb[b]], writes=[OS[c % 2].buf])
                for d in range(2):
                    rb = 32 * d
                    XS = self.tile(1024, BF16, "XS%d" % d)
                    eng = "dve" if d == 0 else "pool"
                    S.op(eng, lambda e, XS=XS, XTOK=XTOK, rb=rb, c=c: e.tensor_tensor(out=V3(XS.ap, 16), in0=V3(XTOK.ap, 16),
                                                                                   in1=tmv[c][:, 1, rb:rb + 16].unsqueeze(2).broadcast_to([128, 16, 64]), op=ALU.mult),
                         reads=[XTOK.buf, TM[c].buf], writes=[XS.buf])
                    S.dma("sp", self.RB[d][ch][:, 2560:3584], XS.ap, reads=[XS.buf])
                    S.dma("sp", self.RF[d][ch][:, 518:534], tmv[c][:, 2, rb:rb + 16], reads=[TM[c].buf], slow=True)
                    S.dma("sp", self.RF[d][ch][:, 534:550], tmv[c][:, 4, rb:rb + 16], reads=[TM[c].buf], slow=True)
                S.dma("sp", self.OI[ch * 128:(ch + 1) * 128, 1024:2048], OS[c % 2].ap, reads=[OS[c % 2].buf])
            self.arena_reset(mark)

            self.stop("stopP3")
            CVG = self.tile(12 * 512, BF16, "CVG")
            cvg = CVG.v("p (c t) -> p c t", c=12)
            for cc in range(12):
                b = self.nb()
                ps = self.bank(b)

                def mmc(e, cc=cc, ps=ps):
                    for k in range(5):
                        i = cc * 5 + k
                        ins = e.matmul(ps, lhsT=dg.ap[:, i * 128:(i + 1) * 128], rhs=rawv[:, cc, k:k + 512], start=(k == 0), stop=(k == 4))
                    return ins
                S.op("pe", mmc, reads=[dg.buf, RAWC.buf], writes=[self.pb[b]])
                S.op("act", lambda e, cc=cc, ps=ps: e.activation(out=cvg[:, cc, :], in_=ps, func=AF.Silu), reads=[self.pb[b]], writes=[CVG.buf])
            QKN = self.tile(8 * 512, BF16, "QKN")
            qkn = QKN.v("p (c t) -> p c t", c=8)
            gmark = self.arena_mark()
            SQs = [self.tile(512, BF16, "SQ%d" % i) for i in range(2)]
            RSTs = [self.tile(512, F32, "RST%d" % i) for i in range(2)]
            for cc in range(8):
                SQ = SQs[cc % 2]
                RST = RSTs[cc % 2]
                S.op("pool", lambda e, cc=cc, SQ=SQ: e.tensor_tensor(out=SQ.ap, in0=cvg[:, cc, :], in1=cvg[:, cc, :], op=ALU.mult), reads=[CVG.buf], writes=[SQ.buf])
                b = self.nb()
                ps = self.bank(b)
                S.op("pe", lambda e, ps=ps, SQ=SQ: e.matmul(ps, lhsT=self.onesb.ap, rhs=SQ.ap, start=True, stop=True), reads=[self.onesb.buf, SQ.buf], writes=[self.pb[b]])
                S.op("act", lambda e, ps=ps, RST=RST: e.activation(out=RST.ap, in_=ps, func=AF.Ln, bias=EPS), reads=[self.pb[b]], writes=[RST.buf])
                bias = float(np.log(128.0 ** -0.5)) if cc < 4 else 0.0
                S.op("act", lambda e, RST=RST, bias=bias: e.activation(out=RST.ap, in_=RST.ap, func=AF.Exp, scale=-0.5, bias=bias), reads=[RST.buf], writes=[RST.buf])
                S.op("dve", lambda e, cc=cc, RST=RST: e.tensor_tensor(out=qkn[:, cc, :], in0=cvg[:, cc, :], in1=RST.ap, op=ALU.mult), reads=[CVG.buf, RST.buf], writes=[QKN.buf])
            self.stop("stopD2")
            for c in range(4):
                self.arena_reset(gmark)
                ch = t * 4 + c
                csl = slice(c * 128, (c + 1) * 128)
                KVT = self.tile(1024, BF16, "KVT")
                NKK = self.tile(512, F32, "NKK")
                QKT = self.tile(512, F32, "QKT")
                b = self.nb()
                pbf = self.bank(b, BF16)

                def trkv(e, pbf=pbf, csl=csl):
                    for h in range(4):
                        ins = e.transpose(pbf[:, h * 128:(h + 1) * 128], qkn[:, 4 + h, csl], idb.ap)
                    for h in range(4):
                        ins = e.transpose(pbf[:, 512 + h * 128:512 + (h + 1) * 128], cvg[:, 8 + h, csl], idb.ap)
                    return ins
                S.op("pe", trkv, reads=[QKN.buf, CVG.buf, idb.buf], writes=[self.pb[b]])
                S.op("act", lambda e, pbf=pbf, KVT=KVT: e.activation(out=KVT.ap, in_=pbf, func=AF.Copy), reads=[self.pb[b]], writes=[KVT.buf])
                b = self.nb()
                ps = self.bank(b)

                def mmkk(e, ps=ps, csl=csl):
                    for h in range(4):
                        ins = e.matmul(ps[:, h * 128:(h + 1) * 128], lhsT=qkn[:, 4 + h, csl], rhs=qkn[:, 4 + h, csl], start=True, stop=True)
                    return ins
                S.op("pe", mmkk, reads=[QKN.buf], writes=[self.pb[b]])
                S.op("act", lambda e, ps=ps, NKK=NKK: e.activation(out=NKK.ap, in_=ps, func=AF.Copy, scale=-1.0), reads=[self.pb[b]], writes=[NKK.buf])
                b = self.nb()
                ps = self.bank(b)

                def mmqk(e, ps=ps, csl=csl):
                    for h in range(4):
                        ins = e.matmul(ps[:, h * 128:(h + 1) * 128], lhsT=qkn[:, 4 + h, csl], rhs=qkn[:, h, csl], start=True, stop=True)
                    return ins
                S.op("pe", mmqk, reads=[QKN.buf], writes=[self.pb[b]])
                S.op("dve", lambda e, ps=ps, QKT=QKT: e.tensor_copy(out=QKT.ap, in_=ps), reads=[self.pb[b]], writes=[QKT.buf])
                self.stop("stopD3")
                U0 = [self.tile(512, BF16, "U0_%d" % d) for d in range(2)]
                N0 = [self.tile(512, BF16, "N0_%d" % d) for d in range(2)]
                emark = self.arena_mark()
                for d in range(2):
                    rb = 32 * d
                    bg = self.nb()
                    psg = self.bank(bg)
                    bgb = self.nb()
                    psgb = self.bank(bgb)

                    def mmbg(e, psg=psg, rb=rb, csl=csl, src=SA):
                        for h in range(4):
                            ins = e.matmul(psg[:, h * 128:(h + 1) * 128], lhsT=self.sel.ap[rb:rb + 20, (16 + h) * 128:(17 + h) * 128], rhs=src.ap[rb:rb + 20, csl], start=True, stop=True)
                        return ins
                    S.op("pe", mmbg, reads=[self.sel.buf, SA.buf], writes=[self.pb[bg]])
                    S.op("pe", lambda e, psgb=psgb, rb=rb, csl=csl: mmbg(e, psgb, rb, csl, SQ6), reads=[self.sel.buf, SQ6.buf], writes=[self.pb[bgb]])
                    nG = tmv[c][:, 0, rb + 16:rb + 20].unsqueeze(2).broadcast_to([128, 4, 128])
                    GB = tmv[c][:, 5, rb + 16:rb + 20].unsqueeze(2).broadcast_to([128, 4, 128])
                    E3 = self.tile(512, F32, "E3_%d" % d)
                    E1 = self.tile(512, F32, "E1_%d" % d)
                    E2 = self.tile(512, F32, "E2_%d" % d)
                    EG = self.tile(512, F32, "EG_%d" % d)
                    AQK = self.tile(512, BF16, "AQK%d" % d)
                    QDT = self.tile(512, BF16, "QDT%d" % d)
                    if d == 0:
                        m3 = ([[0, 4], [1, 128]], -1, 0)
                        m1 = ([[0, 4], [1, 128]], -1, -1)
                        m2 = ([[0, 4], [-1, 128]], 1, -1)
                    else:
                        m3 = ([[0, 4], [-1, 128]], 1, 0)
                        m1 = ([[0, 4], [-1, 128]], 1, -1)
                        m2 = ([[0, 4], [1, 128]], -1, -1)
                    S.op("dve", lambda e, E3=E3, psg=psg, nG=nG: e.tensor_tensor(out=V3(E3.ap, 4), in0=V3(psg, 4), in1=nG, op=ALU.add), reads=[self.pb[bg], TM[c].buf], writes=[E3.buf])
                    S.op("dve", lambda e, E1=E1, psgb=psgb, nG=nG: e.tensor_tensor(out=V3(E1.ap, 4), in0=V3(psgb, 4), in1=nG, op=ALU.add), reads=[self.pb[bgb], TM[c].buf], writes=[E1.buf])
                    S.op("dve", lambda e, E2=E2, psg=psg, GB=GB: e.scalar_tensor_tensor(out=V3(E2.ap, 4), in0=V3(psg, 4), scalar=-1.0, in1=GB, op0=ALU.mult, op1=ALU.add),
                         reads=[self.pb[bg], TM[c].buf], writes=[E2.buf])
                    S.op("act", lambda e, EG=EG, psg=psg: e.activation(out=EG.ap, in_=psg, func=AF.Exp), reads=[self.pb[bg]], writes=[EG.buf])
                    for Et, (pat, cm, base_) in ((E3, m3), (E1, m1), (E2, m2)):
                        S.op("pool", lambda e, Et=Et, pat=pat, cm=cm, base_=base_: e.affine_select(out=V3(Et.ap, 4), in_=V3(Et.ap, 4), pattern=pat, compare_op=ALU.is_ge, fill=self.freg(e, -30000.0),
                                                                                                 base=base_, channel_multiplier=cm), reads=[Et.buf], writes=[Et.buf])
                        S.op("act", lambda e, Et=Et: e.activation(out=Et.ap, in_=Et.ap, func=AF.Exp), reads=[Et.buf], writes=[Et.buf])
                    S.op("dve", lambda e, AQK=AQK, E3=E3, QKT=QKT: e.tensor_tensor(out=AQK.ap, in0=QKT.ap, in1=E3.ap, op=ALU.mult), reads=[QKT.buf, E3.buf], writes=[AQK.buf])
                    S.op("pool", lambda e, d=d, E1=E1, NKK=NKK: e.tensor_tensor(out=U0[d].ap, in0=NKK.ap, in1=E1.ap, op=ALU.mult), reads=[NKK.buf, E1.buf], writes=[U0[d].buf])
                    S.op("pool", lambda e, d=d, E2=E2, NKK=NKK: e.tensor_tensor(out=N0[d].ap, in0=NKK.ap, in1=E2.ap, op=ALU.mult), reads=[NKK.buf, E2.buf], writes=[N0[d].buf])
                    S.op("dve", lambda e, QDT=QDT, EG=EG, csl=csl: e.tensor_tensor(out=V3(QDT.ap, 4), in0=qkn[:, 0:4, csl], in1=V3(EG.ap, 4), op=ALU.mult), reads=[QKN.buf, EG.buf], writes=[QDT.buf])
                    S.dma("sp", self.RB[d][ch][:, 1024:1536], AQK.ap, reads=[AQK.buf])
                    S.dma("sp", self.RB[d][ch][:, 1536:2048], QDT.ap, reads=[QDT.buf])
                    S.dma("sp", self.RF[d][ch][:, 514:518], tmv[c][:, 4, rb + 16:rb + 20], reads=[TM[c].buf], slow=True)
                self.arena_reset(emark)
                idrep = idf.ap.unsqueeze(1).broadcast_to([128, 4, 128])

                def mm4(e, ps, lt, rt):
                    for h in range(4):
                        ins = e.matmul(ps[:, h * 128:(h + 1) * 128], lhsT=lt.ap[:, h * 128:(h + 1) * 128], rhs=rt.ap[:, h * 128:(h + 1) * 128], start=True, stop=True)
                    return ins

                def mmop(lt, rt):
                    b = self.nb()
                    ps = self.bank(b)
                    S.op("pe", lambda e, ps=ps, lt=lt, rt=rt: mm4(e, ps, lt, rt), reads=[lt.buf, rt.buf], writes=[self.pb[b]])
                    return b, ps

                def evac(eng, b, ps, o):
                    if eng == "act":
                        S.op("act", lambda e, ps=ps, o=o: e.activation(out=o.ap, in_=ps, func=AF.Copy), reads=[self.pb[b]], writes=[o.buf])
                    else:
                        S.op("dve", lambda e, ps=ps, o=o: e.tensor_copy(out=o.ap, in_=ps), reads=[self.pb[b]], writes=[o.buf])

                def evacadd(b, ps, o, xi):
                    S.op("dve", lambda e, ps=ps, o=o, xi=xi: e.tensor_tensor(out=o.ap, in0=ps, in1=xi.ap, op=ALU.add), reads=[self.pb[b], xi.buf], writes=[o.buf])

                XT = [None, None]
                T_ = []
                for d in range(2):
                    T_.append(dict(Nn=[self.tile(512, BF16, "Nn%d%d" % (d, i)) for i in range(2)], Un=[self.tile(512, BF16, "Un%d%d" % (d, i)) for i in range(2)],
                                   Yy=[self.tile(512, BF16, "Yy%d%d" % (d, i)) for i in range(2)], Yt=[self.tile(512, BF16, "Yt%d%d" % (d, i)) for i in range(2)],
                                   No=self.tile(512, BF16, "No%d" % d), Uo=self.tile(512, BF16, "Uo%d" % d), Zz=self.tile(512, BF16, "Zz%d" % d), Zp=self.tile(512, BF16, "Zp%d" % d)))
                for d in range(2):
                    Nn, Un, Yy, Yt = T_[d]["Nn"], T_[d]["Un"], T_[d]["Yy"], T_[d]["Yt"]
                    S.op("pool", lambda e, d=d, o=Nn[0]: e.tensor_tensor(out=o.ap, in0=N0[d].ap, in1=self.bmask[0].ap, op=ALU.mult), reads=[N0[d].buf, self.bmask[0].buf], writes=[Nn[0].buf])
                    S.op("dve", lambda e, d=d, o=Un[0]: e.tensor_tensor(out=o.ap, in0=U0[d].ap, in1=self.bmask[0].ap, op=ALU.mult), reads=[U0[d].buf, self.bmask[0].buf], writes=[Un[0].buf])
                    S.op("pool", lambda e, o=Yy[0], i_=Nn[0]: e.tensor_tensor(out=V3(o.ap, 4), in0=V3(i_.ap, 4), in1=idrep, op=ALU.add), reads=[Nn[0].buf, idf.buf], writes=[Yy[0].buf])
                    S.op("pool", lambda e, o=Yt[0], i_=Un[0]: e.tensor_tensor(out=V3(o.ap, 4), in0=V3(i_.ap, 4), in1=idrep, op=ALU.add), reads=[Un[0].buf, idf.buf], writes=[Yt[0].buf])
                cur = 0
                for m in range(3):
                    nxt = 1 - cur
                    pend = []
                    for d in range(2):
                        Nn, Un = T_[d]["Nn"], T_[d]["Un"]
                        b, ps = mmop(Nn[cur], Un[cur])
                        evac("act", b, ps, Un[nxt])
                        b, ps = mmop(Un[cur], Nn[cur])
                        evac("dve" if d == 0 else "act", b, ps, Nn[nxt])
                    for d in range(2):
                        Nn, Un, Yy, Yt = T_[d]["Nn"], T_[d]["Un"], T_[d]["Yy"], T_[d]["Yt"]
                        b, ps = mmop(Nn[nxt], Yt[cur])
                        evacadd(b, ps, Yt[nxt], Yt[cur])
                        b, ps = mmop(Un[nxt], Yy[cur])
                        evacadd(b, ps, Yy[nxt], Yy[cur])
                    cur = nxt
                for lvl in range(3):
                    nxt = 1 - cur
                    mk = self.bmask[1 + lvl]
                    for d in range(2):
                        No, Uo, Zz, Zp, Yy, Yt = T_[d]["No"], T_[d]["Uo"], T_[d]["Zz"], T_[d]["Zp"], T_[d]["Yy"], T_[d]["Yt"]
                        S.op("pool", lambda e, d=d, mk=mk, No=No: e.tensor_tensor(out=No.ap, in0=N0[d].ap, in1=mk.ap, op=ALU.mult), reads=[N0[d].buf, mk.buf], writes=[No.buf])
                        b, ps = mmop(No, Yt[cur])
                        evac("act", b, ps, Zz)
                        if lvl < 2:
                            S.op("pool", lambda e, d=d, mk=mk, Uo=Uo: e.tensor_tensor(out=Uo.ap, in0=U0[d].ap, in1=mk.ap, op=ALU.mult), reads=[U0[d].buf, mk.buf], writes=[Uo.buf])
                            b, ps = mmop(Uo, Yy[cur])
                            evac("dve" if d == 0 else "act", b, ps, Zp)
                    for d in range(2):
                        Zz, Zp, Yy, Yt = T_[d]["Zz"], T_[d]["Zp"], T_[d]["Yy"], T_[d]["Yt"]
                        b, ps = mmop(Yy[cur], Zz)
                        evacadd(b, ps, Yt[nxt], Yt[cur])
                        if lvl < 2:
                            b, ps = mmop(Yt[cur], Zp)
                            evacadd(b, ps, Yy[nxt], Yy[cur])
                    cur = nxt
                for d in range(2):
                    XT[d] = T_[d]["Yt"][cur]
                for d in range(2):
                    rb = 32 * d
                    Xt = XT[d]
                    RK = self.tile(512, BF16, "RK%d" % d)
                    RV = self.tile(512, BF16, "RV%d" % d)
                    KD = self.tile(512, BF16, "KD%d" % d)
                    WT = self.tile(512, BF16, "WT%d" % d)
                    UU = self.tile(512, F32, "UU%d" % d)
                    bc = lambda q: tmv[c][:, q, rb + 16:rb + 20].unsqueeze(2).broadcast_to([128, 4, 128])
                    bc1, bc2, bc3 = bc(1), bc(2), bc(3)
                    S.op("dve", lambda e, RK=RK, KVT=KVT, bc2=bc2: e.tensor_tensor(out=V3(RK.ap, 4), in0=V3(KVT.ap[:, 0:512], 4), in1=bc2, op=ALU.mult), reads=[KVT.buf, TM[c].buf], writes=[RK.buf])
                    S.op("pool", lambda e, RV=RV, KVT=KVT, bc3=bc3: e.tensor_tensor(out=V3(RV.ap, 4), in0=V3(KVT.ap[:, 512:1024], 4), in1=bc3, op=ALU.mult), reads=[KVT.buf, TM[c].buf], writes=[RV.buf])
                    S.op("pool", lambda e, KD=KD, KVT=KVT, bc1=bc1: e.tensor_tensor(out=V3(KD.ap, 4), in0=V3(KVT.ap[:, 0:512], 4), in1=bc1, op=ALU.mult), reads=[KVT.buf, TM[c].buf], writes=[KD.buf])
                    S.dma("sp", self.RB[d][ch][:, 2048:2560], KD.ap, reads=[KD.buf])
                    b = self.nb()
                    ps = self.bank(b)

                    def mmw(e, ps=ps, RK=RK, Xt=Xt):
                        for h in range(4):
                            ins = e.matmul(ps[:, h * 128:(h + 1) * 128], lhsT=RK.ap[:, h * 128:(h + 1) * 128], rhs=Xt.ap[:, h * 128:(h + 1) * 128], start=True, stop=True)
                        return ins
                    S.op("pe", mmw, reads=[RK.buf, Xt.buf], writes=[self.pb[b]])
                    S.op("act", lambda e, ps=ps, WT=WT: e.activation(out=WT.ap, in_=ps, func=AF.Copy), reads=[self.pb[b]], writes=[WT.buf])
                    S.dma("sp", self.RB[d][ch][:, 512:1024], WT.ap, reads=[WT.buf])
                    b = self.nb()
                    ps = self.bank(b)

                    def mmu(e, ps=ps, RV=RV, Xt=Xt):
                        for h in range(4):
                            ins = e.matmul(ps[:, h * 128:(h + 1) * 128], lhsT=Xt.ap[:, h * 128:(h + 1) * 128], rhs=RV.ap[:, h * 128:(h + 1) * 128], start=True, stop=True)
                        return ins
                    S.op("pe", mmu, reads=[RV.buf, Xt.buf], writes=[self.pb[b]])
                    S.op("dve", lambda e, ps=ps, UU=UU: e.tensor_copy(out=UU.ap, in_=ps), reads=[self.pb[b]], writes=[UU.buf])
                    S.dma("sp", self.RF[d][ch][:, 0:512], UU.ap, reads=[UU.buf])
            self.arena_reset(mark)

    def pass_B(self, l):
        S = self.S
        self.reset()
        self._bank_rr = 0
        V3 = lambda ap, h: ap.rearrange("p (h i) -> p h i", h=h)
        XB, XF = self.XB, self.XF
        NBUF = 2
        RBt = [[self.tile(XB, BF16, "RBt%d%d" % (d, i)) for i in range(NBUF)] for d in range(2)]
        RFt = [[self.tile(XF, F32, "RFt%d%d" % (d, i)) for i in range(NBUF)] for d in range(2)]
        RSt = [[self.tile(256, BF16, "RSt%d%d" % (d, i)) for i in range(NBUF)] for d in range(2)]
        Vt = [[self.tile(512, BF16, "Vt%d%d" % (d, i)) for i in range(NBUF)] for d in range(2)]
        Ot = [[self.tile(2048, F32, "Ot%d%d" % (d, i)) for i in range(NBUF)] for d in range(2)]
        Sg = [[self.tile(128, F32, "Sg%d%d" % (d, hp)) for hp in range(2)] for d in range(2)]
        Sgb = [[self.tile(128, BF16, "Sgb%d%d" % (d, hp)) for hp in range(2)] for d in range(2)]
        Sd = [self.tile(512, F32, "Sd%d" % d) for d in range(2)]
        Sdb = [self.tile(512, BF16, "Sdb%d" % d) for d in range(2)]
        Ss = [self.tile(512, F32, "Ss%d" % d) for d in range(2)]
        Ssb = [self.tile(512, BF16, "Ssb%d" % d) for d in range(2)]
        UP = [self.tile(512, BF16, "UP%d" % d) for d in range(2)]

        def load(c, d, i):
            S.dma("sp", RBt[d][i].ap, self.RB[d][c], writes=[RBt[d][i].buf])
            S.dma("sp", RFt[d][i].ap[:, 0:550], self.RF[d][c][:, 0:550], writes=[RFt[d][i].buf])
            S.dma("sp", RSt[d][i].ap, self.RS[c], writes=[RSt[d][i].buf])
            S.dma("sp", Vt[d][i].ap, self.Tv[c * 128:(c + 1) * 128, :], writes=[Vt[d][i].buf])

        def step(c, d, i):
            rb_, rf_, rs_, vt_, ot_ = RBt[d][i], RFt[d][i], RSt[d][i], Vt[d][i], Ot[d][i]
            rb = rb_.ap
            rf = rf_.ap
            b1 = self.nb()
            ps1 = self.bank(b1)

            def mm1(e):
                for h in range(4):
                    ins = e.matmul(ps1[:, h * 128:(h + 1) * 128], lhsT=rb[:, 512 + h * 128:512 + (h + 1) * 128], rhs=Sdb[d].ap[:, h * 128:(h + 1) * 128], start=True, stop=True)
                return ins
            S.op("pe", mm1, reads=[rb_.buf, Sdb[d].buf], writes=[self.pb[b1]])
            S.op("dve", lambda e: e.tensor_tensor(out=UP[d].ap, in0=rf[:, 0:512], in1=ps1, op=ALU.subtract), reads=[rf_.buf, self.pb[b1]], writes=[UP[d].buf])
            b2 = self.nb()
            ps2 = self.bank(b2)

            def mm2(e):
                for h in range(4):
                    hs = slice(h * 128, (h + 1) * 128)
                    e.matmul(ps2[:, hs], lhsT=rb[:, 1536 + h * 128:1536 + (h + 1) * 128], rhs=Sdb[d].ap[:, hs], start=True, stop=False)
                    ins = e.matmul(ps2[:, hs], lhsT=rb[:, 1024 + h * 128:1024 + (h + 1) * 128], rhs=UP[d].ap[:, hs], start=False, stop=True)
                return ins
            S.op("pe", mm2, reads=[rb_.buf, Sdb[d].buf, UP[d].buf], writes=[self.pb[b2]])
            S.op("act", lambda e: e.activation(out=ot_.ap[:, 512:1024], in_=ps2, func=AF.Copy), reads=[self.pb[b2]], writes=[ot_.buf])
            b3 = self.nb()
            ps3 = self.bank(b3)

            def mm3(e):
                for h in range(4):
                    hs = slice(h * 128, (h + 1) * 128)
                    ins = e.matmul(ps3[:, hs], lhsT=rb[:, 2048 + h * 128:2048 + (h + 1) * 128], rhs=UP[d].ap[:, hs], start=True, stop=True)
                return ins
            S.op("pe", mm3, reads=[rb_.buf, UP[d].buf], writes=[self.pb[b3]])
            S.op("dve", lambda e: e.tensor_tensor(out=V3(Sd[d].ap, 4), in0=V3(Sd[d].ap, 4), in1=rf[:, 514:518].unsqueeze(2).broadcast_to([128, 4, 128]), op=ALU.mult),
                 reads=[Sd[d].buf, rf_.buf], writes=[Sd[d].buf])
            S.op("dve", lambda e: e.tensor_tensor(out=Sd[d].ap, in0=Sd[d].ap, in1=ps3, op=ALU.add), reads=[Sd[d].buf, self.pb[b3]], writes=[Sd[d].buf])
            S.op("act", lambda e: e.activation(out=Sdb[d].ap, in_=Sd[d].ap, func=AF.Copy), reads=[Sd[d].buf], writes=[Sdb[d].buf])
            boA, boB = self.nb(), self.nb()
            psoA, psoB = self.bank(boA), self.bank(boB)

            def mmgo(e):
                for h in range(4):
                    hp, ee = h // 2, h % 2
                    pso_ = psoA if ee == 0 else psoB
                    ins = e.matmul(pso_[:, hp * 128:(hp + 1) * 128], lhsT=rb[64 * ee:64 * ee + 64, hp * 128:(hp + 1) * 128], rhs=Sgb[d][hp].ap[64 * ee:64 * ee + 64, :], start=True, stop=True)
                return ins
            S.op("pe", mmgo, reads=[rb_.buf, Sgb[d][0].buf, Sgb[d][1].buf], writes=[self.pb[boA], self.pb[boB]])
            S.op("act", lambda e: e.activation(out=V3(ot_.ap[:, 0:512], 4)[:, 0:4:2, :], in_=V3(psoA[:, 0:256], 2), func=AF.Copy), reads=[self.pb[boA]], writes=[ot_.buf])
            S.op("act", lambda e: e.activation(out=V3(ot_.ap[:, 0:512], 4)[:, 1:4:2, :], in_=V3(psoB[:, 0:256], 2), func=AF.Copy), reads=[self.pb[boB]], writes=[ot_.buf])
            bs = self.nb()
            pss = self.bank(bs)

            def mmgs(e):
                for hp in range(2):
                    ins = e.matmul(pss[:, hp * 256:(hp + 1) * 256], lhsT=rb[:, 256 + hp * 128:256 + (hp + 1) * 128], rhs=vt_.ap[:, hp * 256:(hp + 1) * 256], start=True, stop=True)
                return ins
            S.op("pe", mmgs, reads=[rb_.buf, vt_.buf], writes=[self.pb[bs]])
            for hp in range(2):
                for ee in range(2):
                    psl = slice(64 * ee, 64 * ee + 64)
                    S.op("dve", lambda e, hp=hp, ee=ee, psl=psl: e.scalar_tensor_tensor(out=Sg[d][hp].ap[psl, :], in0=Sg[d][hp].ap[psl, :], scalar=rf[psl, 512 + hp:513 + hp],
                                                                                     in1=pss[psl, hp * 256 + ee * 128:hp * 256 + (ee + 1) * 128], op0=ALU.mult, op1=ALU.add),
                         reads=[Sg[d][hp].buf, rf_.buf, self.pb[bs]], writes=[Sg[d][hp].buf])
                S.op("act", lambda e, hp=hp: e.activation(out=Sgb[d][hp].ap, in_=Sg[d][hp].ap, func=AF.Copy), reads=[Sg[d][hp].buf], writes=[Sgb[d][hp].buf])
            for g in range(2):
                gsl = slice(64 * g, 64 * g + 64)
                by = self.nb()
                psy = self.bank(by)
                S.op("pe", lambda e, psy=psy, gsl=gsl: e.matmul(psy, lhsT=rs_.ap[gsl, 0:128], rhs=Ssb[d].ap[gsl, :], start=True, stop=True), reads=[rs_.buf, Ssb[d].buf], writes=[self.pb[by]])
                S.op("dve", lambda e, psy=psy, g=g: e.tensor_tensor(out=V3(ot_.ap[:, 1024 + g * 512:1536 + g * 512], 8), in0=V3(psy, 8),
                                                                  in1=rf[:, 518 + 8 * g:526 + 8 * g].unsqueeze(2).broadcast_to([128, 8, 64]), op=ALU.mult),
                     reads=[self.pb[by], rf_.buf], writes=[ot_.buf])
            for g in range(2):
                gsl = slice(64 * g, 64 * g + 64)
                bd = self.nb()
                psd = self.bank(bd)
                S.op("pe", lambda e, psd=psd, g=g: e.matmul(psd, lhsT=rs_.ap[:, 128:256], rhs=rb[:, 2560 + g * 512:3072 + g * 512], start=True, stop=True), reads=[rs_.buf, rb_.buf], writes=[self.pb[bd]])
                S.op("dve", lambda e, gsl=gsl, g=g: e.tensor_tensor(out=V3(Ss[d].ap[gsl, :], 8), in0=V3(Ss[d].ap[gsl, :], 8),
                                                                  in1=rf[gsl, 534 + 8 * g:542 + 8 * g].unsqueeze(2).broadcast_to([64, 8, 64]), op=ALU.mult),
                     reads=[Ss[d].buf, rf_.buf], writes=[Ss[d].buf])
                S.op("dve", lambda e, gsl=gsl, psd=psd: e.tensor_tensor(out=Ss[d].ap[gsl, :], in0=Ss[d].ap[gsl, :], in1=psd[gsl, :], op=ALU.add),
                     reads=[Ss[d].buf, self.pb[bd]], writes=[Ss[d].buf])
            S.op("act", lambda e: e.activation(out=Ssb[d].ap, in_=Ss[d].ap, func=AF.Copy), reads=[Ss[d].buf], writes=[Ssb[d].buf])
            S.dma("sp", self.OD[d][c * 128:(c + 1) * 128, :], ot_.ap, reads=[ot_.buf])

        c0 = 0
        for L in self.seq_lens:
            N = L // 128
            for d in range(2):
                for hp in range(2):
                    S.op("pool", lambda e, d=d, hp=hp: e.memset(Sg[d][hp].ap, 0.0), writes=[Sg[d][hp].buf])
                    S.op("pool", lambda e, d=d, hp=hp: e.memset(Sgb[d][hp].ap, 0.0), writes=[Sgb[d][hp].buf])
                for t_ in (Sd[d], Sdb[d], Ss[d], Ssb[d]):
                    S.op("pool", lambda e, t_=t_: e.memset(t_.ap, 0.0), writes=[t_.buf])
            order = []
            for n in range(N):
                order.append((c0 + n, 0))
                order.append((c0 + N - 1 - n, 1))
            cnts = [0, 0]
            slots = []
            for (c, d) in order:
                slots.append(cnts[d] % NBUF)
                cnts[d] += 1
            load(order[0][0], order[0][1], slots[0])
            if len(order) > 1:
                load(order[1][0], order[1][1], slots[1])
            for j, (c, d) in enumerate(order):
                if j + 2 < len(order):
                    load(order[j + 2][0], order[j + 2][1], slots[j + 2])
                step(c, d, slots[j])
            c0 += N

    def pass_C1(self, l, xsrc):
        S = self.S
        W = self.W
        self.reset()
        noscan = "noscan" in self.dbg
        wob = [self.tile(D, BF16, "wob%d" % c) for c in range(16)]
        for c in range(16):
            S.dma("pool", wob[c].ap, W["w_out"][l, c * 128:(c + 1) * 128, :], writes=[wob[c].buf])
        nwr = self.tile(DMIX, F32, "nwrep")
        for h in range(4):
            S.dma("sp", nwr.ap[:, h * 128:(h + 1) * 128], W["gla_norm_w"][l].partition_broadcast(128), writes=[nwr.buf])
            S.dma("sp", nwr.ap[:, 512 + h * 128:512 + (h + 1) * 128], W["gdn_norm_w"][l].partition_broadcast(128), writes=[nwr.buf])
        S.dma("sp", nwr.ap[:, 1024:2048], W["ssd_norm_w"][l].partition_broadcast(128), writes=[nwr.buf])
        NB = 2
        oi = [self.tile(DMIX, F32, "oi%d" % i) for i in range(NB)]
        of = [self.tile(DMIX, F32, "of%d" % i) for i in range(NB)]
        ob = [self.tile(DMIX, F32, "ob%d" % i) for i in range(NB)]
        gt = [self.tile(DMIX, BF16, "gt%d" % i) for i in range(NB)]
        xs = [self.tile(D, F32, "xs%d" % i) for i in range(NB)]
        sq = self.tile(DMIX, F32, "sqC")
        ss = self.tile(16, F32, "ssC")
        rs = self.tile(16, F32, "rsC")
        mix = self.tile(DMIX, BF16, "mix")
        mixT = [self.tile(DMIX, BF16, "mixT%d" % i) for i in range(2)]
        x1 = [self.tile(D, F32, "x1_%d" % i) for i in range(2)]
        n = self.NCH

        def load(c):
            r0 = c * 128
            i = c % NB
            if not noscan:
                S.dma("sp", oi[i].ap, self.OI[r0:r0 + 128, :], writes=[oi[i].buf])
                S.dma("sp", of[i].ap, self.OD[0][r0:r0 + 128, :], writes=[of[i].buf])
                S.dma("sp", ob[i].ap, self.OD[1][r0:r0 + 128, :], writes=[ob[i].buf])
            else:
                S.dma("sp", oi[i].ap[:, 0:1024], self.xsrc_rows(xsrc, r0, 128), writes=[oi[i].buf])
                S.dma("sp", oi[i].ap[:, 1024:2048], self.xsrc_rows(xsrc, r0, 128), writes=[oi[i].buf])
            S.dma("sp", gt[i].ap, self.Tg[r0:r0 + 128, :], writes=[gt[i].buf])
            S.dma("sp", xs[i].ap, self.xsrc_rows(xsrc, r0, 128), writes=[xs[i].buf])

        def comp(c):
            r0 = c * 128
            i = c % NB
            o = oi[i]
            if not noscan:
                S.op("dve", lambda e: e.tensor_tensor(out=o.ap, in0=o.ap, in1=of[i].ap, op=ALU.add), reads=[o.buf, of[i].buf], writes=[o.buf])
                S.op("pool", lambda e: e.tensor_tensor(out=o.ap, in0=o.ap, in1=ob[i].ap, op=ALU.add), reads=[o.buf, ob[i].buf], writes=[o.buf])
            S.op("dve", lambda e: e.tensor_tensor(out=o.ap[:, 1024:2048], in0=o.ap[:, 1024:2048], in1=gt[i].ap[:, 1024:2048], op=ALU.mult),
                 reads=[o.buf, gt[i].buf], writes=[o.buf])
            S.op("pool", lambda e: e.tensor_tensor(out=sq.ap, in0=o.ap, in1=o.ap, op=ALU.mult), reads=[o.buf], writes=[sq.buf])
            S.op("dve", lambda e: e.tensor_reduce(out=ss.ap[:, 0:8], in_=sq.ap[:, 0:1024].rearrange("p (h d) -> p h d", h=8), axis=AX.X, op=ALU.add),
                 reads=[sq.buf], writes=[ss.buf])
            S.op("dve", lambda e: e.tensor_reduce(out=ss.ap[:, 8:10], in_=sq.ap[:, 1024:2048].rearrange("p (h d) -> p h d", h=2), axis=AX.X, op=ALU.add),
                 reads=[sq.buf], writes=[ss.buf])
            S.op("act", lambda e: e.activation(out=rs.ap[:, 0:8], in_=ss.ap[:, 0:8], func=AF.Ln, bias=EPS, scale=1.0 / 128), reads=[ss.buf], writes=[rs.buf])
            S.op("act", lambda e: e.activation(out=rs.ap[:, 8:10], in_=ss.ap[:, 8:10], func=AF.Ln, bias=EPS, scale=1.0 / 512), reads=[ss.buf, rs.buf], writes=[rs.buf])
            S.op("act", lambda e: e.activation(out=rs.ap[:, 0:10], in_=rs.ap[:, 0:10], func=AF.Exp, scale=-0.5), reads=[rs.buf], writes=[rs.buf])
            S.op("dve", lambda e: e.tensor_tensor(out=o.ap[:, 0:1024].rearrange("p (h d) -> p h d", h=8), in0=o.ap[:, 0:1024].rearrange("p (h d) -> p h d", h=8),
                                                  in1=rs.ap[:, 0:8].unsqueeze(2).broadcast_to([128, 8, 128]), op=ALU.mult), reads=[o.buf, rs.buf], writes=[o.buf])
            S.op("dve", lambda e: e.tensor_tensor(out=o.ap[:, 1024:2048].rearrange("p (h d) -> p h d", h=2), in0=o.ap[:, 1024:2048].rearrange("p (h d) -> p h d", h=2),
                                                  in1=rs.ap[:, 8:10].unsqueeze(2).broadcast_to([128, 2, 512]), op=ALU.mult), reads=[o.buf, rs.buf], writes=[o.buf])
            S.op("pool", lambda e: e.tensor_tensor(out=o.ap, in0=o.ap, in1=nwr.ap, op=ALU.mult), reads=[o.buf, nwr.buf], writes=[o.buf])
            S.op("dve", lambda e: e.tensor_tensor(out=mix.ap[:, 0:1024], in0=o.ap[:, 0:1024], in1=gt[i].ap[:, 0:1024], op=ALU.mult),
                 reads=[o.buf, gt[i].buf], writes=[mix.buf])
            S.op("act", lambda e: e.activation(out=mix.ap[:, 1024:2048], in_=o.ap[:, 1024:2048], func=AF.Copy), reads=[o.buf], writes=[mix.buf])
            mT = mixT[c % 2]
            for half in range(2):
                b = half
                pbf = self.bank(b, BF16)

                def tr(e, half=half, pbf=pbf):
                    for j in range(8):
                        cc = half * 8 + j
                        ins = e.transpose(pbf[:, j * 128:(j + 1) * 128], mix.ap[:, cc * 128:(cc + 1) * 128], self.identb.ap)
                    return ins
                S.op("pe", tr, reads=[mix.buf, self.identb.buf], writes=[self.pb[b]])
                if half == 0:
                    S.op("act", lambda e, pbf=pbf: e.activation(out=mT.ap[:, 0:1024], in_=pbf, func=AF.Copy), reads=[self.pb[b]], writes=[mT.buf])
                else:
                    S.op("dve", lambda e, pbf=pbf: e.tensor_copy(out=mT.ap[:, 1024:2048], in_=pbf), reads=[self.pb[b]], writes=[mT.buf])
            xo = x1[c % 2]
            for half in range(2):
                b = 2 + (2 * c + half) % 4
                ps = self.bank(b)

                def mm(e, half=half, ps=ps):
                    for cc in range(16):
                        ins = e.matmul(ps, lhsT=mT.ap[:, cc * 128:(cc + 1) * 128], rhs=wob[cc].ap[:, half * 512:(half + 1) * 512], start=(cc == 0), stop=(cc == 15))
                    return ins
                S.op("pe", mm, reads=[mT.buf] + [w.buf for w in wob], writes=[self.pb[b]])
                S.op("dve", lambda e, half=half, ps=ps: e.tensor_tensor(out=xo.ap[:, half * 512:(half + 1) * 512], in0=ps, in1=xs[i].ap[:, half * 512:(half + 1) * 512], op=ALU.add),
                     reads=[self.pb[b], xs[i].buf], writes=[xo.buf])
            S.dma("sp", self.X1[r0:r0 + 128, :], xo.ap, reads=[xo.buf])

        load(0)
        for c in range(n):
            if c + 1 < n:
                load(c + 1)
            comp(c)

    def pass_C2(self, l):
        S = self.S
        W = self.W
        self.reset()
        T = 256
        ns = 2
        ntiles = self.NT // T
        last = (l == self.n_layers - 1)
        wub = [self.tile(DFF, BF16, "wub%d" % k) for k in range(8)]
        for k in range(8):
            S.dma("pool", wub[k].ap, W["w_up"][l, k * 128:(k + 1) * 128, :], writes=[wub[k].buf])
        wdb = [self.tile(D, BF16, "wdb%d" % f) for f in range(32)]
        for f in range(32):
            S.dma("pool", wdb[f].ap, W["w_down"][l, f * 128:(f + 1) * 128, :], writes=[wdb[f].buf])
        nw = self.tile(8, F32, "nwC")
        S.dma("sp", nw.ap, W["norm_mlp_w"][l].rearrange("(k p) -> p k", p=128), writes=[nw.buf], slow=True)
        if last:
            nfr = self.tile(D, F32, "nfr")
            S.dma("sp", nfr.ap, W["norm_f_w"].partition_broadcast(128), writes=[nfr.buf])
        xt = [self.tile(ns * D, F32, "xtC%d" % i) for i in range(2)]
        junk = self.tile(D, BF16, "junkC")
        ssq = self.tile(4, F32, "ssqC")
        rstd = self.tile(4, F32, "rstdC")
        xn = self.tile(ns * D, BF16, "xnC")
        hT = [self.tile(8 * T, BF16, "hTC%d" % i) for i in range(2)]
        aT = self.tile(32 * T, BF16, "aT")
        rl = [self.tile(T, F32, "rl%d" % i) for i in range(3)]
        xo = [self.tile(D, F32, "xoC0")] * 2
        ss2 = self.tile(4, F32, "ss2")
        rs2 = self.tile(4, F32, "rs2")
        cnt = {"g": 0, "o": 0}

        def prep(t):
            t0 = t * T
            x = xt[t % 2]
            S.dma("sp", x.v("p (s d) -> p s d", s=ns), self.X1[t0:t0 + T, :].rearrange("(s p) d -> p s d", p=128), writes=[x.buf])
            self.rms_T(x, ns, nw, hT[t % 2], junk, ssq, rstd, xn, banks=[0, 1])

        def main(t):
            t0 = t * T
            x = xt[t % 2]
            h = hT[t % 2]
            hv = h.v("p (k t) -> p k t", k=8)
            av = aT.v("p (f t) -> p f t", f=32)
            for f in range(32):
                b = 2 + cnt["g"] % 3
                r = rl[cnt["g"] % 3]
                cnt["g"] += 1
                ps = self.bank(b)

                def mm(e, f=f, ps=ps):
                    for k in range(8):
                        ins = e.matmul(ps[:, 0:T], lhsT=wub[k].ap[:, f * 128:(f + 1) * 128], rhs=hv[:, k, :], start=(k == 0), stop=(k == 7))
                    return ins
                S.op("pe", mm, reads=[h.buf] + [w.buf for w in wub], writes=[self.pb[b]])
                S.op("act", lambda e, ps=ps, r=r: e.activation(out=r.ap, in_=ps[:, 0:T], func=AF.Relu), reads=[self.pb[b]], writes=[r.buf])
                eng = "pool" if f % 2 == 0 else "dve"
                S.op(eng, lambda e, f=f, r=r: e.tensor_tensor(out=av[:, f, :], in0=r.ap, in1=r.ap, op=ALU.mult), reads=[r.buf], writes=[aT.buf])
            xv = x.v("p (s d) -> p s d", s=ns)
            for s in range(ns):
                o = xo[cnt["o"] % 2]
                cnt["o"] += 1
                for half in range(2):
                    b = 5 + cnt["g"] % 3
                    cnt["g"] += 1
                    ps = self.bank(b)

                    def mm(e, s=s, half=half, ps=ps):
                        for f in range(32):
                            ins = e.matmul(ps, lhsT=av[:, f, s * 128:(s + 1) * 128], rhs=wdb[f].ap[:, half * 512:(half + 1) * 512], start=(f == 0), stop=(f == 31))
                        return ins
                    S.op("pe", mm, reads=[aT.buf] + [w.buf for w in wdb], writes=[self.pb[b]])
                    S.op("dve", lambda e, s=s, half=half, ps=ps, o=o: e.tensor_tensor(out=o.ap[:, half * 512:(half + 1) * 512], in0=ps,
                                                                                     in1=xv[:, s, half * 512:(half + 1) * 512], op=ALU.add),
                         reads=[self.pb[b], x.buf], writes=[o.buf])
                r0 = t0 + s * 128
                if not last:
                    S.dma("sp", self.XR[r0:r0 + 128, :], o.ap, reads=[o.buf])
                else:
                    S.op("pool", lambda e: e.memset(ss2.ap[:, 0:1], 0.0), writes=[ss2.buf])
                    S.op("act", lambda e, o=o: e.activation(out=junk.ap, in_=o.ap, func=AF.Square, accum_out=ss2.ap[:, 0:1]), reads=[o.buf, ss2.buf], writes=[junk.buf, ss2.buf])
                    S.op("act", lambda e: e.activation(out=rs2.ap[:, 0:1], in_=ss2.ap[:, 0:1], func=AF.Ln, bias=EPS, scale=1.0 / D), reads=[ss2.buf], writes=[rs2.buf])
                    S.op("act", lambda e: e.activation(out=rs2.ap[:, 0:1], in_=rs2.ap[:, 0:1], func=AF.Exp, scale=-0.5), reads=[rs2.buf], writes=[rs2.buf])
                    S.op("dve", lambda e, o=o: e.scalar_tensor_tensor(out=o.ap, in0=o.ap, scalar=rs2.ap[:, 0:1], in1=nfr.ap, op0=ALU.mult, op1=ALU.mult),
                         reads=[o.buf, rs2.buf, nfr.buf], writes=[o.buf])
                    S.dma("sp", self.xrows(self.yout, r0, 128), o.ap, reads=[o.buf])

        prep(0)
        for t in range(ntiles):
            if t + 1 < ntiles:
                prep(t + 1)
            main(t)


_CACHE = {}


def kernel(**inputs):
    n = 8
    key = "full"
    if key not in _CACHE:
        _CACHE[key] = KB().build()
    nc = _CACHE[key]
    in_maps = []
    for i in range(n):
        m = {"x0": np.ascontiguousarray(inputs["x_prompt"][i], dtype=np.float32),
             "x1": np.ascontiguousarray(inputs["x_sample"][i], dtype=np.float32)}
        for k in WSHAPES:
            m[k] = np.ascontiguousarray(inputs[k], dtype=np.float32)
        in_maps.append(m)
    res = run_bass_kernel_spmd(nc, in_maps, core_ids=list(range(n)))
    yp = np.stack([res.results[i]["y0"] for i in range(n)], axis=0)
    ys = np.stack([res.results[i]["y1"] for i in range(n)], axis=0)
    return (yp, ys)
```
